# Optimizing a Trainium2 kernel written in Bass

```python
import math
import jax, jax.numpy as jnp
from jax import lax
import numpy as np

D_MODEL = 2048
BATCH = 4
SEQ = 2048
DEPTH = 2

HEAD_DIM = 128
ATTN_Q_HEADS = 16
ATTN_KV_HEADS = 4
ATTN_GROUP = ATTN_Q_HEADS // ATTN_KV_HEADS
IDX_HEADS = 16
IDX_DIM = 64
TOPK_MAX = 256
Q_BLOCK = 128
RET_HEADS = 8
RET_QK_DIM = 128
RET_V_DIM = 256
RET_CHUNK = 128
ROPE_BASE = 10000.0
D_FF = 4 * D_MODEL
EPS = 1e-6

ATTN_W = ATTN_Q_HEADS * HEAD_DIM
KV_W = ATTN_KV_HEADS * HEAD_DIM
IDX_Q_W = IDX_HEADS * IDX_DIM
RET_QK_W = RET_HEADS * RET_QK_DIM
RET_V_W = RET_HEADS * RET_V_DIM
IN_SPLITS = (ATTN_W, KV_W, KV_W, IDX_Q_W, IDX_DIM, IDX_HEADS,
             RET_QK_W, RET_QK_W, RET_V_W, RET_V_W, D_MODEL, D_MODEL)
D_IN = ATTN_W + 2 * KV_W + IDX_Q_W + IDX_DIM + IDX_HEADS + 2 * RET_QK_W + 2 * RET_V_W + 2 * D_MODEL

kernel_name = "hybrid_dsa_retention_gated_block"


def split_offsets():
    return np.cumsum(np.array(IN_SPLITS))[:-1].tolist()


def rms_norm(x, g):
    xf = x.astype(jnp.float32)
    r = lax.rsqrt(jnp.mean(xf * xf, axis=-1, keepdims=True) + EPS)
    return (xf * r).astype(x.dtype) * g


def rotary_tables(S, dtype):
    pos = jnp.arange(S, dtype=jnp.float32)
    inv_freq = ROPE_BASE ** (-jnp.arange(0, RET_QK_DIM, 2, dtype=jnp.float32) / RET_QK_DIM)
    ang = pos[:, None] * inv_freq[None, :]
    return jnp.cos(ang)[:, None, :].astype(dtype), jnp.sin(ang)[:, None, :].astype(dtype)


def rotate(x, cos, sin):
    x1, x2 = jnp.split(x, 2, axis=-1)
    return jnp.concatenate([x1 * cos - x2 * sin, x2 * cos + x1 * sin], axis=-1)


def dsa_attention(q, k, v, iq, ik, iw, topk):
    B, S = q.shape[0], q.shape[1]
    n_blocks = S // Q_BLOCK
    scale = HEAD_DIM ** -0.5
    kpos = jnp.arange(S)
    gather = jax.vmap(lambda t, i: t[i])

    def block(i):
        start = i * Q_BLOCK
        qb = lax.dynamic_slice_in_dim(q, start, Q_BLOCK, axis=1)
        iqb = lax.dynamic_slice_in_dim(iq, start, Q_BLOCK, axis=1)
        iwb = lax.dynamic_slice_in_dim(iw, start, Q_BLOCK, axis=1)
        qpos = start + jnp.arange(Q_BLOCK)
        logits = jnp.einsum('bqhd,bsd->bqhs', iqb, ik)
        score = jnp.einsum('bqh,bqhs->bqs', iwb, jax.nn.relu(logits)).astype(jnp.float32)
        causal = kpos[None, :] <= qpos[:, None]
        score = jnp.where(causal[None], score, -jnp.inf)
        _, idx = lax.top_k(score, topk)
        valid = idx <= qpos[None, :, None]
        kg = gather(k, idx)
        vg = gather(v, idx)
        qg = qb.reshape(B, Q_BLOCK, ATTN_KV_HEADS, ATTN_GROUP, HEAD_DIM)
        s = jnp.einsum('bqgrd,bqkgd->bqgrk', qg, kg).astype(jnp.float32) * scale
        s = jnp.where(valid[:, :, None, None, :], s, -jnp.inf)
        p = jax.nn.softmax(s, axis=-1).astype(v.dtype)
        o = jnp.einsum('bqgrk,bqkgd->bqgrd', p, vg)
        return o.reshape(B, Q_BLOCK, ATTN_W)

    out = lax.map(block, jnp.arange(n_blocks))
    return jnp.moveaxis(out, 0, 1).reshape(B, S, ATTN_W)


def retention_chunkwise(q, k, v, log_gamma):
    B, S, H, Dk = q.shape
    Dv = v.shape[-1]
    C = RET_CHUNK
    n = S // C
    dt = v.dtype

    def chunks(t):
        return t.reshape(B, n, C, H, t.shape[-1]).transpose(0, 1, 3, 2, 4)

    qc, kc, vc = chunks(q), chunks(k), chunks(v)
    pos = jnp.arange(C, dtype=jnp.float32)
    lg = log_gamma.astype(jnp.float32)
    diff = pos[:, None] - pos[None, :]
    decay = jnp.where(diff[None] >= 0, jnp.exp(lg[:, None, None] * jnp.maximum(diff, 0.0)[None]), 0.0)
    xi = jnp.exp(lg[:, None] * (pos[None, :] + 1.0))
    zeta = jnp.exp(lg[:, None] * (C - 1.0 - pos[None, :]))
    chunk_decay = jnp.exp(lg * C).astype(dt)

    inner = jnp.einsum('bnhid,bnhjd->bnhij', qc, kc) * decay.astype(dt)
    inner = jnp.einsum('bnhij,bnhje->bnhie', inner, vc)
    upd = jnp.einsum('bnhjd,bnhje->bnhde', kc * zeta[..., None].astype(dt), vc)

    def step(R, u):
        return u + chunk_decay[:, None, None] * R, R

    R0 = jnp.zeros((B, H, Dk, Dv), dt)
    _, R_prev = lax.scan(step, R0, jnp.moveaxis(upd, 1, 0))
    R_prev = jnp.moveaxis(R_prev, 0, 1)
    cross = jnp.einsum('bnhid,bnhde->bnhie', qc, R_prev) * xi[..., None].astype(dt)
    out = inner + cross
    return out.transpose(0, 1, 3, 2, 4).reshape(B, S, H, Dv)


def group_norm_heads(y, g, b):
    B, S = y.shape[0], y.shape[1]
    yf = y.astype(jnp.float32)
    mu = jnp.mean(yf, axis=-1, keepdims=True)
    var = jnp.mean(jnp.square(yf - mu), axis=-1, keepdims=True)
    yn = ((yf - mu) * lax.rsqrt(var + EPS)).astype(y.dtype)
    return yn.reshape(B, S, -1) * g + b


def setup_inputs(seed: int = 0) -> dict:
    key = jax.random.key(seed)
    ks = jax.random.split(key, 14)

    def w(k, shape, fan_in):
        return jax.random.normal(k, shape, jnp.float32) * fan_in ** -0.5

    def gain(k, shape):
        return 1.0 + 0.02 * jax.random.normal(k, shape, jnp.float32)

    return {
        "x": jax.random.normal(ks[0], (BATCH, SEQ, D_MODEL), jnp.float32),
        "ln1_g": gain(ks[1], (DEPTH, D_MODEL)),
        "w_in": w(ks[2], (DEPTH, D_MODEL, D_IN), D_MODEL),
        "q_norm_g": gain(ks[3], (DEPTH, HEAD_DIM)),
        "k_norm_g": gain(ks[4], (DEPTH, HEAD_DIM)),
        "ret_gn_g": gain(ks[5], (DEPTH, RET_V_W)),
        "ret_gn_b": 0.02 * jax.random.normal(ks[6], (DEPTH, RET_V_W), jnp.float32),
        "w_up_attn": w(ks[7], (DEPTH, ATTN_W, D_MODEL), ATTN_W),
        "w_up_ret": w(ks[8], (DEPTH, RET_V_W, D_MODEL), RET_V_W),
        "w_out": w(ks[9], (DEPTH, D_MODEL, D_MODEL), D_MODEL),
        "ln2_g": gain(ks[10], (DEPTH, D_MODEL)),
        "w_ff1": w(ks[11], (DEPTH, D_MODEL, D_FF), D_MODEL),
        "w_ff2": w(ks[12], (DEPTH, D_FF, D_MODEL), D_FF),
    }


def reference(x, ln1_g, w_in, q_norm_g, k_norm_g, ret_gn_g, ret_gn_b,
              w_up_attn, w_up_ret, w_out, ln2_g, w_ff1, w_ff2):
    B, S, _ = x.shape
    topk = min(TOPK_MAX, S // 4)
    cos, sin = rotary_tables(S, x.dtype)
    log_gamma = jnp.log(1.0 - jnp.exp2(-5.0 - jnp.arange(RET_HEADS, dtype=jnp.float32)))
    offsets = split_offsets()

    for l in range(DEPTH):
        h = rms_norm(x, ln1_g[l])
        proj = h @ w_in[l]
        aq, ak, av, iq, ik, iw, rq, rk, rv, rg, ga, gb = jnp.split(proj, offsets, axis=-1)

        aq = rms_norm(aq.reshape(B, S, ATTN_Q_HEADS, HEAD_DIM), q_norm_g[l])
        ak = rms_norm(ak.reshape(B, S, ATTN_KV_HEADS, HEAD_DIM), k_norm_g[l])
        av = av.reshape(B, S, ATTN_KV_HEADS, HEAD_DIM)
        iq = iq.reshape(B, S, IDX_HEADS, IDX_DIM)
        iw = iw * IDX_HEADS ** -0.5
        o_attn = dsa_attention(aq, ak, av, iq, ik, iw, topk)

        rq = rotate(rq.reshape(B, S, RET_HEADS, RET_QK_DIM), cos, sin)
        rk = rotate(rk.reshape(B, S, RET_HEADS, RET_QK_DIM), cos, sin) * RET_QK_DIM ** -0.5
        rv = rv.reshape(B, S, RET_HEADS, RET_V_DIM)
        y_ret = retention_chunkwise(rq, rk, rv, log_gamma)
        o_ret = jax.nn.silu(rg) * group_norm_heads(y_ret, ret_gn_g[l], ret_gn_b[l])

        merged = jax.nn.sigmoid(ga) * (o_attn @ w_up_attn[l]) + jax.nn.sigmoid(gb) * (o_ret @ w_up_ret[l])
        x = x + merged @ w_out[l]

        h2 = rms_norm(x, ln2_g[l])
        x = x + jnp.square(jax.nn.relu(h2 @ w_ff1[l])) @ w_ff2[l]
    return x
```

```python
import contextlib
import numpy as np
import ml_dtypes
import concourse.bass as bass
import concourse.mybir as mybir
from concourse.bass_utils import run_bass_kernel_spmd

F32 = mybir.dt.float32
BF16 = mybir.dt.bfloat16
AF = mybir.ActivationFunctionType
ALU = mybir.AluOpType
AX = mybir.AxisListType

D = 2048
T = 1024
NT = 8
KC = 16
D_IN = 14416
EPS = 1e-6
NEG = -1.0e30
O_AQ, O_AK, O_AV, O_IQ, O_IK, O_IW, O_RQ, O_RK, O_RV, O_RG, O_GA, O_GB = (
    0, 2048, 2560, 3072, 4096, 4160, 4176, 5200, 6224, 8272, 10320, 12368)
GAMMA = [1.0 - 2.0 ** (-5.0 - h) for h in range(8)]
GC = [float(np.float32(np.exp(np.float32(np.log(np.float32(g))) * np.float32(128.0)))) for g in GAMMA]

ENGS = ("pe", "act", "dve", "pool", "sp")
NDMASEM = 8


class Prog:
    def __init__(self, nc):
        self.nc = nc
        self.ops = {e: [] for e in ENGS}
        self.bufs = {}
        self.dma_rr = {e: 0 for e in ENGS}
        self.dma_cnt = {}
        self.dma_last = {}

    def _deps(self, reads, writes):
        deps = []
        for b in reads:
            st = self.bufs.get(b)
            if st and st[0] is not None:
                deps.append(st[0])
        for b in writes:
            st = self.bufs.get(b)
            if st:
                if st[0] is not None:
                    deps.append(st[0])
                deps.extend(st[1])
        return deps

    def _record(self, tok, reads, writes):
        for b in reads:
            st = self.bufs.setdefault(b, [None, []])
            st[1].append(tok)
        for b in writes:
            self.bufs[b] = [tok, []]

    def op(self, eng, fn, reads=(), writes=(), nosync_same=False):
        deps = self._deps(reads, writes)
        if nosync_same:
            deps = [d for d in deps if not (d[0] == 'e' and d[1] == eng)]
        tok = ('e', eng, len(self.ops[eng]))
        self.ops[eng].append(['op', fn, deps, tok])
        self._record(tok, reads, writes)
        return tok

    def dma(self, eng, out, in_, reads=(), writes=(), **kw):
        deps = self._deps(reads, writes)
        k = self.dma_rr[eng]
        self.dma_rr[eng] = (k + 1) % NDMASEM
        prev = self.dma_last.get((eng, k))
        if prev is not None:
            deps.append(prev)
        val = self.dma_cnt.get((eng, k), 0) + 16
        self.dma_cnt[(eng, k)] = val
        tok = ('d', eng, k, val)
        self.dma_last[(eng, k)] = tok

        def fn(e, out=out, in_=in_, kw=kw):
            return e.dma_start(out=out, in_=in_, **kw)
        self.ops[eng].append(['dma', fn, deps, tok])
        self._record(tok, reads, writes)
        return tok

    def collective(self, fn, reads=(), writes=()):
        eng, k = 'pool', 'cc'
        deps = self._deps(reads, writes)
        prev = self.dma_last.get((eng, k))
        if prev is not None:
            deps.append(prev)
        val = self.dma_cnt.get((eng, k), 0) + 1
        self.dma_cnt[(eng, k)] = val
        tok = ('d', eng, k, val)
        self.dma_last[(eng, k)] = tok
        self.ops[eng].append(['cc', fn, deps, tok])
        self._record(tok, reads, writes)
        return tok

    def barrier(self):
        toks = []
        for e in ENGS:
            for o in reversed(self.ops[e]):
                if o[3] is not None and o[3][0] == 'e':
                    toks.append(o[3])
                    break
        for key, tok in self.dma_last.items():
            toks.append(tok)
        for e in ENGS:
            self.ops[e].append(['bar', None, list(toks), None])
        self.bufs = {}

    def emit(self):
        nc = self.nc
        needed = {e: set() for e in ENGS}
        for e in ENGS:
            for o in self.ops[e]:
                for d in o[2]:
                    if d[0] == 'e':
                        needed[d[1]].add(d[2])
        rank = {}
        for e in ENGS:
            r = 0
            for i in sorted(needed[e]):
                r += 1
                rank[(e, i)] = r
        with contextlib.ExitStack() as st:
            esem = {e: st.enter_context(nc.semaphore("s_" + e)) for e in ENGS}
            dsem = {}
            for (e, k) in self.dma_cnt:
                dsem[(e, k)] = st.enter_context(nc.semaphore("d_%s%s" % (e, k)))
            block = st.enter_context(nc.Block())
            engobj = {"pe": "tensor", "act": "scalar", "dve": "vector", "pool": "gpsimd", "sp": "sync"}

            def body_for(e):
                def body(eng):
                    waited = {}
                    for kind, fn, deps, tok in self.ops[e]:
                        for d in deps:
                            if d[0] == 'e':
                                s, v = esem[d[1]], rank[(d[1], d[2])]
                                key = ('e', d[1])
                            else:
                                s, v = dsem[(d[1], d[2])], d[3]
                                key = ('d', d[1], d[2])
                            if waited.get(key, 0) >= v:
                                continue
                            waited[key] = v
                            eng.wait_ge(s, v)
                        if kind == 'op':
                            ins = fn(eng)
                            if tok[2] in needed[e]:
                                ins.then_inc(esem[e], 1)
                        elif kind == 'dma':
                            ins = fn(eng)
                            ins.then_inc(dsem[(tok[1], tok[2])], 16)
                        elif kind == 'cc':
                            ins = fn(eng)
                            ins.then_inc(dsem[(tok[1], tok[2])])
                return body
            for e in ENGS:
                getattr(block, engobj[e])(body_for(e))


class Ctx:
    pass


_UID = [0]


def sbuf(nc, st, name, shape, dt):
    _UID[0] += 1
    return st.enter_context(nc.sbuf_tensor("sb%d_%s" % (_UID[0], name), shape, dt))


def psum(nc, st, name, shape, dt):
    _UID[0] += 1
    return st.enter_context(nc.psum_tensor("ps%d_%s" % (_UID[0], name), shape, dt))


def ln_phase(P, nc, st, x_dram, g_dram, hT, ident, tag, xres=None):
    gt = sbuf(nc, st, tag + "gt", [128, KC], F32)
    P.dma('sp', gt[:], g_dram, writes=[tag + 'gt'])
    xt = [sbuf(nc, st, tag + "xt%d" % i, [128, D], F32) for i in range(2)] if xres is None else None
    hb = [sbuf(nc, st, tag + "hb%d" % i, [128, D], BF16) for i in range(2)]
    junk = sbuf(nc, st, tag + "junk", [128, D], F32)
    sm = [sbuf(nc, st, tag + "sm%d" % i, [128, 4], F32) for i in range(2)]
    pT = [psum(nc, st, tag + "pT%d" % i, [128, 8, 128], BF16) for i in range(2)]
    for t in range(NT):
        b = t % 2
        if xres is None:
            P.dma('sp', xt[b][:], x_dram[t * 128:(t + 1) * 128, :], writes=[(tag, 'xt', b)])
            xin = xt[b][:]
            rk = [(tag, 'xt', b)]
        else:
            xin = xres[:, t, :]
            rk = [('xres', t)]
        s = sm[b]
        P.op('pool', lambda e, xin=xin: e.tensor_tensor(out=junk[:], in0=xin, in1=xin, op=ALU.mult),
             reads=rk, writes=[(tag, 'junk')])
        P.op('dve', lambda e, s=s: e.tensor_reduce(out=s[:, 0:1], in_=junk[:], axis=AX.X, op=ALU.add),
             reads=[(tag, 'junk')], writes=[(tag, 'sm', b)])
        P.op('dve', lambda e, s=s: e.tensor_scalar(out=s[:, 1:2], in0=s[:, 0:1], scalar1=1.0 / D, scalar2=EPS,
                                                   op0=ALU.mult, op1=ALU.add),
             reads=[(tag, 'sm', b)], writes=[(tag, 'sm', b)])
        P.op('act', lambda e, s=s: e.activation(out=s[:, 2:3], in_=s[:, 1:2], func=AF.Sqrt),
             reads=[(tag, 'sm', b)], writes=[(tag, 'sm', b)])
        P.op('dve', lambda e, s=s: e.reciprocal(out=s[:, 3:4], in_=s[:, 2:3]),
             reads=[(tag, 'sm', b)], writes=[(tag, 'sm', b)])
        P.op('act', lambda e, xin=xin, s=s, b=b: e.activation(out=hb[b][:], in_=xin, func=AF.Copy, scale=s[:, 3:4]),
             reads=rk + [(tag, 'sm', b)], writes=[(tag, 'hb', b)])
        for half in range(2):
            pb = pT[half]
            for kk in range(8):
                k = half * 8 + kk
                P.op('pe', lambda e, pb=pb, kk=kk, k=k, b=b: e.transpose(out=pb[:, kk, :], in_=hb[b][:, k * 128:(k + 1) * 128],
                                                                         identity=ident[:]),
                     reads=[(tag, 'hb', b), 'ident'], writes=[(tag, 'pT', half)], nosync_same=True)
            P.op('dve', lambda e, pb=pb, half=half, t=t: e.tensor_tensor(
                out=hT[:, half * 8:half * 8 + 8, t * 128:(t + 1) * 128], in0=pb[:],
                in1=gt[:, half * 8:half * 8 + 8].unsqueeze(2).to_broadcast([128, 8, 128]), op=ALU.mult),
                reads=[(tag, 'pT', half), tag + 'gt'], writes=[('hT', t)])


class Gemm:
    def __init__(self, P, nc, st, nacc, tag):
        self.P, self.nc = P, nc
        self.Wb = [sbuf(nc, st, tag + "Wb%d" % i, [128, KC, 512], BF16) for i in range(2)]
        self.acc = [psum(nc, st, tag + "acc%d" % i, [128, 512], F32) for i in range(nacc)]
        self.wcnt = 0
        self.acnt = 0
        self.tag = tag

    def block(self, w_ap, ncols, mode, actT, act_key, epi, tiles=range(8), keyfn=None):
        P = self.P
        slot = self.wcnt % 2
        self.wcnt += 1
        Wb = self.Wb[slot]
        wkey = (self.tag, 'W', slot)
        P.dma('pool', Wb[:, :, 0:ncols], w_ap.rearrange("(k p) n -> p k n", p=128), writes=[wkey])
        for ti in tiles:
            ai = self.acnt % len(self.acc)
            self.acnt += 1
            ps = self.acc[ai]
            akey = (self.tag, 'acc', ai)
            for k in range(KC):
                if mode == 'TM':
                    lhsT = actT[:, k, ti * 128:(ti + 1) * 128]
                    rhs = Wb[:, k, 0:ncols]
                    out = ps[:, 0:ncols]
                    rkeys = [wkey] + (keyfn('TM', ti) if keyfn else [(act_key, ti)])
                else:
                    mc, tg = ti // 2, ti % 2
                    lhsT = Wb[:, k, mc * 128:(mc + 1) * 128]
                    rhs = actT[:, k, tg * 512:(tg + 1) * 512]
                    out = ps[:, :]
                    rkeys = [wkey] + (keyfn('FM', ti) if keyfn else [(act_key, tg * 4 + q) for q in range(4)])
                P.op('pe', lambda e, out=out, lhsT=lhsT, rhs=rhs, k=k: e.matmul(out, lhsT=lhsT, rhs=rhs, start=(k == 0),
                                                                                stop=(k == KC - 1)),
                     reads=rkeys, writes=[akey], nosync_same=True)
            epi(ti, ps, akey)


def rotary(P, eng, xin, C, S, out, tmp, keys_in, key_out, key_tmp):
    x1, x2 = xin[:, :, 0, :], xin[:, :, 1, :]
    t1, t2 = tmp[:, :, 0, :], tmp[:, :, 1, :]
    P.op(eng, lambda e: e.tensor_tensor(out=t1, in0=x1, in1=C, op=ALU.mult), reads=keys_in, writes=[key_tmp])
    P.op(eng, lambda e: e.tensor_tensor(out=t2, in0=x2, in1=S, op=ALU.mult), reads=keys_in + [key_tmp], writes=[key_tmp])
    P.op(eng, lambda e: e.tensor_tensor(out=out[:, :, 0, :], in0=t1, in1=t2, op=ALU.subtract), reads=[key_tmp], writes=[key_out])
    P.op(eng, lambda e: e.tensor_tensor(out=t1, in0=x2, in1=C, op=ALU.mult), reads=keys_in + [key_out], writes=[key_tmp])
    P.op(eng, lambda e: e.tensor_tensor(out=t2, in0=x1, in1=S, op=ALU.mult), reads=keys_in + [key_tmp], writes=[key_tmp])
    P.op(eng, lambda e: e.tensor_tensor(out=out[:, :, 1, :], in0=t1, in1=t2, op=ALU.add), reads=[key_tmp, key_out], writes=[key_out])


def head_rstd(P, xin, nh, hd, sq, sm, keys_in, tag):
    P.op('pool', lambda e: e.tensor_tensor(out=sq[:, 0:nh * hd], in0=xin, in1=xin, op=ALU.mult), reads=keys_in, writes=[tag + 'sq'])
    P.op('dve', lambda e: e.tensor_reduce(out=sm[:, 0, 0:nh], in_=sq[:, 0:nh * hd].rearrange("p (h d) -> p h d", d=hd),
                                          axis=AX.X, op=ALU.add), reads=[tag + 'sq'], writes=[tag + 'sm'])
    P.op('dve', lambda e: e.tensor_scalar(out=sm[:, 1, 0:nh], in0=sm[:, 0, 0:nh], scalar1=1.0 / hd, scalar2=EPS,
                                          op0=ALU.mult, op1=ALU.add), reads=[tag + 'sm'], writes=[tag + 'sm'])
    P.op('act', lambda e: e.activation(out=sm[:, 2, 0:nh], in_=sm[:, 1, 0:nh], func=AF.Sqrt), reads=[tag + 'sm'], writes=[tag + 'sm'])
    P.op('dve', lambda e: e.reciprocal(out=sm[:, 3, 0:nh], in_=sm[:, 2, 0:nh]), reads=[tag + 'sm'], writes=[tag + 'sm'])


def phase_A(P, nc, d):
    projK = d['projK']
    with contextlib.ExitStack() as st:
        ident = sbuf(nc, st, "identA", [128, 128], BF16)
        P.dma('pool', ident[:], d['ident'], writes=['ident'])
        hT = sbuf(nc, st, "hT", [128, KC, T], BF16)
        with contextlib.ExitStack() as st2:
            ln_phase(P, nc, st2, d['x'], d['ln1g'], hT, ident, "lnA")
            P.barrier()
            if d.get('_stop') == 1:
                P.dma('sp', d['projK'].rearrange("t (k c) -> t k c", c=288)[0:128, :, 0:256].bitcast(BF16) if False else d['KTO'].rearrange("(k p) t -> p k t", p=128)[:, 0:8, :], hT[:, 0:8, :], reads=[('hT', t) for t in range(NT)])
                P.barrier()
                return
        with contextlib.ExitStack() as st2:
            G = Gemm(P, nc, st2, 6, "gA")
            stg = [sbuf(nc, st2, "stgA%d" % i, [128, 512], F32) for i in range(4)]
            cnt = [0]
            blocks = [(O_AK, 512, 0), (O_AV, 512, 512), (O_IK, 80, 1024), (O_RK, 512, 1536), (O_RK + 512, 512, 2048),
                      (O_RV, 512, 2560), (O_RV + 512, 512, 3072), (O_RV + 1024, 512, 3584), (O_RV + 1536, 512, 4096)]
            for (c0, ncols, dst) in blocks:
                def epi(ti, ps, akey, ncols=ncols, dst=dst):
                    s = cnt[0] % 4
                    cnt[0] += 1
                    eng = 'act' if s % 2 == 0 else 'dve'
                    if eng == 'act':
                        P.op('act', lambda e: e.copy(out=stg[s][:, 0:ncols], in_=ps[:, 0:ncols]), reads=[akey], writes=[('stgA', s)])
                    else:
                        P.op('dve', lambda e: e.tensor_copy(out=stg[s][:, 0:ncols], in_=ps[:, 0:ncols]), reads=[akey], writes=[('stgA', s)])
                    P.dma('sp', projK[ti * 128:(ti + 1) * 128, dst:dst + ncols], stg[s][:, 0:ncols], reads=[('stgA', s)],
                          writes=[('projK', ti)])
                G.block(d['w_in'][:, c0:c0 + ncols], ncols, 'TM', hT, 'hT', epi)
                if d.get('_stop') == 2 and d.get('_nblk', 99) <= blocks.index((c0, ncols, dst)) + 1:
                    break
            P.barrier()
    if d.get('_stop') == 2:
        return
    with contextlib.ExitStack() as st:
        ident = sbuf(nc, st, "identA2", [128, 128], BF16)
        P.dma('pool', ident[:], d['ident'], writes=['ident'])
        kng = sbuf(nc, st, "kng", [128, 1], F32)
        P.dma('sp', kng[:], d['kng'], writes=['kng'])
        KTs = sbuf(nc, st, "KTs", [128, 4, T], BF16)
        Vs = sbuf(nc, st, "Vs", [128, NT, 512], BF16)
        IKs = sbuf(nc, st, "IKs", [128, T], BF16)
        KTOs = sbuf(nc, st, "KTOs", [128, 8, T], BF16)
        akv = [sbuf(nc, st, "akv%d" % i, [128, 1024], F32) for i in range(2)]
        ikt = [sbuf(nc, st, "ikt%d" % i, [128, 64], F32) for i in range(2)]
        rkt = [sbuf(nc, st, "rkt%d" % i, [128, 8, 2, 64], F32) for i in range(2)]
        rvt = [sbuf(nc, st, "rvt%d" % i, [128, 2048], F32) for i in range(2)]
        ckt = [sbuf(nc, st, "ckt%d" % i, [128, 8, 64], F32) for i in range(2)]
        skt = [sbuf(nc, st, "skt%d" % i, [128, 8, 64], F32) for i in range(2)]
        sq = sbuf(nc, st, "sqA", [128, 512], F32)
        sm = sbuf(nc, st, "smA", [128, 4, 16], F32)
        akn = sbuf(nc, st, "akn", [128, 4, 128], BF16)
        ikd = sbuf(nc, st, "ikd", [128, 128], BF16)
        rtmp = sbuf(nc, st, "rtmp", [128, 8, 2, 64], F32)
        kb = sbuf(nc, st, "kb", [128, 8, 2, 64], BF16)
        rvb = [sbuf(nc, st, "rvb%d" % i, [128, 2048], BF16) for i in range(2)]
        us = [sbuf(nc, st, "us%d" % i, [128, 2048], F32) for i in range(2)]
        pTk = psum(nc, st, "pTk", [128, 8, 128], BF16)
        pTi = psum(nc, st, "pTi", [128, 8, 128], BF16)
        pTr = psum(nc, st, "pTr", [128, 8, 128], BF16)
        pU = [psum(nc, st, "pU%d" % i, [128, 512], F32) for i in range(4)]
        for t in range(NT):
            b = t % 2
            r0, r1 = t * 128, (t + 1) * 128
            P.dma('sp', akv[b][:], projK[r0:r1, 0:1024], writes=[('akv', b)])
            P.dma('sp', ikt[b][:], projK[r0:r1, 1024:1088], writes=[('ikt', b)])
            P.dma('sp', rkt[b][:].rearrange("p a b c -> p (a b c)"), projK[r0:r1, 1536:2560], writes=[('rkt', b)])
            P.dma('sp', rvt[b][:], projK[r0:r1, 2560:4608], writes=[('rvt', b)])
            P.dma('sp', ckt[b][:].rearrange("p a c -> p (a c)"), d['cosk'][r0:r1, :], writes=[('ckt', b)])
            P.dma('sp', skt[b][:].rearrange("p a c -> p (a c)"), d['sink'][r0:r1, :], writes=[('skt', b)])
            lvl = d.get('_stop', 99)
            head_rstd(P, akv[b][:, 0:512], 4, 128, sq, sm, [('akv', b)], 'A')
            P.op('dve', lambda e, b=b: e.tensor_tensor(out=akn[:], in0=akv[b][:, 0:512].rearrange("p (h d) -> p h d", d=128),
                                                       in1=sm[:, 3, 0:4].unsqueeze(2).to_broadcast([128, 4, 128]), op=ALU.mult),
                 reads=[('akv', b), 'Asm'], writes=['akn'])
            for h in range(4):
                P.op('pe', lambda e, h=h: e.transpose(out=pTk[:, h, :], in_=akn[:, h, :], identity=ident[:]),
                     reads=['akn', 'ident'], writes=['pTk'], nosync_same=True)
            P.op('act', lambda e, t=t: e.activation(out=KTs[:, :, t * 128:(t + 1) * 128], in_=pTk[:, 0:4, :], func=AF.Copy,
                                                    scale=kng[:, 0:1]), reads=['pTk', 'kng'], writes=[('KTs', t)])
            if lvl < 4:
                continue
            P.op('pool', lambda e, b=b, t=t: e.tensor_copy(out=Vs[:, t, :], in_=akv[b][:, 512:1024]), reads=[('akv', b)], writes=[('Vs', t)])
            P.op('pool', lambda e, b=b: e.tensor_copy(out=ikd[:, 0:64], in_=ikt[b][:]), reads=[('ikt', b)], writes=['ikd'])
            P.op('pool', lambda e, b=b: e.tensor_copy(out=ikd[:, 64:128], in_=ikt[b][:]), reads=[('ikt', b), 'ikd'], writes=['ikd'])
            P.op('pe', lambda e: e.transpose(out=pTi[:, 0, :], in_=ikd[:], identity=ident[:]), reads=['ikd', 'ident'], writes=['pTi'],
                 nosync_same=True)
            P.op('act', lambda e, t=t: e.copy(out=IKs[:, t * 128:(t + 1) * 128], in_=pTi[:, 0, :]), reads=['pTi'], writes=[('IKs', t)])
            if lvl < 5:
                continue
            rotary(P, 'pool', rkt[b], ckt[b][:], skt[b][:], kb, rtmp, [('rkt', b), ('ckt', b), ('skt', b)], 'kb', 'rtmp')
            for h in range(8):
                P.op('pe', lambda e, h=h: e.transpose(out=pTr[:, h, :], in_=kb[:, h, :, :].rearrange("p a c -> p (a c)"),
                                                      identity=ident[:]), reads=['kb', 'ident'], writes=['pTr'], nosync_same=True)
            P.op('act', lambda e, t=t: e.copy(out=KTOs[:, :, t * 128:(t + 1) * 128], in_=pTr[:]), reads=['pTr'], writes=[('KTOs', t)])
            if lvl < 6:
                continue
            P.op('act', lambda e, b=b: e.copy(out=rvb[b][:], in_=rvt[b][:]), reads=[('rvt', b)], writes=[('rvb', b)])
            P.dma('sp', d['RVO'][r0:r1, :], rvb[b][:], reads=[('rvb', b)])
            if lvl < 7:
                continue
            for h in range(8):
                pu = pU[h // 2]
                P.op('pe', lambda e, h=h, pu=pu, b=b: e.matmul(pu[:, (h % 2) * 256:(h % 2) * 256 + 256],
                                                               lhsT=kb[:, h, :, :].rearrange("p a c -> p (a c)"),
                                                               rhs=rvb[b][:, h * 256:(h + 1) * 256], start=True, stop=True),
                     reads=['kb', ('rvb', b)], writes=[('pU', h // 2)], nosync_same=True)
            if lvl < 8:
                continue
            for q in range(4):
                eng = 'act' if q % 2 == 0 else 'dve'
                if eng == 'act':
                    P.op('act', lambda e, q=q, b=b: e.copy(out=us[b][:, q * 512:(q + 1) * 512], in_=pU[q][:]), reads=[('pU', q)],
                         writes=[('us', b)])
                else:
                    P.op('dve', lambda e, q=q, b=b: e.tensor_copy(out=us[b][:, q * 512:(q + 1) * 512], in_=pU[q][:]), reads=[('pU', q)],
                         writes=[('us', b)])
            P.dma('sp', d['UPD'][r0:r1, :], us[b][:], reads=[('us', b)])
        P.dma('sp', d['KT'].rearrange("(h p) t -> p h t", p=128), KTs[:], reads=[('KTs', t) for t in range(NT)])
        P.dma('sp', d['V'].rearrange("(t p) c -> p t c", p=128), Vs[:], reads=[('Vs', t) for t in range(NT)])
        P.dma('sp', d['IKT'], IKs[:], reads=[('IKs', t) for t in range(NT)])
        P.dma('sp', d['KTO'].rearrange("(h p) t -> p h t", p=128), KTOs[:], reads=[('KTOs', t) for t in range(NT)])
        P.barrier()


def phase_B(P, nc, d):
    projQ, G = d['projQ'], d['G']
    SCALE = 128.0 ** -0.5
    with contextlib.ExitStack() as st:
        ident12 = sbuf(nc, st, "identB", [128, 128], BF16)
        P.dma('pool', ident12[:], d['ident'], writes=['ident'])
        hT = sbuf(nc, st, "hTB", [128, KC, T], BF16)
        with contextlib.ExitStack() as st2:
            ln_phase(P, nc, st2, d['x'], d['ln1g'], hT, ident12, "lnB")
            P.barrier()
        with contextlib.ExitStack() as st2:
            Gm = Gemm(P, nc, st2, 6, "gB")
            stg = [sbuf(nc, st2, "stgB%d" % i, [128, 512], F32) for i in range(4)]
            cnt = [0]
            tm_blocks = [(O_AQ + 512 * i, 512, 512 * i) for i in range(4)] + [(O_IQ, 512, 2048), (O_IQ + 512, 512, 2560),
                                                                               (O_IK, 80, 3072), (O_RQ, 512, 3584), (O_RQ + 512, 512, 4096)]
            for (c0, ncols, dst) in tm_blocks:
                def epi(ti, ps, akey, ncols=ncols, dst=dst):
                    s = cnt[0] % 4
                    cnt[0] += 1
                    if s % 2 == 0:
                        P.op('act', lambda e: e.copy(out=stg[s][:, 0:ncols], in_=ps[:, 0:ncols]), reads=[akey], writes=[('stgB', s)])
                    else:
                        P.op('dve', lambda e: e.tensor_copy(out=stg[s][:, 0:ncols], in_=ps[:, 0:ncols]), reads=[akey], writes=[('stgB', s)])
                    P.dma('sp', projQ[ti * 128:(ti + 1) * 128, dst:dst + ncols], stg[s][:, 0:ncols], reads=[('stgB', s)],
                          writes=[('projQ', ti)])
                Gm.block(d['w_in'][:, c0:c0 + ncols], ncols, 'TM', hT, 'hT', epi)
            for gi, c0 in enumerate((O_RG, O_GA, O_GB)):
                for nb in range(4):
                    def epi(ti, ps, akey, gi=gi, nb=nb):
                        s = cnt[0] % 4
                        cnt[0] += 1
                        mc, tg = ti // 2, ti % 2
                        P.op('act', lambda e: e.activation(out=stg[s][:], in_=ps[:], func=AF.Sigmoid), reads=[akey], writes=[('stgB', s)])
                        if gi == 0:
                            P.op('dve', lambda e: e.tensor_tensor(out=stg[s][:], in0=stg[s][:], in1=ps[:], op=ALU.mult),
                                 reads=[akey, ('stgB', s)], writes=[('stgB', s)])
                        f0 = nb * 512 + mc * 128
                        P.dma('sp', G[gi, f0:f0 + 128, tg * 512:(tg + 1) * 512], stg[s][:], reads=[('stgB', s)], writes=[('G', gi)])
                    Gm.block(d['w_in'][:, c0 + nb * 512:c0 + (nb + 1) * 512], 512, 'FM', hT, 'hT', epi)
            P.barrier()
    with contextlib.ExitStack() as st:
        sb = lambda name, shape, dt: sbuf(nc, st, name, shape, dt)
        ident5 = sb("identB5", [128, 128], BF16)
        P.dma('pool', ident5[:], d['ident'], writes=['ident'])
        tri = sb("tri", [128, 128], BF16)
        P.dma('pool', tri[:], d['tri'], writes=['tri'])
        ones = sb("ones", [128, 128], BF16)
        P.op('pool', lambda e: e.memset(ones[:], 1.0), writes=['ones'])
        idxm = sb("idxm", [128, 256], F32)
        P.dma('sp', idxm[:], d['idxmask'], writes=['idxm'])
        cfl = sb("cfl", [128, 2], F32)
        P.dma('sp', cfl[:], d['cflag'], writes=['cfl'])
        qng = sb("qng", [128, 2], F32)
        P.dma('sp', qng[:, 0:1], d['qng'], writes=['qng'])
        P.op('pool', lambda e: e.tensor_scalar(out=qng[:, 1:2], in0=qng[:, 0:1], scalar1=SCALE, scalar2=None, op0=ALU.mult),
             reads=['qng'], writes=['qng'])
        gng = sb("gng", [128, KC], F32)
        gnb = sb("gnb", [128, KC], F32)
        P.dma('sp', gng[:], d['gng'], writes=['gng'])
        P.dma('sp', gnb[:], d['gnb'], writes=['gnb'])
        KTsb = sb("KTsb", [128, 4, 8, 2, 128], BF16)
        Vsb = sb("Vsb", [128, 8, 2, 512], BF16)
        IKsb = sb("IKsb", [128, 8, 2, 128], BF16)
        for c2 in range(2):
            for h in range(4):
                P.dma('sp', KTsb[:, h, :, c2, :], d['KTf'][c2, h * 128:(h + 1) * 128, :].rearrange("p (i r) -> p i r", r=128),
                      writes=[('KTsb', c2, h)])
            P.dma('sp', Vsb[:, :, c2, :], d['Vf'][c2].rearrange("(i p) c -> p i c", p=128), writes=[('Vsb', c2)])
            P.dma('sp', IKsb[:, :, c2, :], d['IKTf'][c2].rearrange("p (i r) -> p i r", r=128), writes=[('IKsb', c2)])
        kside = [('KTsb', c2, h) for c2 in range(2) for h in range(4)] + [('Vsb', 0), ('Vsb', 1), ('IKsb', 0), ('IKsb', 1)]
        KTv = KTsb[:].rearrange("p h i c r -> p h (i c r)")
        Vv = Vsb[:].rearrange("p i c f -> p (i c) f")
        IKv = IKsb[:].rearrange("p i c r -> p (i c r)")
        S = sb("Sst", [128, 8, 256], F32)
        P.op('pool', lambda e: e.memset(S[:], 0.0), writes=['S'])
        aqt = sb("aqt", [128, 2048], F32)
        iqt = sb("iqt", [128, 1024], F32)
        iwt = sb("iwt", [128, 16], F32)
        iws = sb("iws", [128, 16], F32)
        rqt = sb("rqt", [128, 8, 2, 64], F32)
        cqt = sb("cqt", [128, 8, 64], F32)
        sqt = sb("sqt", [128, 8, 64], F32)
        ktot = sb("ktot", [128, 8, 128], BF16)
        rvot = sb("rvot", [128, 2048], BF16)
        U0 = sb("U0", [128, 8, 256], F32)
        U1 = sb("U1", [128, 8, 256], F32)
        srg = sb("srg", [128, KC, 128], F32)
        sq = sb("sqB", [128, 2048], F32)
        sm = sb("smB", [128, 4, 16], F32)
        aqn = sb("aqn", [128, 16, 128], BF16)
        aqT = sb("aqT", [128, 16, 128], BF16)
        iqb = sb("iqb", [128, 1024], BF16)
        iqT = sb("iqT", [128, 8, 128], BF16)
        rtmp = sb("rtmpB", [128, 8, 2, 64], F32)
        qb = sb("qb", [128, 8, 2, 64], BF16)
        qT = sb("qT", [128, 8, 128], BF16)
        score = sb("score", [128, 2048], F32)
        work = sb("work", [128, 2048], F32)
        m8 = sb("m8", [128, 8], F32)
        thr = sb("thr", [128, 1], F32)
        sel = sb("sel", [128, 2048], BF16)
        selT = sb("selT", [128, 16, 128], BF16)
        rlu = [sb("rlu%d" % i, [128, 512], F32) for i in range(2)]
        ex = [sb("ex%d" % i, [128, 512], BF16) for i in range(2)]
        pm = [sb("pm%d" % i, [128, 4, 128], BF16) for i in range(2)]
        rden = sb("rden", [128, 512], F32)
        otf = sb("otf", [128, 512], F32)
        OTt = sb("OTt", [128, 16, 128], BF16)
        Rb = sb("Rb", [128, 8, 256], BF16)
        inf = sb("inf", [128, 512], F32)
        Pm = sb("Pm", [128, 8, 128], BF16)
        st6 = sb("st6", [128, 8, 6], F32)
        mv = sb("mv", [128, 8, 2], F32)
        gs = sb("gs", [128, 4, 8], F32)
        yn = sb("yn", [128, 2048], BF16)
        otmp = sb("otmp", [128, 8, 128], F32)
        ORt = sb("ORt", [128, 16, 128], BF16)
        pT = [psum(nc, st, "pTB%d" % i, [128, 8, 128], BF16) for i in range(2)]
        pg = [psum(nc, st, "pg%d" % i, [128, 512], F32) for i in range(2)]
        poT = psum(nc, st, "poT", [128, 512], F32)
        pden = psum(nc, st, "pden", [128, 512], F32)
        py = [psum(nc, st, "py%d" % i, [128, 512], F32) for i in range(2)]
        ptc = [0]
        pgc = [0]
        exc = [0]

        def transposes(src_fn, n, evac_fn, rkeys):
            for g0 in range(0, n, 8):
                c = min(8, n - g0)
                pi = ptc[0] % 2
                ptc[0] += 1
                for kk in range(c):
                    P.op('pe', lambda e, pi=pi, kk=kk, g0=g0: e.transpose(out=pT[pi][:, kk, :], in_=src_fn(g0 + kk), identity=ident5[:]),
                         reads=rkeys + ['ident'], writes=[('pT', pi)], nosync_same=True)
                evac_fn(pT[pi], g0, c, ('pT', pi))

        for i in range(NT):
            r0, r1 = i * 128, (i + 1) * 128
            nk = 2 * i + 2
            n = nk * 128
            P.dma('sp', aqt[:], projQ[r0:r1, 0:2048], writes=['aqt'])
            P.dma('sp', iqt[:], projQ[r0:r1, 2048:3072], writes=['iqt'])
            P.dma('sp', iwt[:], projQ[r0:r1, 3072 + 64:3072 + 80], writes=['iwt'])
            P.dma('sp', rqt[:].rearrange("p a b c -> p (a b c)"), projQ[r0:r1, 3584:4608], writes=['rqt'])
            P.dma('sp', cqt[:].rearrange("p a c -> p (a c)"), d['cosq'][r0:r1, :], writes=['cqt'])
            P.dma('sp', sqt[:].rearrange("p a c -> p (a c)"), d['sinq'][r0:r1, :], writes=['sqt'])
            P.dma('sp', ktot[:], d['KTO'].rearrange("(h p) t -> p h t", p=128)[:, :, r0:r1], writes=['ktot'])
            P.dma('sp', rvot[:], d['RVO'][r0:r1, :], writes=['rvot'])
            P.dma('sp', U0[:].rearrange("p h v -> p (h v)"), d['UPDf_fn'](0, i), writes=['U0'])
            P.dma('sp', U1[:].rearrange("p h v -> p (h v)"), d['UPDf_fn'](1, i), writes=['U1'])
            P.dma('sp', srg[:], G[0].rearrange("(k p) t -> p k t", p=128)[:, :, r0:r1], writes=['srg'])
            head_rstd(P, aqt[:], 16, 128, sq, sm, ['aqt'], 'B')
            P.op('pool', lambda e: e.tensor_tensor(out=aqn[:], in0=aqt[:].rearrange("p (h d) -> p h d", d=128),
                                                   in1=sm[:, 3, 0:16].unsqueeze(2).to_broadcast([128, 16, 128]), op=ALU.mult),
                 reads=['aqt', 'Bsm'], writes=['aqn'])
            transposes(lambda k: aqn[:, k, :], 16,
                       lambda pt, g0, c, key: P.op('act', lambda e: e.activation(out=aqT[:, g0:g0 + c, :], in_=pt[:, 0:c, :], func=AF.Copy,
                                                                                 scale=qng[:, 1:2]),
                                                   reads=[key, 'qng'], writes=['aqT']), ['aqn'])
            P.op('pool', lambda e: e.tensor_copy(out=iqb[:], in_=iqt[:]), reads=['iqt'], writes=['iqb'])
            transposes(lambda k: iqb[:, k * 128:(k + 1) * 128], 8,
                       lambda pt, g0, c, key: P.op('act', lambda e: e.copy(out=iqT[:, g0:g0 + c, :], in_=pt[:, 0:c, :]),
                                                   reads=[key], writes=['iqT']), ['iqb'])
            P.op('pool', lambda e: e.tensor_scalar(out=iws[:], in0=iwt[:], scalar1=0.25, scalar2=None, op0=ALU.mult),
                 reads=['iwt'], writes=['iws'])
            rotary(P, 'pool', rqt, cqt[:], sqt[:], qb, rtmp, ['rqt', 'cqt', 'sqt'], 'qb', 'rtmpB')
            transposes(lambda k: qb[:, k, :, :].rearrange("p a c -> p (a c)"), 8,
                       lambda pt, g0, c, key: P.op('act', lambda e: e.copy(out=qT[:, g0:g0 + c, :], in_=pt[:, 0:c, :]),
                                                   reads=[key], writes=['qT']), ['qb'])
            for kc0 in range(0, n, 512):
                w = min(512, n - kc0)
                for h in range(16):
                    hp = h % 2
                    gi = pgc[0] % 2
                    pgc[0] += 1
                    P.op('pe', lambda e, gi=gi, h=h, hp=hp, kc0=kc0, w=w: e.matmul(
                        pg[gi][:, 0:w], lhsT=iqT[hp * 64:(hp + 1) * 64, h // 2, :], rhs=IKv[hp * 64:(hp + 1) * 64, kc0:kc0 + w],
                        start=True, stop=True), reads=['iqT'] + kside, writes=[('pg', gi)], nosync_same=True)
                    P.op('act', lambda e, gi=gi, w=w: e.activation(out=rlu[gi][:, 0:w], in_=pg[gi][:, 0:w], func=AF.Relu),
                         reads=[('pg', gi)], writes=[('rlu', gi)])
                    if h == 0:
                        P.op('dve', lambda e, gi=gi, w=w, kc0=kc0: e.tensor_scalar(out=score[:, kc0:kc0 + w], in0=rlu[gi][:, 0:w],
                                                                                   scalar1=iws[:, 0:1], scalar2=None, op0=ALU.mult),
                             reads=[('rlu', gi), 'iws'], writes=['score'])
                    else:
                        P.op('dve', lambda e, gi=gi, w=w, kc0=kc0, h=h: e.scalar_tensor_tensor(
                            out=score[:, kc0:kc0 + w], in0=rlu[gi][:, 0:w], scalar=iws[:, h:h + 1], in1=score[:, kc0:kc0 + w],
                            op0=ALU.mult, op1=ALU.add), reads=[('rlu', gi), 'iws', 'score'], writes=['score'])
            P.op('dve', lambda e, n=n: e.tensor_tensor(out=score[:, n - 256:n], in0=score[:, n - 256:n], in1=idxm[:], op=ALU.add),
                 reads=['score', 'idxm'], writes=['score'])
            for r in range(32):
                src = score if r == 0 else work
                P.op('dve', lambda e, src=src, n=n: e.max(out=m8[:], in_=src[:, 0:n]), reads=['score', 'work'], writes=['m8'])
                if r < 31:
                    P.op('dve', lambda e, src=src, n=n: e.match_replace(out=work[:, 0:n], in_to_replace=m8[:], in_values=src[:, 0:n],
                                                                        imm_value=NEG), reads=['score', 'work', 'm8'], writes=['work'])
            P.op('dve', lambda e: e.tensor_scalar(out=thr[:], in0=m8[:, 7:8], scalar1=-1.0e29, scalar2=None, op0=ALU.max),
                 reads=['m8'], writes=['thr'])
            P.op('dve', lambda e, n=n: e.tensor_scalar(out=sel[:, 0:n], in0=score[:, 0:n], scalar1=thr[:, 0:1], scalar2=None, op0=ALU.is_ge),
                 reads=['score', 'thr'], writes=['sel'])
            transposes(lambda k: sel[:, k * 128:(k + 1) * 128], nk,
                       lambda pt, g0, c, key: P.op('act', lambda e: e.copy(out=selT[:, g0:g0 + c, :], in_=pt[:, 0:c, :]),
                                                   reads=[key], writes=['selT']), ['sel'])
            for g in range(4):
                for kt in range(nk):
                    gi = pgc[0] % 2
                    pgc[0] += 1
                    xi = exc[0] % 2
                    exc[0] += 1
                    P.op('pe', lambda e, gi=gi, g=g, kt=kt: e.matmul(pg[gi][:], lhsT=KTv[:, g, kt * 128:(kt + 1) * 128],
                                                                     rhs=aqT[:, 4 * g:4 * g + 4, :].rearrange("p h q -> p (h q)"),
                                                                     start=True, stop=True),
                         reads=['aqT'] + kside, writes=[('pg', gi)], nosync_same=True)
                    P.op('act', lambda e, gi=gi, xi=xi: e.activation(out=ex[xi][:], in_=pg[gi][:], func=AF.Exp),
                         reads=[('pg', gi)], writes=[('ex', xi)])
                    P.op('pool', lambda e, xi=xi, kt=kt: e.tensor_tensor(out=pm[xi][:], in0=ex[xi][:].rearrange("p (h q) -> p h q", q=128),
                                                                         in1=selT[:, kt:kt + 1, :].to_broadcast([128, 4, 128]), op=ALU.mult),
                         reads=[('ex', xi), 'selT'], writes=[('pm', xi)])
                    P.op('pe', lambda e, xi=xi, g=g, kt=kt, nk=nk: e.matmul(poT[:], lhsT=Vv[:, kt, g * 128:(g + 1) * 128],
                                                                            rhs=pm[xi][:].rearrange("p h q -> p (h q)"),
                                                                            start=(kt == 0), stop=(kt == nk - 1)),
                         reads=[('pm', xi)] + kside, writes=['poT'], nosync_same=True)
                    P.op('pe', lambda e, xi=xi, kt=kt, nk=nk: e.matmul(pden[:], lhsT=ones[:], rhs=pm[xi][:].rearrange("p h q -> p (h q)"),
                                                                       start=(kt == 0), stop=(kt == nk - 1)),
                         reads=[('pm', xi), 'ones'], writes=['pden'], nosync_same=True)
                P.op('dve', lambda e: e.reciprocal(out=rden[:], in_=pden[:]), reads=['pden'], writes=['rden'])
                P.op('act', lambda e: e.copy(out=otf[:], in_=poT[:]), reads=['poT'], writes=['otf'])
                P.op('pool', lambda e, g=g: e.tensor_tensor(out=OTt[:, 4 * g:4 * g + 4, :], in0=otf[:].rearrange("p (h q) -> p h q", q=128),
                                                            in1=rden[:].rearrange("p (h q) -> p h q", q=128), op=ALU.mult),
                     reads=['otf', 'rden'], writes=['OTt'])
            P.dma('sp', d['OT'].rearrange("(h p) t -> p h t", p=128)[:, :, r0:r1], OTt[:], reads=['OTt'])
            P.op('pool', lambda e: e.tensor_tensor(out=U0[:], in0=U0[:], in1=S[:], op=ALU.add), reads=['U0', 'S'], writes=['U0'])
            for h in range(8):
                P.op('pool', lambda e, h=h: e.tensor_scalar(out=U0[:, h, :], in0=U0[:, h, :], scalar1=GC[h], scalar2=None, op0=ALU.mult),
                     reads=['U0'], writes=['U0'])
            P.op('pool', lambda e: e.tensor_tensor(out=U1[:], in0=U1[:], in1=U0[:], op=ALU.add), reads=['U0', 'U1'], writes=['U1'])
            for h in range(8):
                P.op('pool', lambda e, h=h: e.tensor_scalar(out=U1[:, h, :], in0=U1[:, h, :], scalar1=GC[h], scalar2=None, op0=ALU.mult),
                     reads=['U1'], writes=['U1'])
            P.op('pool', lambda e: e.tensor_scalar(out=S[:], in0=S[:], scalar1=cfl[:, 1:2], scalar2=None, op0=ALU.mult),
                 reads=['S', 'cfl'], writes=['S'])
            P.op('pool', lambda e: e.tensor_scalar(out=U0[:], in0=U0[:], scalar1=cfl[:, 0:1], scalar2=None, op0=ALU.mult),
                 reads=['U0', 'cfl'], writes=['U0'])
            P.op('pool', lambda e: e.tensor_tensor(out=Rb[:], in0=S[:], in1=U0[:], op=ALU.add), reads=['S', 'U0'], writes=['Rb'])
            P.op('pool', lambda e: e.tensor_copy(out=S[:], in_=U1[:]), reads=['U1', 'Rb'], writes=['S'])
            for hh in range(2):
                gi = pgc[0] % 2
                pgc[0] += 1
                for h4 in range(4):
                    h = hh * 4 + h4
                    P.op('pe', lambda e, gi=gi, h=h, h4=h4: e.matmul(pg[gi][:, h4 * 128:(h4 + 1) * 128], lhsT=ktot[:, h, :], rhs=qT[:, h, :],
                                                                     start=True, stop=True),
                         reads=['ktot', 'qT'], writes=[('pg', gi)], nosync_same=True)
                P.op('act', lambda e, gi=gi: e.copy(out=inf[:], in_=pg[gi][:]), reads=[('pg', gi)], writes=['inf'])
                P.op('pool', lambda e, hh=hh: e.tensor_tensor(out=Pm[:, hh * 4:hh * 4 + 4, :], in0=inf[:].rearrange("p (h q) -> p h q", q=128),
                                                              in1=tri[:].unsqueeze(1).to_broadcast([128, 4, 128]), op=ALU.mult),
                     reads=['inf', 'tri'], writes=[('Pm', hh)])
            for hh in range(2):
                for h4 in range(4):
                    h = hh * 4 + h4
                    yo = py[h4 // 2][:, (h4 % 2) * 256:(h4 % 2) * 256 + 256]
                    P.op('pe', lambda e, yo=yo, h=h: e.matmul(yo, lhsT=Pm[:, h, :], rhs=rvot[:, h * 256:(h + 1) * 256], start=True, stop=False),
                         reads=[('Pm', hh), 'rvot'], writes=[('py', h4 // 2)], nosync_same=True)
                    P.op('pe', lambda e, yo=yo, h=h: e.matmul(yo, lhsT=qT[:, h, :], rhs=Rb[:, h, :], start=False, stop=True),
                         reads=['qT', 'Rb'], writes=[('py', h4 // 2)], nosync_same=True)
                for h4 in range(4):
                    h = hh * 4 + h4
                    yo = py[h4 // 2][:, (h4 % 2) * 256:(h4 % 2) * 256 + 256]
                    P.op('dve', lambda e, yo=yo, h=h: e.bn_stats(out=st6[:, h, :], in_=yo), reads=[('py', h4 // 2)], writes=['st6'])
                    P.op('dve', lambda e, h=h: e.bn_aggr(out=mv[:, h, :], in_=st6[:, h, :]), reads=['st6'], writes=['mv'])
                sl = slice(hh * 4, hh * 4 + 4)
                P.op('dve', lambda e, sl=sl: e.tensor_scalar(out=gs[:, 0, sl], in0=mv[:, sl, 1], scalar1=EPS, scalar2=None, op0=ALU.add),
                     reads=['mv'], writes=['gs'])
                P.op('act', lambda e, sl=sl: e.activation(out=gs[:, 1, sl], in_=gs[:, 0, sl], func=AF.Sqrt), reads=['gs'], writes=['gs'])
                P.op('dve', lambda e, sl=sl: e.reciprocal(out=gs[:, 2, sl], in_=gs[:, 1, sl]), reads=['gs'], writes=['gs'])
                for h4 in range(4):
                    h = hh * 4 + h4
                    yo = py[h4 // 2][:, (h4 % 2) * 256:(h4 % 2) * 256 + 256]
                    P.op('dve', lambda e, yo=yo, h=h: e.tensor_scalar(out=yn[:, h * 256:(h + 1) * 256], in0=yo, scalar1=mv[:, h, 0:1],
                                                                      scalar2=gs[:, 2, h:h + 1], op0=ALU.subtract, op1=ALU.mult),
                         reads=[('py', h4 // 2), 'mv', 'gs'], writes=['yn'])
            def evac_or(pt, g0, c, key):
                P.op('dve', lambda e: e.tensor_tensor(out=otmp[:, 0:c, :], in0=pt[:, 0:c, :],
                                                      in1=gng[:, g0:g0 + c].unsqueeze(2).to_broadcast([128, c, 128]), op=ALU.mult),
                     reads=[key, 'gng'], writes=['otmp'])
                P.op('pool', lambda e: e.tensor_tensor(out=otmp[:, 0:c, :], in0=otmp[:, 0:c, :],
                                                       in1=gnb[:, g0:g0 + c].unsqueeze(2).to_broadcast([128, c, 128]), op=ALU.add),
                     reads=['otmp', 'gnb'], writes=['otmp'])
                P.op('pool', lambda e: e.tensor_tensor(out=ORt[:, g0:g0 + c, :], in0=otmp[:, 0:c, :], in1=srg[:, g0:g0 + c, :], op=ALU.mult),
                     reads=['otmp', 'srg'], writes=['ORt'])
            transposes(lambda k: yn[:, k * 128:(k + 1) * 128], 16, evac_or, ['yn'])
            P.dma('sp', d['ORT'].rearrange("(h p) t -> p h t", p=128)[:, :, r0:r1], ORt[:], reads=['ORt'])
            if 'DBG' in d and i == d.get('_dbg_tile', 0):
                P.barrier()
                D_ = d['DBG']
                dumps = [(aqT[:].rearrange("p h q -> p (h q)"), 0, 2048), (iqT[:].rearrange("p h q -> p (h q)"), 2048, 1024),
                         (qT[:].rearrange("p h q -> p (h q)"), 3072, 1024), (score[:], 4096, 2048), (sel[:], 6144, 2048),
                         (selT[:].rearrange("p h q -> p (h q)"), 8192, 2048), (rden[:], 10240, 512), (otf[:], 10752, 512),
                         (Rb[:].rearrange("p h q -> p (h q)"), 11264, 2048), (yn[:], 13312, 2048), (m8[:], 15360, 8), (thr[:].to_broadcast([128, 2]) if False else m8[:, 6:8], 15368, 2),
                         (mv[:].rearrange("p h q -> p (h q)"), 15376, 16), (gs[:].rearrange("p h q -> p (h q)"), 15392, 32),
                         (iqb[:], 16384, 1024), (iqt[:], 17408, 1024), (qb[:].rearrange("p a b c -> p (a b c)"), 18432, 1024), (aqn[:, 0:8, :].rearrange("p h q -> p (h q)"), 19456, 1024)]
                for (src, o, w) in dumps:
                    P.dma('pool', D_[:, o:o + w], src)
                P.barrier()
        P.barrier()
    with contextlib.ExitStack() as st:
        OTa = sbuf(nc, st, "OTa", [128, KC, T], BF16)
        ORa = sbuf(nc, st, "ORa", [128, KC, T], BF16)
        MTa = sbuf(nc, st, "MTa", [128, KC, T], BF16)
        P.dma('sp', OTa[:], d['OT'].rearrange("(k p) t -> p k t", p=128), writes=[('OTa', q) for q in range(8)])
        P.dma('sp', ORa[:], d['ORT'].rearrange("(k p) t -> p k t", p=128), writes=[('ORa', q) for q in range(8)])
        Gm = Gemm(P, nc, st, 6, "g6")
        tmpA = [sbuf(nc, st, "tmpA%d" % i, [128, 512], F32) for i in range(8)]
        gta = [sbuf(nc, st, "gta%d" % i, [128, 512], F32) for i in range(2)]
        mm = [sbuf(nc, st, "mm%d" % i, [128, 512], F32) for i in range(2)]
        cnt = [0]
        for nb in range(4):
            def epiA(ti, ps, akey, nb=nb):
                s = cnt[0] % 2
                cnt[0] += 1
                mc, tg = ti // 2, ti % 2
                f0 = nb * 512 + mc * 128
                P.dma('sp', gta[s][:], G[1, f0:f0 + 128, tg * 512:(tg + 1) * 512], writes=[('gta', s)])
                P.op('dve', lambda e: e.tensor_tensor(out=tmpA[ti][:], in0=ps[:], in1=gta[s][:], op=ALU.mult),
                     reads=[akey, ('gta', s)], writes=[('tmpA', ti)])
            Gm.block(d['w_ua'][:, nb * 512:(nb + 1) * 512], 512, 'FM', OTa, 'OTa', epiA)

            def epiR(ti, ps, akey, nb=nb):
                s = cnt[0] % 2
                cnt[0] += 1
                mc, tg = ti // 2, ti % 2
                f0 = nb * 512 + mc * 128
                P.dma('sp', gta[s][:], G[2, f0:f0 + 128, tg * 512:(tg + 1) * 512], writes=[('gta', s)])
                P.op('dve', lambda e: e.tensor_tensor(out=mm[s][:], in0=ps[:], in1=gta[s][:], op=ALU.mult),
                     reads=[akey, ('gta', s)], writes=[('mm', s)])
                P.op('pool', lambda e: e.tensor_tensor(out=MTa[:, nb * 4 + mc, tg * 512:(tg + 1) * 512], in0=mm[s][:], in1=tmpA[ti][:],
                                                       op=ALU.add), reads=[('mm', s), ('tmpA', ti)], writes=[('MTa', nb * 4 + mc, tg)])
            Gm.block(d['w_ur'][:, nb * 512:(nb + 1) * 512], 512, 'FM', ORa, 'ORa', epiR)
        P.dma('sp', d['MT'].rearrange("(k p) t -> p k t", p=128), MTa[:], reads=[('MTa', k, tg) for k in range(KC) for tg in range(2)])
        P.barrier()
    with contextlib.ExitStack() as st:
        xres = sbuf(nc, st, "xres", [128, NT, D], F32)
        for t in range(NT):
            P.dma('sp', xres[:, t, :], d['x'][t * 128:(t + 1) * 128, :], writes=[('xres', t)])
        with contextlib.ExitStack() as st2:
            MTb = sbuf(nc, st2, "MTb", [128, KC, T], BF16)
            P.dma('sp', MTb[:], d['MT'].rearrange("(k p) t -> p k t", p=128), writes=[('MTb', q) for q in range(8)])
            Gm = Gemm(P, nc, st2, 6, "g7")
            for nb in range(4):
                def epi(ti, ps, akey, nb=nb):
                    P.op('dve', lambda e: e.tensor_tensor(out=xres[:, ti, nb * 512:(nb + 1) * 512], in0=ps[:],
                                                          in1=xres[:, ti, nb * 512:(nb + 1) * 512], op=ALU.add),
                         reads=[akey, ('xres', ti)], writes=[('xres', ti)])
                Gm.block(d['w_out'][:, nb * 512:(nb + 1) * 512], 512, 'TM', MTb, 'MTb', epi)
            P.barrier()
        with contextlib.ExitStack() as st2:
            ident8 = sbuf(nc, st2, "identB8", [128, 128], BF16)
            P.dma('pool', ident8[:], d['ident'], writes=['ident'])
            h2T = sbuf(nc, st2, "h2T", [128, KC, T], BF16)
            with contextlib.ExitStack() as st3:
                ln_phase(P, nc, st3, None, d['ln2g'], h2T, ident8, "ln2", xres=xres)
                P.barrier()
            aT = sbuf(nc, st2, "aT", [128, KC, T], BF16)
            Gm = Gemm(P, nc, st2, 6, "g8")
            rl = [sbuf(nc, st2, "rl%d" % i, [128, 512], F32) for i in range(2)]
            cnt = [0]
            for kg in range(4):
                for nb in range(4):
                    def epi1(ti, ps, akey, nb=nb):
                        s = cnt[0] % 2
                        cnt[0] += 1
                        mc, tg = ti // 2, ti % 2
                        P.op('act', lambda e: e.activation(out=rl[s][:], in_=ps[:], func=AF.Relu), reads=[akey], writes=[('rl', s)])
                        P.op('pool', lambda e: e.tensor_tensor(out=aT[:, nb * 4 + mc, tg * 512:(tg + 1) * 512], in0=rl[s][:], in1=rl[s][:],
                                                               op=ALU.mult), reads=[('rl', s)], writes=[('aT', nb * 4 + mc, tg)])
                    Gm.block(d['w_ff1'][:, kg * 2048 + nb * 512:kg * 2048 + (nb + 1) * 512], 512, 'FM', h2T, 'hT', epi1)
                for nb in range(4):
                    def epi2(ti, ps, akey, nb=nb):
                        P.op('dve', lambda e: e.tensor_tensor(out=xres[:, ti, nb * 512:(nb + 1) * 512], in0=ps[:],
                                                              in1=xres[:, ti, nb * 512:(nb + 1) * 512], op=ALU.add),
                             reads=[akey, ('xres', ti)], writes=[('xres', ti)])
                    Gm.block(d['w_ff2'][kg * 2048:(kg + 1) * 2048, nb * 512:(nb + 1) * 512], 512, 'TM', aT, 'aT', epi2,
                             keyfn=lambda mode, ti: [('aT', k, ti // 4) for k in range(KC)])
            for t in range(NT):
                P.dma('sp', d['xout'][t * 128:(t + 1) * 128, :], xres[:, t, :], reads=[('xres', t)])
            P.barrier()


def _dr(nc, name, shape, dt, kind):
    return nc.dram_tensor(name, list(shape), dt, kind=kind).ap()


A_IN = dict(x=([T, D], F32), w_in=([D, D_IN], F32), ln1g=([128, KC], F32), kng=([128, 1], F32),
            cosk=([T, 512], F32), sink=([T, 512], F32), ident=([128, 128], F32))
A_OUT = dict(KT=([512, T], BF16), V=([T, 512], BF16), IKT=([128, T], BF16), UPD=([T, D], F32),
             KTO=([1024, T], BF16), RVO=([T, D], BF16))
A_TMP = dict(projK=([T, 4608], F32))
B_IN = dict(x=([T, D], F32), w_in=([D, D_IN], F32), w_ua=([D, D], F32), w_ur=([D, D], F32), w_out=([D, D], F32),
            w_ff1=([D, 4 * D], F32), w_ff2=([4 * D, D], F32), ln1g=([128, KC], F32), ln2g=([128, KC], F32),
            qng=([128, 1], F32), gng=([128, KC], F32), gnb=([128, KC], F32), cosq=([T, 512], F32), sinq=([T, 512], F32),
            ident=([128, 128], F32), tri=([128, 128], F32), idxmask=([128, 256], F32), cflag=([128, 2], F32),
            KTf=([2, 512, T], BF16), Vf=([2, T, 512], BF16), IKTf=([2, 128, T], BF16), UPDf=([2, T, D], F32),
            KTO=([1024, T], BF16), RVO=([T, D], BF16))
B_OUT = dict(xout=([T, D], F32))
B_DBG = dict(DBG=([128, 20480], F32))
B_TMP = dict(projQ=([T, 4608], F32), G=([3, D, T], F32), OT=([D, T], BF16), ORT=([D, T], BF16), MT=([D, T], BF16))


def build_A(debug=()):
    nc = bass.Bass("TRN2", target_bir_lowering=False)
    d = {}
    for k, (s, t) in A_IN.items():
        d[k] = _dr(nc, k, s, t, "ExternalInput")
    for k, (s, t) in A_OUT.items():
        d[k] = _dr(nc, k, s, t, "ExternalOutput")
    for k, (s, t) in A_TMP.items():
        d[k] = _dr(nc, k, s, t, "ExternalOutput" if k in debug else "Internal")
    P = Prog(nc)
    phase_A(P, nc, d)
    P.emit()
    return nc


def build_B(debug=()):
    nc = bass.Bass("TRN2", target_bir_lowering=False)
    d = {}
    for k, (s, t) in B_IN.items():
        d[k] = _dr(nc, k, s, t, "ExternalInput")
    for k, (s, t) in B_OUT.items():
        d[k] = _dr(nc, k, s, t, "ExternalOutput")
    for k, (s, t) in B_TMP.items():
        d[k] = _dr(nc, k, s, t, "ExternalOutput" if k in debug else "Internal")
    if 'DBG' in debug:
        d['DBG'] = _dr(nc, 'DBG', [128, 20480], F32, "ExternalOutput")
    P = Prog(nc)
    phase_B(P, nc, d)
    P.emit()
    return nc


def _pk(v):
    return np.ascontiguousarray(np.asarray(v, np.float32).reshape(KC, 128).T)


def const_tables(c):
    i = np.arange(NT)[:, None]
    r = np.arange(128)[None, :]
    pos = (128 * (2 * i + c) + r).reshape(-1).astype(np.float64)
    inv = 10000.0 ** (-np.arange(0, 128, 2, dtype=np.float64) / 128.0)
    ang = pos[:, None] * inv[None, :]
    cos = np.cos(ang.astype(np.float32).astype(np.float64))
    sin = np.sin(ang.astype(np.float32).astype(np.float64))
    lg = np.log(np.asarray(GAMMA, np.float64))
    rr = np.tile(np.arange(128, dtype=np.float64), NT)
    xiq = np.exp(lg[None, :] * (rr[:, None] + 1.0))
    xik = np.exp(-lg[None, :] * (rr[:, None] + 1.0)) * 128.0 ** -0.5
    tabs = {}
    tabs['cosq'] = (cos[:, None, :] * xiq[:, :, None]).reshape(T, 512).astype(np.float32)
    tabs['sinq'] = (sin[:, None, :] * xiq[:, :, None]).reshape(T, 512).astype(np.float32)
    tabs['cosk'] = (cos[:, None, :] * xik[:, :, None]).reshape(T, 512).astype(np.float32)
    tabs['sink'] = (sin[:, None, :] * xik[:, :, None]).reshape(T, 512).astype(np.float32)
    tabs['ident'] = np.eye(128, dtype=np.float32)
    j = np.arange(128)
    tabs['tri'] = (j[:, None] <= j[None, :]).astype(np.float32)
    causal = np.where(j[None, :] <= j[:, None], 0.0, NEG).astype(np.float32)
    full = np.full((128, 128), NEG, np.float32)
    zero = np.zeros((128, 128), np.float32)
    tabs['idxmask'] = np.concatenate([causal, full], 1) if c == 0 else np.concatenate([zero, causal], 1)
    tabs['cflag'] = np.tile(np.array([[float(c), 1.0 - float(c)]], np.float32), (128, 1))
    return tabs


def shard_x(x):
    out = []
    for b in range(4):
        xt = x[b].reshape(16, 128, D)
        for c in range(2):
            out.append(np.ascontiguousarray(xt[c::2].reshape(T, D)))
    return out


def unshard_x(parts):
    out = np.empty((4, 2048, D), np.float32)
    for b in range(4):
        xt = out[b].reshape(16, 128, D)
        for c in range(2):
            xt[c::2] = parts[2 * b + c].reshape(NT, 128, D)
    return out


_CACHE = {}
RG = [[0, 1], [2, 3], [4, 5], [6, 7]]

F_IN = dict(x=([T, D], F32), w_in=([2, D, D_IN], F32), w_ua=([2, D, D], F32), w_ur=([2, D, D], F32), w_out=([2, D, D], F32),
            w_ff1=([2, D, 4 * D], F32), w_ff2=([2, 4 * D, D], F32), ln1g=([2, 128, KC], F32), ln2g=([2, 128, KC], F32),
            qng=([2, 128, 1], F32), kng=([2, 128, 1], F32), gng=([2, 128, KC], F32), gnb=([2, 128, KC], F32),
            cosq=([T, 512], F32), sinq=([T, 512], F32), cosk=([T, 512], F32), sink=([T, 512], F32),
            ident=([128, 128], F32), tri=([128, 128], F32), idxmask=([128, 256], F32), cflag=([128, 2], F32))


def build_fused():
    nc = bass.Bass("TRN2", target_bir_lowering=False)
    I = {k: _dr(nc, k, s_, t_, "ExternalInput") for k, (s_, t_) in F_IN.items()}
    xout = _dr(nc, "xout", [T, D], F32, "ExternalOutput")
    x1 = _dr(nc, "x1", [T, D], F32, "Internal")
    P = Prog(nc)
    for l in range(2):
        tmp = lambda nm, shp, dt: _dr(nc, "%s_l%d" % (nm, l), shp, dt, "Internal")
        SB = tmp("SB", [1152, T], BF16)
        RBa = tmp("RBa", [2 * 512, T], BF16)
        RBb = tmp("RBb", [2 * 640, T], BF16)
        US = tmp("US", [T, D], F32)
        UB = tmp("UB", [2 * T, D], F32)
        xin = I['x'] if l == 0 else x1
        dA = dict(x=xin, w_in=I['w_in'][l], ln1g=I['ln1g'][l], kng=I['kng'][l], cosk=I['cosk'], sink=I['sink'], ident=I['ident'],
                  projK=tmp("projK", [T, 4608], F32), KT=SB[0:512, :],
                  V=SB[512:1024, :].rearrange("a (two c) -> (a two) c", two=2), IKT=SB[1024:1152, :], UPD=US,
                  KTO=tmp("KTO", [1024, T], BF16), RVO=tmp("RVO", [T, D], BF16))
        phase_A(P, nc, dA)
        P.collective(lambda e, SB=SB, RBa=RBa: e.collective_compute("AllGather", ALU.bypass, replica_groups=RG, ins=[SB[0:512, :]],
                                                                    outs=[RBa]), writes=['RBa'])
        P.collective(lambda e, SB=SB, RBb=RBb: e.collective_compute("AllGather", ALU.bypass, replica_groups=RG, ins=[SB[512:1152, :]],
                                                                    outs=[RBb]), writes=['RBb'])
        for j in range(4):
            P.collective(lambda e, US=US, UB=UB, j=j: e.collective_compute("AllGather", ALU.bypass, replica_groups=RG,
                                                                           ins=[US[j * 256:(j + 1) * 256, :]],
                                                                           outs=[UB[j * 512:(j + 1) * 512, :]]), writes=[('UB', j)])
        P.barrier()
        RBa3 = RBa.rearrange("(r a) t -> r a t", r=2)
        RBb3 = RBb.rearrange("(r a) t -> r a t", r=2)
        updf = lambda c2, i, UB=UB: UB[(i // 2) * 512 + c2 * 256 + (i % 2) * 128:(i // 2) * 512 + c2 * 256 + (i % 2) * 128 + 128, :]
        dB = dict(x=xin, xout=(x1 if l == 0 else xout), w_in=I['w_in'][l], w_ua=I['w_ua'][l], w_ur=I['w_ur'][l], w_out=I['w_out'][l],
                  w_ff1=I['w_ff1'][l], w_ff2=I['w_ff2'][l], ln1g=I['ln1g'][l], ln2g=I['ln2g'][l], qng=I['qng'][l], gng=I['gng'][l],
                  gnb=I['gnb'][l], cosq=I['cosq'], sinq=I['sinq'], ident=I['ident'], tri=I['tri'], idxmask=I['idxmask'],
                  cflag=I['cflag'], KTf=RBa3, Vf=RBb3[:, 0:512, :].rearrange("r a (two c) -> r (a two) c", two=2),
                  IKTf=RBb3[:, 512:640, :], UPDf_fn=updf, KTO=dA['KTO'], RVO=dA['RVO'],
                  projQ=tmp("projQ", [T, 4608], F32), G=tmp("G", [3, D, T], F32), OT=tmp("OT", [D, T], BF16),
                  ORT=tmp("ORT", [D, T], BF16), MT=tmp("MT", [D, T], BF16))
        phase_B(P, nc, dB)
    P.emit()
    return nc


def kernel(x, ln1_g, w_in, q_norm_g, k_norm_g, ret_gn_g, ret_gn_b, w_up_attn, w_up_ret, w_out, ln2_g, w_ff1, w_ff2):
    f = lambda a: np.ascontiguousarray(np.asarray(a, dtype=np.float32))
    if 'F' not in _CACHE:
        _CACHE['F'] = build_fused()
    nc = _CACHE['F']
    tabs = [const_tables(c) for c in range(2)]
    xs = shard_x(f(x))
    pk2 = lambda v: np.stack([_pk(v[l]) for l in range(2)])
    shared = dict(w_in=f(w_in), w_ua=f(w_up_attn), w_ur=f(w_up_ret), w_out=f(w_out), w_ff1=f(w_ff1), w_ff2=f(w_ff2),
                  ln1g=pk2(ln1_g), ln2g=pk2(ln2_g), gng=pk2(ret_gn_g), gnb=pk2(ret_gn_b),
                  qng=f(q_norm_g).reshape(2, 128, 1), kng=f(k_norm_g).reshape(2, 128, 1))
    in_maps = []
    for k in range(8):
        m = dict(shared)
        m['x'] = xs[k]
        for nm in ('cosq', 'sinq', 'cosk', 'sink', 'ident', 'tri', 'idxmask', 'cflag'):
            m[nm] = tabs[k % 2][nm]
        in_maps.append(m)
    res = run_bass_kernel_spmd(nc, in_maps, core_ids=list(range(8))).results
    return unshard_x([np.asarray(res[k]['xout'], np.float32) for k in range(8)])
```

```python
import contextlib
import numpy as np
import ml_dtypes
import concourse.bass as bass
import concourse.mybir as mybir
from concourse.bass_utils import run_bass_kernel_spmd

F32 = mybir.dt.float32
BF16 = mybir.dt.bfloat16
AF = mybir.ActivationFunctionType
ALU = mybir.AluOpType
AX = mybir.AxisListType

D = 2048
T = 1024
NT = 8
KC = 16
D_IN = 14416
EPS = 1e-6
NEG = -1.0e30
O_AQ, O_AK, O_AV, O_IQ, O_IK, O_IW, O_RQ, O_RK, O_RV, O_RG, O_GA, O_GB = (
    0, 2048, 2560, 3072, 4096, 4160, 4176, 5200, 6224, 8272, 10320, 12368)
GAMMA = [1.0 - 2.0 ** (-5.0 - h) for h in range(8)]
GC = [float(np.float32(np.exp(np.float32(np.log(np.float32(g))) * np.float32(128.0)))) for g in GAMMA]

ENGS = ("pe", "act", "dve", "pool", "sp")
NDMASEM = 8


class Prog:
    def __init__(self, nc):
        self.nc = nc
        self.ops = {e: [] for e in ENGS}
        self.bufs = {}
        self.dma_rr = {e: 0 for e in ENGS}
        self.dma_cnt = {}
        self.dma_last = {}

    def _deps(self, reads, writes):
        deps = []
        for b in reads:
            st = self.bufs.get(b)
            if st and st[0] is not None:
                deps.append(st[0])
        for b in writes:
            st = self.bufs.get(b)
            if st:
                if st[0] is not None:
                    deps.append(st[0])
                deps.extend(st[1])
        return deps

    def _record(self, tok, reads, writes):
        for b in reads:
            st = self.bufs.setdefault(b, [None, []])
            st[1].append(tok)
        for b in writes:
            self.bufs[b] = [tok, []]

    def op(self, eng, fn, reads=(), writes=(), nosync_same=False):
        deps = self._deps(reads, writes)
        if nosync_same:
            deps = [d for d in deps if not (d[0] == 'e' and d[1] == eng)]
        tok = ('e', eng, len(self.ops[eng]))
        self.ops[eng].append(['op', fn, deps, tok])
        self._record(tok, reads, writes)
        return tok

    def dma(self, eng, out, in_, reads=(), writes=(), **kw):
        deps = self._deps(reads, writes)
        k = self.dma_rr[eng]
        self.dma_rr[eng] = (k + 1) % NDMASEM
        prev = self.dma_last.get((eng, k))
        if prev is not None:
            deps.append(prev)
        val = self.dma_cnt.get((eng, k), 0) + 16
        self.dma_cnt[(eng, k)] = val
        tok = ('d', eng, k, val)
        self.dma_last[(eng, k)] = tok

        def fn(e, out=out, in_=in_, kw=kw):
            return e.dma_start(out=out, in_=in_, **kw)
        self.ops[eng].append(['dma', fn, deps, tok])
        self._record(tok, reads, writes)
        return tok

    def collective(self, fn, reads=(), writes=()):
        eng, k = 'pool', 'cc'
        deps = self._deps(reads, writes)
        prev = self.dma_last.get((eng, k))
        if prev is not None:
            deps.append(prev)
        val = self.dma_cnt.get((eng, k), 0) + 1
        self.dma_cnt[(eng, k)] = val
        tok = ('d', eng, k, val)
        self.dma_last[(eng, k)] = tok
        self.ops[eng].append(['cc', fn, deps, tok])
        self._record(tok, reads, writes)
        return tok

    def barrier(self):
        toks = []
        for e in ENGS:
            for o in reversed(self.ops[e]):
                if o[3] is not None and o[3][0] == 'e':
                    toks.append(o[3])
                    break
        for key, tok in self.dma_last.items():
            toks.append(tok)
        for e in ENGS:
            self.ops[e].append(['bar', None, list(toks), None])
        self.bufs = {}

    def emit(self):
        nc = self.nc
        needed = {e: set() for e in ENGS}
        for e in ENGS:
            for o in self.ops[e]:
                for d in o[2]:
                    if d[0] == 'e':
                        needed[d[1]].add(d[2])
        rank = {}
        for e in ENGS:
            r = 0
            for i in sorted(needed[e]):
                r += 1
                rank[(e, i)] = r
        with contextlib.ExitStack() as st:
            esem = {e: st.enter_context(nc.semaphore("s_" + e)) for e in ENGS}
            dsem = {}
            for (e, k) in self.dma_cnt:
                dsem[(e, k)] = st.enter_context(nc.semaphore("d_%s%s" % (e, k)))
            block = st.enter_context(nc.Block())
            engobj = {"pe": "tensor", "act": "scalar", "dve": "vector", "pool": "gpsimd", "sp": "sync"}

            def body_for(e):
                def body(eng):
                    waited = {}
                    for kind, fn, deps, tok in self.ops[e]:
                        for d in deps:
                            if d[0] == 'e':
                                s, v = esem[d[1]], rank[(d[1], d[2])]
                                key = ('e', d[1])
                            else:
                                s, v = dsem[(d[1], d[2])], d[3]
                                key = ('d', d[1], d[2])
                            if waited.get(key, 0) >= v:
                                continue
                            waited[key] = v
                            eng.wait_ge(s, v)
                        if kind == 'op':
                            ins = fn(eng)
                            if tok[2] in needed[e]:
                                ins.then_inc(esem[e], 1)
                        elif kind == 'dma':
                            ins = fn(eng)
                            ins.then_inc(dsem[(tok[1], tok[2])], 16)
                        elif kind == 'cc':
                            ins = fn(eng)
                            ins.then_inc(dsem[(tok[1], tok[2])])
                return body
            for e in ENGS:
                getattr(block, engobj[e])(body_for(e))


class Ctx:
    pass


_UID = [0]


def sbuf(nc, st, name, shape, dt):
    _UID[0] += 1
    return st.enter_context(nc.sbuf_tensor("sb%d_%s" % (_UID[0], name), shape, dt))


def psum(nc, st, name, shape, dt):
    _UID[0] += 1
    return st.enter_context(nc.psum_tensor("ps%d_%s" % (_UID[0], name), shape, dt))


def ln_phase(P, nc, st, x_dram, g_dram, hT, ident, tag, xres=None):
    gt = sbuf(nc, st, tag + "gt", [128, KC], F32)
    P.dma('sp', gt[:], g_dram, writes=[tag + 'gt'])
    xt = [sbuf(nc, st, tag + "xt%d" % i, [128, D], F32) for i in range(2)] if xres is None else None
    hb = [sbuf(nc, st, tag + "hb%d" % i, [128, D], BF16) for i in range(2)]
    junk = sbuf(nc, st, tag + "junk", [128, D], F32)
    sm = [sbuf(nc, st, tag + "sm%d" % i, [128, 4], F32) for i in range(2)]
    pT = [psum(nc, st, tag + "pT%d" % i, [128, 8, 128], BF16) for i in range(2)]
    for t in range(NT):
        b = t % 2
        if xres is None:
            P.dma('sp', xt[b][:], x_dram[t * 128:(t + 1) * 128, :], writes=[(tag, 'xt', b)])
            xin = xt[b][:]
            rk = [(tag, 'xt', b)]
        else:
            xin = xres[:, t, :]
            rk = [('xres', t)]
        s = sm[b]
        P.op('act', lambda e, xin=xin: e.activation(out=junk[:], in_=xin, func=AF.Square),
             reads=rk, writes=[(tag, 'junk')])
        P.op('dve', lambda e, s=s: e.tensor_reduce(out=s[:, 0:1], in_=junk[:], axis=AX.X, op=ALU.add),
             reads=[(tag, 'junk')], writes=[(tag, 'sm', b)])
        P.op('dve', lambda e, s=s: e.tensor_scalar(out=s[:, 1:2], in0=s[:, 0:1], scalar1=1.0 / D, scalar2=EPS,
                                                   op0=ALU.mult, op1=ALU.add),
             reads=[(tag, 'sm', b)], writes=[(tag, 'sm', b)])
        P.op('act', lambda e, s=s: e.activation(out=s[:, 2:3], in_=s[:, 1:2], func=AF.Sqrt),
             reads=[(tag, 'sm', b)], writes=[(tag, 'sm', b)])
        P.op('dve', lambda e, s=s: e.reciprocal(out=s[:, 3:4], in_=s[:, 2:3]),
             reads=[(tag, 'sm', b)], writes=[(tag, 'sm', b)])
        P.op('act', lambda e, xin=xin, s=s, b=b: e.activation(out=hb[b][:], in_=xin, func=AF.Copy, scale=s[:, 3:4]),
             reads=rk + [(tag, 'sm', b)], writes=[(tag, 'hb', b)])
        for half in range(2):
            pb = pT[half]
            for kk in range(8):
                k = half * 8 + kk
                P.op('pe', lambda e, pb=pb, kk=kk, k=k, b=b: e.transpose(out=pb[:, kk, :], in_=hb[b][:, k * 128:(k + 1) * 128],
                                                                         identity=ident[:]),
                     reads=[(tag, 'hb', b), 'ident'], writes=[(tag, 'pT', half)], nosync_same=True)
            P.op('dve', lambda e, pb=pb, half=half, t=t: e.tensor_tensor(
                out=hT[:, half * 8:half * 8 + 8, t * 128:(t + 1) * 128], in0=pb[:],
                in1=gt[:, half * 8:half * 8 + 8].unsqueeze(2).to_broadcast([128, 8, 128]), op=ALU.mult),
                reads=[(tag, 'pT', half), tag + 'gt'], writes=[('hT', t)])


class Gemm:
    def __init__(self, P, nc, st, nacc, tag):
        self.P, self.nc = P, nc
        self.Wb = [sbuf(nc, st, tag + "Wb%d" % i, [128, KC, 512], BF16) for i in range(2)]
        self.Wf = [sbuf(nc, st, tag + "Wf%d" % i, [128, 8, 512], F32) for i in range(2)]
        self.fcnt = 0
        self.acc = [psum(nc, st, tag + "acc%d" % i, [128, 512], F32) for i in range(nacc)]
        self.wcnt = 0
        self.acnt = 0
        self.tag = tag

    def block(self, w_ap, ncols, mode, actT, act_key, epi, tiles=range(8), keyfn=None):
        P = self.P
        slot = self.wcnt % 2
        self.wcnt += 1
        Wb = self.Wb[slot]
        wkeys = [(self.tag, 'W', slot, 0), (self.tag, 'W', slot, 1)]
        for half in range(2):
            fs = self.fcnt % 2
            self.fcnt += 1
            Wf = self.Wf[fs]
            fkey = (self.tag, 'Wf', fs)
            P.dma('sp', Wf[:, :, 0:ncols], w_ap[half * 1024:(half + 1) * 1024, :].rearrange("(k p) n -> p k n", p=128), writes=[fkey])
            if half == 0:
                P.op('act', lambda e, Wf=Wf, half=half: e.copy(out=Wb[:, half * 8:half * 8 + 8, 0:ncols], in_=Wf[:, :, 0:ncols]),
                     reads=[fkey], writes=[wkeys[half]])
            else:
                P.op('dve', lambda e, Wf=Wf, half=half: e.tensor_copy(out=Wb[:, half * 8:half * 8 + 8, 0:ncols], in_=Wf[:, :, 0:ncols]),
                     reads=[fkey], writes=[wkeys[half]])
        for ti in tiles:
            ai = self.acnt % len(self.acc)
            self.acnt += 1
            ps = self.acc[ai]
            akey = (self.tag, 'acc', ai)
            for k in range(KC):
                if mode == 'TM':
                    lhsT = actT[:, k, ti * 128:(ti + 1) * 128]
                    rhs = Wb[:, k, 0:ncols]
                    out = ps[:, 0:ncols]
                    rkeys = [wkeys[k // 8]] + (keyfn('TM', ti) if keyfn else [(act_key, ti)])
                else:
                    mc, tg = ti // 2, ti % 2
                    lhsT = Wb[:, k, mc * 128:(mc + 1) * 128]
                    rhs = actT[:, k, tg * 512:(tg + 1) * 512]
                    out = ps[:, :]
                    rkeys = [wkeys[k // 8]] + (keyfn('FM', ti) if keyfn else [(act_key, tg * 4 + q) for q in range(4)])
                P.op('pe', lambda e, out=out, lhsT=lhsT, rhs=rhs, k=k: e.matmul(out, lhsT=lhsT, rhs=rhs, start=(k == 0),
                                                                                stop=(k == KC - 1)),
                     reads=rkeys, writes=[akey], nosync_same=True)
            epi(ti, ps, akey)


def rotary(P, eng, xin, C, S, out, tmp, keys_in, key_out, key_tmp):
    x1, x2 = xin[:, :, 0, :], xin[:, :, 1, :]
    t1, t2 = tmp[:, :, 0, :], tmp[:, :, 1, :]
    P.op(eng, lambda e: e.tensor_tensor(out=t1, in0=x1, in1=C, op=ALU.mult), reads=keys_in, writes=[key_tmp])
    P.op(eng, lambda e: e.tensor_tensor(out=t2, in0=x2, in1=S, op=ALU.mult), reads=keys_in + [key_tmp], writes=[key_tmp])
    P.op(eng, lambda e: e.tensor_tensor(out=out[:, :, 0, :], in0=t1, in1=t2, op=ALU.subtract), reads=[key_tmp], writes=[key_out])
    P.op(eng, lambda e: e.tensor_tensor(out=t1, in0=x2, in1=C, op=ALU.mult), reads=keys_in + [key_out], writes=[key_tmp])
    P.op(eng, lambda e: e.tensor_tensor(out=t2, in0=x1, in1=S, op=ALU.mult), reads=keys_in + [key_tmp], writes=[key_tmp])
    P.op(eng, lambda e: e.tensor_tensor(out=out[:, :, 1, :], in0=t1, in1=t2, op=ALU.add), reads=[key_tmp, key_out], writes=[key_out])


def head_rstd(P, xin, nh, hd, sq, sm, keys_in, tag):
    P.op('act', lambda e: e.activation(out=sq[:, 0:nh * hd], in_=xin, func=AF.Square), reads=keys_in, writes=[tag + 'sq'])
    P.op('dve', lambda e: e.tensor_reduce(out=sm[:, 0, 0:nh], in_=sq[:, 0:nh * hd].rearrange("p (h d) -> p h d", d=hd),
                                          axis=AX.X, op=ALU.add), reads=[tag + 'sq'], writes=[tag + 'sm'])
    P.op('dve', lambda e: e.tensor_scalar(out=sm[:, 1, 0:nh], in0=sm[:, 0, 0:nh], scalar1=1.0 / hd, scalar2=EPS,
                                          op0=ALU.mult, op1=ALU.add), reads=[tag + 'sm'], writes=[tag + 'sm'])
    P.op('act', lambda e: e.activation(out=sm[:, 2, 0:nh], in_=sm[:, 1, 0:nh], func=AF.Sqrt), reads=[tag + 'sm'], writes=[tag + 'sm'])
    P.op('dve', lambda e: e.reciprocal(out=sm[:, 3, 0:nh], in_=sm[:, 2, 0:nh]), reads=[tag + 'sm'], writes=[tag + 'sm'])


def phase_A(P, nc, d):
    projK = d['projK']
    with contextlib.ExitStack() as st:
        ident = sbuf(nc, st, "identA", [128, 128], BF16)
        P.dma('pool', ident[:], d['ident'], writes=['ident'])
        hT = sbuf(nc, st, "hT", [128, KC, T], BF16)
        with contextlib.ExitStack() as st2:
            ln_phase(P, nc, st2, d['x'], d['ln1g'], hT, ident, "lnA")
            P.barrier()
            if d.get('_stop') == 1:
                P.dma('sp', d['projK'].rearrange("t (k c) -> t k c", c=288)[0:128, :, 0:256].bitcast(BF16) if False else d['KTO'].rearrange("(k p) t -> p k t", p=128)[:, 0:8, :], hT[:, 0:8, :], reads=[('hT', t) for t in range(NT)])
                P.barrier()
                return
        with contextlib.ExitStack() as st2:
            G = Gemm(P, nc, st2, 6, "gA")
            stg = [sbuf(nc, st2, "stgA%d" % i, [128, 512], F32) for i in range(4)]
            cnt = [0]
            blocks = [(O_AK, 512, 0), (O_AV, 512, 512), (O_IK, 80, 1024), (O_RK, 512, 1536), (O_RK + 512, 512, 2048),
                      (O_RV, 512, 2560), (O_RV + 512, 512, 3072), (O_RV + 1024, 512, 3584), (O_RV + 1536, 512, 4096)]
            for (c0, ncols, dst) in blocks:
                def epi(ti, ps, akey, ncols=ncols, dst=dst):
                    s = cnt[0] % 4
                    cnt[0] += 1
                    eng = 'act' if s % 2 == 0 else 'dve'
                    if eng == 'act':
                        P.op('act', lambda e: e.copy(out=stg[s][:, 0:ncols], in_=ps[:, 0:ncols]), reads=[akey], writes=[('stgA', s)])
                    else:
                        P.op('dve', lambda e: e.tensor_copy(out=stg[s][:, 0:ncols], in_=ps[:, 0:ncols]), reads=[akey], writes=[('stgA', s)])
                    P.dma('pool', projK[ti * 128:(ti + 1) * 128, dst:dst + ncols], stg[s][:, 0:ncols], reads=[('stgA', s)],
                          writes=[('projK', ti)])
                G.block(d['w_in'][:, c0:c0 + ncols], ncols, 'TM', hT, 'hT', epi)
                if d.get('_stop') == 2 and d.get('_nblk', 99) <= blocks.index((c0, ncols, dst)) + 1:
                    break
            P.barrier()
    if d.get('_stop') == 2:
        return
    with contextlib.ExitStack() as st:
        ident = sbuf(nc, st, "identA2", [128, 128], BF16)
        P.dma('pool', ident[:], d['ident'], writes=['ident'])
        kng = sbuf(nc, st, "kng", [128, 1], F32)
        P.dma('sp', kng[:], d['kng'], writes=['kng'])
        KTs = sbuf(nc, st, "KTs", [128, 4, T], BF16)
        Vs = sbuf(nc, st, "Vs", [128, NT, 512], BF16)
        IKs = sbuf(nc, st, "IKs", [128, T], BF16)
        KTOs = sbuf(nc, st, "KTOs", [128, 8, T], BF16)
        akv = [sbuf(nc, st, "akv%d" % i, [128, 1024], F32) for i in range(2)]
        ikt = [sbuf(nc, st, "ikt%d" % i, [128, 64], F32) for i in range(2)]
        rkt = [sbuf(nc, st, "rkt%d" % i, [128, 8, 2, 64], F32) for i in range(2)]
        rvt = [sbuf(nc, st, "rvt%d" % i, [128, 2048], F32) for i in range(2)]
        ckt = [sbuf(nc, st, "ckt%d" % i, [128, 8, 64], F32) for i in range(2)]
        skt = [sbuf(nc, st, "skt%d" % i, [128, 8, 64], F32) for i in range(2)]
        sq = sbuf(nc, st, "sqA", [128, 512], F32)
        sm = sbuf(nc, st, "smA", [128, 4, 16], F32)
        akn = sbuf(nc, st, "akn", [128, 4, 128], BF16)
        ikd = sbuf(nc, st, "ikd", [128, 128], BF16)
        rtmp = sbuf(nc, st, "rtmp", [128, 8, 2, 64], F32)
        kb = sbuf(nc, st, "kb", [128, 8, 2, 64], BF16)
        rvb = [sbuf(nc, st, "rvb%d" % i, [128, 2048], BF16) for i in range(2)]
        us = [sbuf(nc, st, "us%d" % i, [128, 2048], F32) for i in range(2)]
        pTk = psum(nc, st, "pTk", [128, 8, 128], BF16)
        pTi = psum(nc, st, "pTi", [128, 8, 128], BF16)
        pTr = psum(nc, st, "pTr", [128, 8, 128], BF16)
        pU = [psum(nc, st, "pU%d" % i, [128, 512], F32) for i in range(4)]
        for t in range(NT):
            b = t % 2
            r0, r1 = t * 128, (t + 1) * 128
            P.dma('sp', akv[b][:], projK[r0:r1, 0:1024], writes=[('akv', b)])
            P.dma('sp', ikt[b][:], projK[r0:r1, 1024:1088], writes=[('ikt', b)])
            P.dma('sp', rkt[b][:].rearrange("p a b c -> p (a b c)"), projK[r0:r1, 1536:2560], writes=[('rkt', b)])
            P.dma('sp', rvt[b][:], projK[r0:r1, 2560:4608], writes=[('rvt', b)])
            P.dma('sp', ckt[b][:].rearrange("p a c -> p (a c)"), d['cosk'][r0:r1, :], writes=[('ckt', b)])
            P.dma('sp', skt[b][:].rearrange("p a c -> p (a c)"), d['sink'][r0:r1, :], writes=[('skt', b)])
            lvl = d.get('_stop', 99)
            head_rstd(P, akv[b][:, 0:512], 4, 128, sq, sm, [('akv', b)], 'A')
            P.op('dve', lambda e, b=b: e.tensor_tensor(out=akn[:], in0=akv[b][:, 0:512].rearrange("p (h d) -> p h d", d=128),
                                                       in1=sm[:, 3, 0:4].unsqueeze(2).to_broadcast([128, 4, 128]), op=ALU.mult),
                 reads=[('akv', b), 'Asm'], writes=['akn'])
            for h in range(4):
                P.op('pe', lambda e, h=h: e.transpose(out=pTk[:, h, :], in_=akn[:, h, :], identity=ident[:]),
                     reads=['akn', 'ident'], writes=['pTk'], nosync_same=True)
            P.op('act', lambda e, t=t: e.activation(out=KTs[:, :, t * 128:(t + 1) * 128], in_=pTk[:, 0:4, :], func=AF.Copy,
                                                    scale=kng[:, 0:1]), reads=['pTk', 'kng'], writes=[('KTs', t)])
            if lvl < 4:
                continue
            P.op('act', lambda e, b=b, t=t: e.copy(out=Vs[:, t, :], in_=akv[b][:, 512:1024]), reads=[('akv', b)], writes=[('Vs', t)])
            P.op('dve', lambda e, b=b: e.tensor_copy(out=ikd[:, 0:64], in_=ikt[b][:]), reads=[('ikt', b)], writes=['ikd'])
            P.op('dve', lambda e, b=b: e.tensor_copy(out=ikd[:, 64:128], in_=ikt[b][:]), reads=[('ikt', b), 'ikd'], writes=['ikd'])
            P.op('pe', lambda e: e.transpose(out=pTi[:, 0, :], in_=ikd[:], identity=ident[:]), reads=['ikd', 'ident'], writes=['pTi'],
                 nosync_same=True)
            P.op('act', lambda e, t=t: e.copy(out=IKs[:, t * 128:(t + 1) * 128], in_=pTi[:, 0, :]), reads=['pTi'], writes=[('IKs', t)])
            if lvl < 5:
                continue
            rotary(P, 'dve', rkt[b], ckt[b][:], skt[b][:], kb, rtmp, [('rkt', b), ('ckt', b), ('skt', b)], 'kb', 'rtmp')
            for h in range(8):
                P.op('pe', lambda e, h=h: e.transpose(out=pTr[:, h, :], in_=kb[:, h, :, :].rearrange("p a c -> p (a c)"),
                                                      identity=ident[:]), reads=['kb', 'ident'], writes=['pTr'], nosync_same=True)
            P.op('act', lambda e, t=t: e.copy(out=KTOs[:, :, t * 128:(t + 1) * 128], in_=pTr[:]), reads=['pTr'], writes=[('KTOs', t)])
            if lvl < 6:
                continue
            P.op('act', lambda e, b=b: e.copy(out=rvb[b][:], in_=rvt[b][:]), reads=[('rvt', b)], writes=[('rvb', b)])
            P.dma('sp', d['RVO'][r0:r1, :], rvb[b][:], reads=[('rvb', b)])
            if lvl < 7:
                continue
            for h in range(8):
                pu = pU[h // 2]
                P.op('pe', lambda e, h=h, pu=pu, b=b: e.matmul(pu[:, (h % 2) * 256:(h % 2) * 256 + 256],
                                                               lhsT=kb[:, h, :, :].rearrange("p a c -> p (a c)"),
                                                               rhs=rvb[b][:, h * 256:(h + 1) * 256], start=True, stop=True),
                     reads=['kb', ('rvb', b)], writes=[('pU', h // 2)], nosync_same=True)
            if lvl < 8:
                continue
            for q in range(4):
                eng = 'act' if q % 2 == 0 else 'dve'
                if eng == 'act':
                    P.op('act', lambda e, q=q, b=b: e.copy(out=us[b][:, q * 512:(q + 1) * 512], in_=pU[q][:]), reads=[('pU', q)],
                         writes=[('us', b)])
                else:
                    P.op('dve', lambda e, q=q, b=b: e.tensor_copy(out=us[b][:, q * 512:(q + 1) * 512], in_=pU[q][:]), reads=[('pU', q)],
                         writes=[('us', b)])
            P.dma('sp', d['UPD'][r0:r1, :], us[b][:], reads=[('us', b)])
        P.dma('sp', d['KT'].rearrange("(h p) t -> p h t", p=128), KTs[:], reads=[('KTs', t) for t in range(NT)])
        P.dma('sp', d['V'].rearrange("(t p) c -> p t c", p=128), Vs[:], reads=[('Vs', t) for t in range(NT)])
        P.dma('sp', d['IKT'], IKs[:], reads=[('IKs', t) for t in range(NT)])
        P.dma('sp', d['KTO'].rearrange("(h p) t -> p h t", p=128), KTOs[:], reads=[('KTOs', t) for t in range(NT)])
        P.barrier()


def phase_B(P, nc, d):
    projQ, G = d['projQ'], d['G']
    SCALE = 128.0 ** -0.5
    with contextlib.ExitStack() as st:
        ident12 = sbuf(nc, st, "identB", [128, 128], BF16)
        P.dma('pool', ident12[:], d['ident'], writes=['ident'])
        hT = sbuf(nc, st, "hTB", [128, KC, T], BF16)
        with contextlib.ExitStack() as st2:
            ln_phase(P, nc, st2, d['x'], d['ln1g'], hT, ident12, "lnB")
            P.barrier()
        with contextlib.ExitStack() as st2:
            Gm = Gemm(P, nc, st2, 6, "gB")
            stg = [sbuf(nc, st2, "stgB%d" % i, [128, 512], F32) for i in range(4)]
            cnt = [0]
            tm_blocks = [(O_AQ + 512 * i, 512, 512 * i) for i in range(4)] + [(O_IQ, 512, 2048), (O_IQ + 512, 512, 2560),
                                                                               (O_IK, 80, 3072), (O_RQ, 512, 3584), (O_RQ + 512, 512, 4096)]
            for (c0, ncols, dst) in tm_blocks:
                def epi(ti, ps, akey, ncols=ncols, dst=dst):
                    s = cnt[0] % 4
                    cnt[0] += 1
                    if s % 2 == 0:
                        P.op('act', lambda e: e.copy(out=stg[s][:, 0:ncols], in_=ps[:, 0:ncols]), reads=[akey], writes=[('stgB', s)])
                    else:
                        P.op('dve', lambda e: e.tensor_copy(out=stg[s][:, 0:ncols], in_=ps[:, 0:ncols]), reads=[akey], writes=[('stgB', s)])
                    P.dma('pool', projQ[ti * 128:(ti + 1) * 128, dst:dst + ncols], stg[s][:, 0:ncols], reads=[('stgB', s)],
                          writes=[('projQ', ti)])
                Gm.block(d['w_in'][:, c0:c0 + ncols], ncols, 'TM', hT, 'hT', epi)
            for gi, c0 in enumerate((O_RG, O_GA, O_GB)):
                for nb in range(4):
                    def epi(ti, ps, akey, gi=gi, nb=nb):
                        s = cnt[0] % 4
                        cnt[0] += 1
                        mc, tg = ti // 2, ti % 2
                        P.op('act', lambda e: e.activation(out=stg[s][:], in_=ps[:], func=AF.Sigmoid), reads=[akey], writes=[('stgB', s)])
                        if gi == 0:
                            P.op('dve', lambda e: e.tensor_tensor(out=stg[s][:], in0=stg[s][:], in1=ps[:], op=ALU.mult),
                                 reads=[akey, ('stgB', s)], writes=[('stgB', s)])
                        f0 = nb * 512 + mc * 128
                        P.dma('pool', G[gi, f0:f0 + 128, tg * 512:(tg + 1) * 512], stg[s][:], reads=[('stgB', s)], writes=[('G', gi)])
                    Gm.block(d['w_in'][:, c0 + nb * 512:c0 + (nb + 1) * 512], 512, 'FM', hT, 'hT', epi)
            P.barrier()
    with contextlib.ExitStack() as st:
        sb = lambda name, shape, dt: sbuf(nc, st, name, shape, dt)
        ident5 = sb("identB5", [128, 128], BF16)
        P.dma('pool', ident5[:], d['ident'], writes=['ident'])
        tri = sb("tri", [128, 128], BF16)
        P.dma('pool', tri[:], d['tri'], writes=['tri'])
        ones = sb("ones", [128, 128], BF16)
        P.op('pool', lambda e: e.memset(ones[:], 1.0), writes=['ones'])
        idxm = sb("idxm", [128, 256], F32)
        P.dma('sp', idxm[:], d['idxmask'], writes=['idxm'])
        cfl = sb("cfl", [128, 2], F32)
        P.dma('sp', cfl[:], d['cflag'], writes=['cfl'])
        qng = sb("qng", [128, 2], F32)
        P.dma('sp', qng[:, 0:1], d['qng'], writes=['qng'])
        P.op('pool', lambda e: e.tensor_scalar(out=qng[:, 1:2], in0=qng[:, 0:1], scalar1=SCALE, scalar2=None, op0=ALU.mult),
             reads=['qng'], writes=['qng'])
        gng = sb("gng", [128, KC], F32)
        gnb = sb("gnb", [128, KC], F32)
        P.dma('sp', gng[:], d['gng'], writes=['gng'])
        P.dma('sp', gnb[:], d['gnb'], writes=['gnb'])
        KTsb = sb("KTsb", [128, 4, 8, 2, 128], BF16)
        Vsb = sb("Vsb", [128, 8, 2, 512], BF16)
        IKsb = sb("IKsb", [128, 8, 2, 128], BF16)
        for c2 in range(2):
            for h in range(4):
                P.dma('sp', KTsb[:, h, :, c2, :], d['KTf'][c2, h * 128:(h + 1) * 128, :].rearrange("p (i r) -> p i r", r=128),
                      writes=[('KTsb', c2, h)])
            P.dma('sp', Vsb[:, :, c2, :], d['Vf'][c2].rearrange("(i p) c -> p i c", p=128), writes=[('Vsb', c2)])
            P.dma('sp', IKsb[:, :, c2, :], d['IKTf'][c2].rearrange("p (i r) -> p i r", r=128), writes=[('IKsb', c2)])
        kside = [('KTsb', c2, h) for c2 in range(2) for h in range(4)] + [('Vsb', 0), ('Vsb', 1), ('IKsb', 0), ('IKsb', 1)]
        KTv = KTsb[:].rearrange("p h i c r -> p h (i c r)")
        Vv = Vsb[:].rearrange("p i c f -> p (i c) f")
        IKv = IKsb[:].rearrange("p i c r -> p (i c r)")
        S = sb("Sst", [128, 8, 256], F32)
        P.op('pool', lambda e: e.memset(S[:], 0.0), writes=['S'])
        gct = sb("gct", [128, 8, 256], F32)
        for h in range(8):
            P.op('pool', lambda e, h=h: e.memset(gct[:, h, :], GC[h]), reads=['gct'], writes=['gct'])
        aqt = sb("aqt", [128, 2048], F32)
        iqt = sb("iqt", [128, 1024], F32)
        iwt = sb("iwt", [128, 16], F32)
        iws = sb("iws", [128, 16], F32)
        rqt = sb("rqt", [128, 8, 2, 64], F32)
        cqt = sb("cqt", [128, 8, 64], F32)
        sqt = sb("sqt", [128, 8, 64], F32)
        ktot = sb("ktot", [128, 8, 128], BF16)
        rvot = sb("rvot", [128, 2048], BF16)
        U0 = sb("U0", [128, 8, 256], F32)
        U1 = sb("U1", [128, 8, 256], F32)
        srg = sb("srg", [128, KC, 128], F32)
        sq = sb("sqB", [128, 2048], F32)
        sm = sb("smB", [128, 4, 16], F32)
        aqn = sb("aqn", [128, 16, 128], BF16)
        aqT2 = [sb("aqT%d" % q, [128, 16, 128], BF16) for q in range(2)]
        qT2 = [sb("qT%d" % q, [128, 8, 128], BF16) for q in range(2)]
        nselT2 = [sb("nselT%d" % q, [128, 16, 128], BF16) for q in range(2)]
        negI = sb("negI", [128, 128], BF16)
        P.op('dve', lambda e: e.tensor_scalar(out=negI[:], in0=ident5[:], scalar1=-30000.0, scalar2=None, op0=ALU.mult),
             reads=['ident'], writes=['negI'])
        iqb = sb("iqb", [128, 1024], BF16)
        iqT = sb("iqT", [128, 8, 128], BF16)
        rtmp = sb("rtmpB", [128, 8, 2, 64], F32)
        qb = sb("qb", [128, 8, 2, 64], BF16)
        score = sb("score", [128, 2048], F32)
        work = sb("work", [128, 2048], F32)
        m8 = sb("m8", [128, 8], F32)
        thr = sb("thr", [128, 1], F32)
        sel = sb("sel", [128, 2048], BF16)
        rlu = [sb("rlu%d" % i, [128, 512], F32) for i in range(2)]
        ex = [sb("ex%d" % i, [128, 512], BF16) for i in range(2)]
        rden = sb("rden", [128, 512], F32)
        OTt = sb("OTt", [128, 16, 128], BF16)
        Rb = sb("Rb", [128, 8, 256], BF16)
        Pm = sb("Pm", [128, 8, 128], BF16)
        st6 = sb("st6", [128, 8, 6], F32)
        mv = sb("mv", [128, 8, 2], F32)
        gs = sb("gs", [128, 4, 8], F32)
        yn = sb("yn", [128, 2048], BF16)
        otmp = sb("otmp", [128, 8, 128], F32)
        ORt = sb("ORt", [128, 16, 128], BF16)
        pT = [psum(nc, st, "pTB%d" % i, [128, 8, 128], BF16) for i in range(2)]
        pg = [psum(nc, st, "pg%d" % i, [128, 512], F32) for i in range(2)]
        poT = psum(nc, st, "poT", [128, 512], F32)
        pden = psum(nc, st, "pden", [128, 512], F32)
        py = [psum(nc, st, "py%d" % i, [128, 512], F32) for i in range(2)]
        ptc = [0]
        pgc = [0]
        exc = [0]

        def transposes(src_fn, n, evac_fn, rkeys):
            for g0 in range(0, n, 8):
                c = min(8, n - g0)
                pi = ptc[0] % 2
                ptc[0] += 1
                for kk in range(c):
                    P.op('pe', lambda e, pi=pi, kk=kk, g0=g0: e.transpose(out=pT[pi][:, kk, :], in_=src_fn(g0 + kk), identity=ident5[:]),
                         reads=rkeys + ['ident'], writes=[('pT', pi)], nosync_same=True)
                evac_fn(pT[pi], g0, c, ('pT', pi))

        def stage1(i):
            r0, r1 = i * 128, (i + 1) * 128
            nk = 2 * i + 2
            n = nk * 128
            bq = i % 2
            aqT, qT, nselT = aqT2[bq], qT2[bq], nselT2[bq]
            kq = ('q', bq)
            P.dma('sp', aqt[:], projQ[r0:r1, 0:2048], writes=['aqt'])
            P.dma('sp', iqt[:], projQ[r0:r1, 2048:3072], writes=['iqt'])
            P.dma('sp', iwt[:], projQ[r0:r1, 3072 + 64:3072 + 80], writes=['iwt'])
            P.dma('sp', rqt[:].rearrange("p a b c -> p (a b c)"), projQ[r0:r1, 3584:4608], writes=['rqt'])
            P.dma('sp', cqt[:].rearrange("p a c -> p (a c)"), d['cosq'][r0:r1, :], writes=['cqt'])
            P.dma('sp', sqt[:].rearrange("p a c -> p (a c)"), d['sinq'][r0:r1, :], writes=['sqt'])
            head_rstd(P, aqt[:], 16, 128, sq, sm, ['aqt'], 'B')
            P.op('dve', lambda e: e.tensor_tensor(out=aqn[:], in0=aqt[:].rearrange("p (h d) -> p h d", d=128),
                                                   in1=sm[:, 3, 0:16].unsqueeze(2).to_broadcast([128, 16, 128]), op=ALU.mult),
                 reads=['aqt', 'Bsm'], writes=['aqn'])
            transposes(lambda k: aqn[:, k, :], 16,
                       lambda pt, g0, c, key: P.op('act', lambda e: e.activation(out=aqT[:, g0:g0 + c, :], in_=pt[:, 0:c, :], func=AF.Copy,
                                                                                 scale=qng[:, 1:2]),
                                                   reads=[key, 'qng'], writes=[('aqT', bq)]), ['aqn'])
            P.op('act', lambda e: e.copy(out=iqb[:], in_=iqt[:]), reads=['iqt'], writes=['iqb'])
            transposes(lambda k: iqb[:, k * 128:(k + 1) * 128], 8,
                       lambda pt, g0, c, key: P.op('act', lambda e: e.copy(out=iqT[:, g0:g0 + c, :], in_=pt[:, 0:c, :]),
                                                   reads=[key], writes=['iqT']), ['iqb'])
            P.op('dve', lambda e: e.tensor_scalar(out=iws[:], in0=iwt[:], scalar1=0.25, scalar2=None, op0=ALU.mult),
                 reads=['iwt'], writes=['iws'])
            rotary(P, 'dve', rqt, cqt[:], sqt[:], qb, rtmp, ['rqt', 'cqt', 'sqt'], 'qb', 'rtmpB')
            transposes(lambda k: qb[:, k, :, :].rearrange("p a c -> p (a c)"), 8,
                       lambda pt, g0, c, key: P.op('act', lambda e: e.copy(out=qT[:, g0:g0 + c, :], in_=pt[:, 0:c, :]),
                                                   reads=[key], writes=[('qT', bq)]), ['qb'])
            for kc0 in range(0, n, 512):
                w = min(512, n - kc0)
                for h in range(16):
                    hp = h % 2
                    gi = pgc[0] % 2
                    pgc[0] += 1
                    P.op('pe', lambda e, gi=gi, h=h, hp=hp, kc0=kc0, w=w: e.matmul(
                        pg[gi][:, 0:w], lhsT=iqT[hp * 64:(hp + 1) * 64, h // 2, :], rhs=IKv[hp * 64:(hp + 1) * 64, kc0:kc0 + w],
                        start=True, stop=True), reads=['iqT'] + kside, writes=[('pg', gi)], nosync_same=True)
                    P.op('act', lambda e, gi=gi, w=w: e.activation(out=rlu[gi][:, 0:w], in_=pg[gi][:, 0:w], func=AF.Relu),
                         reads=[('pg', gi)], writes=[('rlu', gi)])
                    if h == 0:
                        P.op('dve', lambda e, gi=gi, w=w, kc0=kc0: e.tensor_scalar(out=score[:, kc0:kc0 + w], in0=rlu[gi][:, 0:w],
                                                                                   scalar1=iws[:, 0:1], scalar2=None, op0=ALU.mult),
                             reads=[('rlu', gi), 'iws'], writes=['score'])
                    else:
                        P.op('dve', lambda e, gi=gi, w=w, kc0=kc0, h=h: e.scalar_tensor_tensor(
                            out=score[:, kc0:kc0 + w], in0=rlu[gi][:, 0:w], scalar=iws[:, h:h + 1], in1=score[:, kc0:kc0 + w],
                            op0=ALU.mult, op1=ALU.add), reads=[('rlu', gi), 'iws', 'score'], writes=['score'])
            P.op('dve', lambda e, n=n: e.tensor_tensor(out=score[:, n - 256:n], in0=score[:, n - 256:n], in1=idxm[:], op=ALU.add),
                 reads=['score', 'idxm'], writes=['score'])
            for r in range(32 if n > 256 else 0):
                src = score if r == 0 else work
                P.op('dve', lambda e, src=src, n=n: e.max(out=m8[:], in_=src[:, 0:n]), reads=['score', 'work'], writes=['m8'])
                if r < 31:
                    P.op('dve', lambda e, src=src, n=n: e.match_replace(out=work[:, 0:n], in_to_replace=m8[:], in_values=src[:, 0:n],
                                                                        imm_value=NEG), reads=['score', 'work', 'm8'], writes=['work'])
            if n > 256:
                P.op('dve', lambda e: e.tensor_scalar(out=thr[:], in0=m8[:, 7:8], scalar1=-1.0e29, scalar2=None, op0=ALU.max),
                     reads=['m8'], writes=['thr'])
            else:
                P.op('dve', lambda e: e.memset(thr[:], -1.0e29), reads=['thr'], writes=['thr'])
            P.op('dve', lambda e, n=n: e.tensor_scalar(out=sel[:, 0:n], in0=score[:, 0:n], scalar1=thr[:, 0:1], scalar2=None, op0=ALU.is_ge),
                 reads=['score', 'thr'], writes=['sel'])
            transposes(lambda k: sel[:, k * 128:(k + 1) * 128], nk,
                       lambda pt, g0, c, key: P.op('dve', lambda e: e.tensor_scalar(out=nselT[:, g0:g0 + c, :], in0=pt[:, 0:c, :], scalar1=-1.0,
                                                                                    scalar2=1.0, op0=ALU.mult, op1=ALU.add),
                                                   reads=[key], writes=[('nselT', bq)]), ['sel'])
        def stage2(i):
            r0, r1 = i * 128, (i + 1) * 128
            nk = 2 * i + 2
            n = nk * 128
            bq = i % 2
            aqT, qT, nselT = aqT2[bq], qT2[bq], nselT2[bq]
            kq = ('q', bq)
            P.dma('sp', ktot[:], d['KTO'].rearrange("(h p) t -> p h t", p=128)[:, :, r0:r1], writes=['ktot'])
            P.dma('sp', rvot[:], d['RVO'][r0:r1, :], writes=['rvot'])
            P.dma('sp', U0[:].rearrange("p h v -> p (h v)"), d['UPDf_fn'](0, i), writes=['U0'])
            P.dma('sp', U1[:].rearrange("p h v -> p (h v)"), d['UPDf_fn'](1, i), writes=['U1'])
            P.dma('sp', srg[:], G[0].rearrange("(k p) t -> p k t", p=128)[:, :, r0:r1], writes=['srg'])
            for g in range(4):
                for kt in range(nk):
                    gi = pgc[0] % 2
                    pgc[0] += 1
                    xi = exc[0] % 2
                    exc[0] += 1
                    P.op('pe', lambda e, gi=gi, g=g, kt=kt: e.matmul(pg[gi][:].rearrange("p (h q) -> p h q", q=128),
                                                                     lhsT=KTv[:, g, kt * 128:(kt + 1) * 128], rhs=aqT[:, 4 * g:4 * g + 4, :],
                                                                     start=True, stop=False),
                         reads=[('aqT', bq)] + kside, writes=[('pg', gi)], nosync_same=True)
                    P.op('pe', lambda e, gi=gi, kt=kt: e.matmul(pg[gi][:].rearrange("p (h q) -> p h q", q=128), lhsT=negI[:],
                                                                rhs=nselT[:, kt:kt + 1, :].to_broadcast([128, 4, 128]),
                                                                start=False, stop=True),
                         reads=[('nselT', bq), 'negI'], writes=[('pg', gi)], nosync_same=True)
                    P.op('act', lambda e, gi=gi, xi=xi: e.activation(out=ex[xi][:], in_=pg[gi][:], func=AF.Exp),
                         reads=[('pg', gi)], writes=[('ex', xi)])
                    P.op('pe', lambda e, xi=xi, g=g, kt=kt, nk=nk: e.matmul(poT[:], lhsT=Vv[:, kt, g * 128:(g + 1) * 128], rhs=ex[xi][:],
                                                                            start=(kt == 0), stop=(kt == nk - 1)),
                         reads=[('ex', xi)] + kside, writes=['poT'], nosync_same=True)
                    P.op('pe', lambda e, xi=xi, kt=kt, nk=nk: e.matmul(pden[:], lhsT=ones[:], rhs=ex[xi][:],
                                                                       start=(kt == 0), stop=(kt == nk - 1)),
                         reads=[('ex', xi), 'ones'], writes=['pden'], nosync_same=True)
                P.op('dve', lambda e: e.reciprocal(out=rden[:], in_=pden[:]), reads=['pden'], writes=['rden'])
                P.op('dve', lambda e, g=g: e.tensor_tensor(out=OTt[:, 4 * g:4 * g + 4, :], in0=poT[:].rearrange("p (h q) -> p h q", q=128),
                                                           in1=rden[:].rearrange("p (h q) -> p h q", q=128), op=ALU.mult),
                     reads=['poT', 'rden'], writes=['OTt'])
            P.dma('sp', d['OT'].rearrange("(h p) t -> p h t", p=128)[:, :, r0:r1], OTt[:], reads=['OTt'])
            P.op('dve', lambda e: e.tensor_tensor(out=U0[:], in0=U0[:], in1=S[:], op=ALU.add), reads=['U0', 'S'], writes=['U0'])
            P.op('dve', lambda e: e.tensor_tensor(out=U0[:], in0=U0[:], in1=gct[:], op=ALU.mult), reads=['U0', 'gct'], writes=['U0'])
            P.op('dve', lambda e: e.tensor_tensor(out=U1[:], in0=U1[:], in1=U0[:], op=ALU.add), reads=['U0', 'U1'], writes=['U1'])
            P.op('dve', lambda e: e.tensor_scalar(out=U0[:], in0=U0[:], scalar1=cfl[:, 0:1], scalar2=None, op0=ALU.mult),
                 reads=['U0', 'cfl'], writes=['U0'])
            P.op('dve', lambda e: e.scalar_tensor_tensor(out=Rb[:], in0=S[:], scalar=cfl[:, 1:2], in1=U0[:], op0=ALU.mult, op1=ALU.add),
                 reads=['S', 'U0', 'cfl'], writes=['Rb'])
            P.op('dve', lambda e: e.tensor_tensor(out=S[:], in0=U1[:], in1=gct[:], op=ALU.mult), reads=['U1', 'gct', 'Rb'], writes=['S'])
            for hh in range(2):
                gi = pgc[0] % 2
                pgc[0] += 1
                for h4 in range(4):
                    h = hh * 4 + h4
                    P.op('pe', lambda e, gi=gi, h=h, h4=h4: e.matmul(pg[gi][:, h4 * 128:(h4 + 1) * 128], lhsT=ktot[:, h, :], rhs=qT[:, h, :],
                                                                     start=True, stop=True),
                         reads=['ktot', ('qT', bq)], writes=[('pg', gi)], nosync_same=True)
                P.op('dve', lambda e, hh=hh, gi=gi: e.tensor_tensor(out=Pm[:, hh * 4:hh * 4 + 4, :],
                                                                    in0=pg[gi][:].rearrange("p (h q) -> p h q", q=128),
                                                                    in1=tri[:].unsqueeze(1).to_broadcast([128, 4, 128]), op=ALU.mult),
                     reads=[('pg', gi), 'tri'], writes=[('Pm', hh)])
            for hh in range(2):
                for h4 in range(4):
                    h = hh * 4 + h4
                    yo = py[h4 // 2][:, (h4 % 2) * 256:(h4 % 2) * 256 + 256]
                    P.op('pe', lambda e, yo=yo, h=h: e.matmul(yo, lhsT=Pm[:, h, :], rhs=rvot[:, h * 256:(h + 1) * 256], start=True, stop=False),
                         reads=[('Pm', hh), 'rvot'], writes=[('py', h4 // 2)], nosync_same=True)
                    P.op('pe', lambda e, yo=yo, h=h: e.matmul(yo, lhsT=qT[:, h, :], rhs=Rb[:, h, :], start=False, stop=True),
                         reads=[('qT', bq), 'Rb'], writes=[('py', h4 // 2)], nosync_same=True)
                for h4 in range(4):
                    h = hh * 4 + h4
                    yo = py[h4 // 2][:, (h4 % 2) * 256:(h4 % 2) * 256 + 256]
                    P.op('dve', lambda e, yo=yo, h=h: e.bn_stats(out=st6[:, h, :], in_=yo), reads=[('py', h4 // 2)], writes=['st6'])
                    P.op('dve', lambda e, h=h: e.bn_aggr(out=mv[:, h, :], in_=st6[:, h, :]), reads=['st6'], writes=['mv'])
                sl = slice(hh * 4, hh * 4 + 4)
                P.op('dve', lambda e, sl=sl: e.tensor_scalar(out=gs[:, 0, sl], in0=mv[:, sl, 1], scalar1=EPS, scalar2=None, op0=ALU.add),
                     reads=['mv'], writes=['gs'])
                P.op('act', lambda e, sl=sl: e.activation(out=gs[:, 1, sl], in_=gs[:, 0, sl], func=AF.Sqrt), reads=['gs'], writes=['gs'])
                P.op('dve', lambda e, sl=sl: e.reciprocal(out=gs[:, 2, sl], in_=gs[:, 1, sl]), reads=['gs'], writes=['gs'])
                for h4 in range(4):
                    h = hh * 4 + h4
                    yo = py[h4 // 2][:, (h4 % 2) * 256:(h4 % 2) * 256 + 256]
                    P.op('dve', lambda e, yo=yo, h=h: e.tensor_scalar(out=yn[:, h * 256:(h + 1) * 256], in0=yo, scalar1=mv[:, h, 0:1],
                                                                      scalar2=gs[:, 2, h:h + 1], op0=ALU.subtract, op1=ALU.mult),
                         reads=[('py', h4 // 2), 'mv', 'gs'], writes=['yn'])
            def evac_or(pt, g0, c, key):
                for kk in range(c):
                    P.op('act', lambda e, kk=kk: e.activation(out=otmp[:, kk, :], in_=pt[:, kk, :], func=AF.Identity,
                                                              scale=gng[:, g0 + kk:g0 + kk + 1], bias=gnb[:, g0 + kk:g0 + kk + 1]),
                         reads=[key, 'gng', 'gnb'], writes=['otmp'], nosync_same=True)
                P.op('dve', lambda e: e.tensor_tensor(out=ORt[:, g0:g0 + c, :], in0=otmp[:, 0:c, :], in1=srg[:, g0:g0 + c, :], op=ALU.mult),
                     reads=['otmp', 'srg'], writes=['ORt'])
            transposes(lambda k: yn[:, k * 128:(k + 1) * 128], 16, evac_or, ['yn'])
            P.dma('sp', d['ORT'].rearrange("(h p) t -> p h t", p=128)[:, :, r0:r1], ORt[:], reads=['ORt'])
        stage1(0)
        for i in range(NT):
            if i + 1 < NT:
                stage1(i + 1)
            stage2(i)
        P.barrier()
    with contextlib.ExitStack() as st:
        OTa = sbuf(nc, st, "OTa", [128, KC, T], BF16)
        ORa = sbuf(nc, st, "ORa", [128, KC, T], BF16)
        MTa = sbuf(nc, st, "MTa", [128, KC, T], BF16)
        P.dma('sp', OTa[:], d['OT'].rearrange("(k p) t -> p k t", p=128), writes=[('OTa', q) for q in range(8)])
        P.dma('sp', ORa[:], d['ORT'].rearrange("(k p) t -> p k t", p=128), writes=[('ORa', q) for q in range(8)])
        Gm = Gemm(P, nc, st, 6, "g6")
        tmpA = [sbuf(nc, st, "tmpA%d" % i, [128, 512], F32) for i in range(8)]
        gta = [sbuf(nc, st, "gta%d" % i, [128, 512], F32) for i in range(2)]
        mm = [sbuf(nc, st, "mm%d" % i, [128, 512], F32) for i in range(2)]
        cnt = [0]
        for nb in range(4):
            def epiA(ti, ps, akey, nb=nb):
                s = cnt[0] % 2
                cnt[0] += 1
                mc, tg = ti // 2, ti % 2
                f0 = nb * 512 + mc * 128
                P.dma('pool', gta[s][:], G[1, f0:f0 + 128, tg * 512:(tg + 1) * 512], writes=[('gta', s)])
                P.op('dve', lambda e: e.tensor_tensor(out=tmpA[ti][:], in0=ps[:], in1=gta[s][:], op=ALU.mult),
                     reads=[akey, ('gta', s)], writes=[('tmpA', ti)])
            Gm.block(d['w_ua'][:, nb * 512:(nb + 1) * 512], 512, 'FM', OTa, 'OTa', epiA)

            def epiR(ti, ps, akey, nb=nb):
                s = cnt[0] % 2
                cnt[0] += 1
                mc, tg = ti // 2, ti % 2
                f0 = nb * 512 + mc * 128
                P.dma('pool', gta[s][:], G[2, f0:f0 + 128, tg * 512:(tg + 1) * 512], writes=[('gta', s)])
                P.op('dve', lambda e: e.tensor_tensor(out=mm[s][:], in0=ps[:], in1=gta[s][:], op=ALU.mult),
                     reads=[akey, ('gta', s)], writes=[('mm', s)])
                P.op('dve', lambda e: e.tensor_tensor(out=MTa[:, nb * 4 + mc, tg * 512:(tg + 1) * 512], in0=mm[s][:], in1=tmpA[ti][:],
                                                      op=ALU.add), reads=[('mm', s), ('tmpA', ti)], writes=[('MTa', nb * 4 + mc, tg)])
            Gm.block(d['w_ur'][:, nb * 512:(nb + 1) * 512], 512, 'FM', ORa, 'ORa', epiR)
        P.dma('sp', d['MT'].rearrange("(k p) t -> p k t", p=128), MTa[:], reads=[('MTa', k, tg) for k in range(KC) for tg in range(2)])
        P.barrier()
    with contextlib.ExitStack() as st:
        xres = sbuf(nc, st, "xres", [128, NT, D], F32)
        for t in range(NT):
            P.dma('sp', xres[:, t, :], d['x'][t * 128:(t + 1) * 128, :], writes=[('xres', t)])
        with contextlib.ExitStack() as st2:
            MTb = sbuf(nc, st2, "MTb", [128, KC, T], BF16)
            P.dma('sp', MTb[:], d['MT'].rearrange("(k p) t -> p k t", p=128), writes=[('MTb', q) for q in range(8)])
            Gm = Gemm(P, nc, st2, 6, "g7")
            for nb in range(4):
                def epi(ti, ps, akey, nb=nb):
                    P.op('dve', lambda e: e.tensor_tensor(out=xres[:, ti, nb * 512:(nb + 1) * 512], in0=ps[:],
                                                          in1=xres[:, ti, nb * 512:(nb + 1) * 512], op=ALU.add),
                         reads=[akey, ('xres', ti)], writes=[('xres', ti)])
                Gm.block(d['w_out'][:, nb * 512:(nb + 1) * 512], 512, 'TM', MTb, 'MTb', epi)
            P.barrier()
        with contextlib.ExitStack() as st2:
            ident8 = sbuf(nc, st2, "identB8", [128, 128], BF16)
            P.dma('pool', ident8[:], d['ident'], writes=['ident'])
            h2T = sbuf(nc, st2, "h2T", [128, KC, T], BF16)
            with contextlib.ExitStack() as st3:
                ln_phase(P, nc, st3, None, d['ln2g'], h2T, ident8, "ln2", xres=xres)
                P.barrier()
            aT = sbuf(nc, st2, "aT", [128, KC, T], BF16)
            Gm = Gemm(P, nc, st2, 6, "g8")
            rl = [sbuf(nc, st2, "rl%d" % i, [128, 512], F32) for i in range(2)]
            cnt = [0]
            for kg in range(4):
                for nb in range(4):
                    def epi1(ti, ps, akey, nb=nb):
                        s = cnt[0] % 2
                        cnt[0] += 1
                        mc, tg = ti // 2, ti % 2
                        P.op('act', lambda e: e.activation(out=rl[s][:], in_=ps[:], func=AF.Relu), reads=[akey], writes=[('rl', s)])
                        P.op('act', lambda e: e.activation(out=aT[:, nb * 4 + mc, tg * 512:(tg + 1) * 512], in_=rl[s][:], func=AF.Square),
                             reads=[('rl', s)], writes=[('aT', nb * 4 + mc, tg)])
                    Gm.block(d['w_ff1'][:, kg * 2048 + nb * 512:kg * 2048 + (nb + 1) * 512], 512, 'FM', h2T, 'hT', epi1)
                for nb in range(4):
                    def epi2(ti, ps, akey, nb=nb):
                        P.op('dve', lambda e: e.tensor_tensor(out=xres[:, ti, nb * 512:(nb + 1) * 512], in0=ps[:],
                                                              in1=xres[:, ti, nb * 512:(nb + 1) * 512], op=ALU.add),
                             reads=[akey, ('xres', ti)], writes=[('xres', ti)])
                    Gm.block(d['w_ff2'][kg * 2048:(kg + 1) * 2048, nb * 512:(nb + 1) * 512], 512, 'TM', aT, 'aT', epi2,
                             keyfn=lambda mode, ti: [('aT', k, ti // 4) for k in range(KC)])
            for t in range(NT):
                P.dma('sp', d['xout'][t * 128:(t + 1) * 128, :], xres[:, t, :], reads=[('xres', t)])
            P.barrier()


def _dr(nc, name, shape, dt, kind):
    return nc.dram_tensor(name, list(shape), dt, kind=kind).ap()


A_IN = dict(x=([T, D], F32), w_in=([D, D_IN], F32), ln1g=([128, KC], F32), kng=([128, 1], F32),
            cosk=([T, 512], F32), sink=([T, 512], F32), ident=([128, 128], F32))
A_OUT = dict(KT=([512, T], BF16), V=([T, 512], BF16), IKT=([128, T], BF16), UPD=([T, D], F32),
             KTO=([1024, T], BF16), RVO=([T, D], BF16))
A_TMP = dict(projK=([T, 4608], F32))
B_IN = dict(x=([T, D], F32), w_in=([D, D_IN], F32), w_ua=([D, D], F32), w_ur=([D, D], F32), w_out=([D, D], F32),
            w_ff1=([D, 4 * D], F32), w_ff2=([4 * D, D], F32), ln1g=([128, KC], F32), ln2g=([128, KC], F32),
            qng=([128, 1], F32), gng=([128, KC], F32), gnb=([128, KC], F32), cosq=([T, 512], F32), sinq=([T, 512], F32),
            ident=([128, 128], F32), tri=([128, 128], F32), idxmask=([128, 256], F32), cflag=([128, 2], F32),
            KTf=([2, 512, T], BF16), Vf=([2, T, 512], BF16), IKTf=([2, 128, T], BF16), UPDf=([2, T, D], F32),
            KTO=([1024, T], BF16), RVO=([T, D], BF16))
B_OUT = dict(xout=([T, D], F32))
B_DBG = dict(DBG=([128, 20480], F32))
B_TMP = dict(projQ=([T, 4608], F32), G=([3, D, T], F32), OT=([D, T], BF16), ORT=([D, T], BF16), MT=([D, T], BF16))


def build_A(debug=()):
    nc = bass.Bass("TRN2", target_bir_lowering=False)
    d = {}
    for k, (s, t) in A_IN.items():
        d[k] = _dr(nc, k, s, t, "ExternalInput")
    for k, (s, t) in A_OUT.items():
        d[k] = _dr(nc, k, s, t, "ExternalOutput")
    for k, (s, t) in A_TMP.items():
        d[k] = _dr(nc, k, s, t, "ExternalOutput" if k in debug else "Internal")
    P = Prog(nc)
    phase_A(P, nc, d)
    P.emit()
    return nc


def build_B(debug=()):
    nc = bass.Bass("TRN2", target_bir_lowering=False)
    d = {}
    for k, (s, t) in B_IN.items():
        d[k] = _dr(nc, k, s, t, "ExternalInput")
    for k, (s, t) in B_OUT.items():
        d[k] = _dr(nc, k, s, t, "ExternalOutput")
    for k, (s, t) in B_TMP.items():
        d[k] = _dr(nc, k, s, t, "ExternalOutput" if k in debug else "Internal")
    if 'DBG' in debug:
        d['DBG'] = _dr(nc, 'DBG', [128, 20480], F32, "ExternalOutput")
    d['UPDf_fn'] = lambda c2, i: d['UPDf'][c2, i * 128:(i + 1) * 128, :]
    P = Prog(nc)
    phase_B(P, nc, d)
    P.emit()
    return nc


def _pk(v):
    return np.ascontiguousarray(np.asarray(v, np.float32).reshape(KC, 128).T)


def const_tables(c):
    i = np.arange(NT)[:, None]
    r = np.arange(128)[None, :]
    pos = (128 * (2 * i + c) + r).reshape(-1).astype(np.float64)
    inv = 10000.0 ** (-np.arange(0, 128, 2, dtype=np.float64) / 128.0)
    ang = pos[:, None] * inv[None, :]
    cos = np.cos(ang.astype(np.float32).astype(np.float64))
    sin = np.sin(ang.astype(np.float32).astype(np.float64))
    lg = np.log(np.asarray(GAMMA, np.float64))
    rr = np.tile(np.arange(128, dtype=np.float64), NT)
    xiq = np.exp(lg[None, :] * (rr[:, None] + 1.0))
    xik = np.exp(-lg[None, :] * (rr[:, None] + 1.0)) * 128.0 ** -0.5
    tabs = {}
    tabs['cosq'] = (cos[:, None, :] * xiq[:, :, None]).reshape(T, 512).astype(np.float32)
    tabs['sinq'] = (sin[:, None, :] * xiq[:, :, None]).reshape(T, 512).astype(np.float32)
    tabs['cosk'] = (cos[:, None, :] * xik[:, :, None]).reshape(T, 512).astype(np.float32)
    tabs['sink'] = (sin[:, None, :] * xik[:, :, None]).reshape(T, 512).astype(np.float32)
    tabs['ident'] = np.eye(128, dtype=np.float32)
    j = np.arange(128)
    tabs['tri'] = (j[:, None] <= j[None, :]).astype(np.float32)
    causal = np.where(j[None, :] <= j[:, None], 0.0, NEG).astype(np.float32)
    full = np.full((128, 128), NEG, np.float32)
    zero = np.zeros((128, 128), np.float32)
    tabs['idxmask'] = np.concatenate([causal, full], 1) if c == 0 else np.concatenate([zero, causal], 1)
    tabs['cflag'] = np.tile(np.array([[float(c), 1.0 - float(c)]], np.float32), (128, 1))
    return tabs


def shard_x(x):
    out = []
    for b in range(4):
        xt = x[b].reshape(16, 128, D)
        for c in range(2):
            out.append(np.ascontiguousarray(xt[c::2].reshape(T, D)))
    return out


def unshard_x(parts):
    out = np.empty((4, 2048, D), np.float32)
    for b in range(4):
        xt = out[b].reshape(16, 128, D)
        for c in range(2):
            xt[c::2] = parts[2 * b + c].reshape(NT, 128, D)
    return out


_CACHE = {}
RG = [[0, 1], [2, 3], [4, 5], [6, 7]]

F_IN = dict(x=([T, D], F32), w_in=([2, D, D_IN], F32), w_ua=([2, D, D], F32), w_ur=([2, D, D], F32), w_out=([2, D, D], F32),
            w_ff1=([2, D, 4 * D], F32), w_ff2=([2, 4 * D, D], F32), ln1g=([2, 128, KC], F32), ln2g=([2, 128, KC], F32),
            qng=([2, 128, 1], F32), kng=([2, 128, 1], F32), gng=([2, 128, KC], F32), gnb=([2, 128, KC], F32),
            cosq=([T, 512], F32), sinq=([T, 512], F32), cosk=([T, 512], F32), sink=([T, 512], F32),
            ident=([128, 128], F32), tri=([128, 128], F32), idxmask=([128, 256], F32), cflag=([128, 2], F32))


def build_fused():
    nc = bass.Bass("TRN2", target_bir_lowering=False)
    I = {k: _dr(nc, k, s_, t_, "ExternalInput") for k, (s_, t_) in F_IN.items()}
    xout = _dr(nc, "xout", [T, D], F32, "ExternalOutput")
    x1 = _dr(nc, "x1", [T, D], F32, "Internal")
    P = Prog(nc)
    for l in range(2):
        tmp = lambda nm, shp, dt: _dr(nc, "%s_l%d" % (nm, l), shp, dt, "Internal")
        SB = tmp("SB", [1152, T], BF16)
        RBa = tmp("RBa", [2 * 512, T], BF16)
        RBb = tmp("RBb", [2 * 640, T], BF16)
        US = tmp("US", [T, D], F32)
        UB = tmp("UB", [2 * T, D], F32)
        xin = I['x'] if l == 0 else x1
        dA = dict(x=xin, w_in=I['w_in'][l], ln1g=I['ln1g'][l], kng=I['kng'][l], cosk=I['cosk'], sink=I['sink'], ident=I['ident'],
                  projK=tmp("projK", [T, 4608], F32), KT=SB[0:512, :],
                  V=SB[512:1024, :].rearrange("a (two c) -> (a two) c", two=2), IKT=SB[1024:1152, :], UPD=US,
                  KTO=tmp("KTO", [1024, T], BF16), RVO=tmp("RVO", [T, D], BF16))
        phase_A(P, nc, dA)
        P.collective(lambda e, SB=SB, RBa=RBa: e.collective_compute("AllGather", ALU.bypass, replica_groups=RG, ins=[SB[0:512, :]],
                                                                    outs=[RBa]), writes=['RBa'])
        P.collective(lambda e, SB=SB, RBb=RBb: e.collective_compute("AllGather", ALU.bypass, replica_groups=RG, ins=[SB[512:1152, :]],
                                                                    outs=[RBb]), writes=['RBb'])
        for j in range(4):
            P.collective(lambda e, US=US, UB=UB, j=j: e.collective_compute("AllGather", ALU.bypass, replica_groups=RG,
                                                                           ins=[US[j * 256:(j + 1) * 256, :]],
                                                                           outs=[UB[j * 512:(j + 1) * 512, :]]), writes=[('UB', j)])
        P.barrier()
        RBa3 = RBa.rearrange("(r a) t -> r a t", r=2)
        RBb3 = RBb.rearrange("(r a) t -> r a t", r=2)
        updf = lambda c2, i, UB=UB: UB[(i // 2) * 512 + c2 * 256 + (i % 2) * 128:(i // 2) * 512 + c2 * 256 + (i % 2) * 128 + 128, :]
        dB = dict(x=xin, xout=(x1 if l == 0 else xout), w_in=I['w_in'][l], w_ua=I['w_ua'][l], w_ur=I['w_ur'][l], w_out=I['w_out'][l],
                  w_ff1=I['w_ff1'][l], w_ff2=I['w_ff2'][l], ln1g=I['ln1g'][l], ln2g=I['ln2g'][l], qng=I['qng'][l], gng=I['gng'][l],
                  gnb=I['gnb'][l], cosq=I['cosq'], sinq=I['sinq'], ident=I['ident'], tri=I['tri'], idxmask=I['idxmask'],
                  cflag=I['cflag'], KTf=RBa3, Vf=RBb3[:, 0:512, :].rearrange("r a (two c) -> r (a two) c", two=2),
                  IKTf=RBb3[:, 512:640, :], UPDf_fn=updf, KTO=dA['KTO'], RVO=dA['RVO'],
                  projQ=tmp("projQ", [T, 4608], F32), G=tmp("G", [3, D, T], F32), OT=tmp("OT", [D, T], BF16),
                  ORT=tmp("ORT", [D, T], BF16), MT=tmp("MT", [D, T], BF16))
        phase_B(P, nc, dB)
    P.emit()
    return nc


def kernel(x, ln1_g, w_in, q_norm_g, k_norm_g, ret_gn_g, ret_gn_b, w_up_attn, w_up_ret, w_out, ln2_g, w_ff1, w_ff2):
    f = lambda a: np.ascontiguousarray(np.asarray(a, dtype=np.float32))
    if 'F' not in _CACHE:
        _CACHE['F'] = build_fused()
    nc = _CACHE['F']
    tabs = [const_tables(c) for c in range(2)]
    xs = shard_x(f(x))
    pk2 = lambda v: np.stack([_pk(v[l]) for l in range(2)])
    shared = dict(w_in=f(w_in), w_ua=f(w_up_attn), w_ur=f(w_up_ret), w_out=f(w_out), w_ff1=f(w_ff1), w_ff2=f(w_ff2),
                  ln1g=pk2(ln1_g), ln2g=pk2(ln2_g), gng=pk2(ret_gn_g), gnb=pk2(ret_gn_b),
                  qng=f(q_norm_g).reshape(2, 128, 1), kng=f(k_norm_g).reshape(2, 128, 1))
    in_maps = []
    for k in range(8):
        m = dict(shared)
        m['x'] = xs[k]
        for nm in ('cosq', 'sinq', 'cosk', 'sink', 'ident', 'tri', 'idxmask', 'cflag'):
            m[nm] = tabs[k % 2][nm]
        in_maps.append(m)
    res = run_bass_kernel_spmd(nc, in_maps, core_ids=list(range(8))).results
    return unshard_x([np.asarray(res[k]['xout'], np.float32) for k in range(8)])
```

```python
import contextlib
import numpy as np
import ml_dtypes
import concourse.bass as bass
import concourse.mybir as mybir
from concourse.bass_utils import run_bass_kernel_spmd

F32 = mybir.dt.float32
BF16 = mybir.dt.bfloat16
AF = mybir.ActivationFunctionType
ALU = mybir.AluOpType
AX = mybir.AxisListType

D = 2048
T = 1024
NT = 8
KC = 16
D_IN = 14416
EPS = 1e-6
NEG = -1.0e30
O_AQ, O_AK, O_AV, O_IQ, O_IK, O_IW, O_RQ, O_RK, O_RV, O_RG, O_GA, O_GB = (
    0, 2048, 2560, 3072, 4096, 4160, 4176, 5200, 6224, 8272, 10320, 12368)
GAMMA = [1.0 - 2.0 ** (-5.0 - h) for h in range(8)]
GC = [float(np.float32(np.exp(np.float32(np.log(np.float32(g))) * np.float32(128.0)))) for g in GAMMA]

ENGS = ("pe", "act", "dve", "pool", "sp")
NDMASEM = 8


class Prog:
    def __init__(self, nc):
        self.nc = nc
        self.ops = {e: [] for e in ENGS}
        self.bufs = {}
        self.dma_rr = {e: 0 for e in ENGS}
        self.dma_cnt = {}
        self.dma_last = {}

    def _deps(self, reads, writes):
        deps = []
        for b in reads:
            st = self.bufs.get(b)
            if st and st[0] is not None:
                deps.append(st[0])
        for b in writes:
            st = self.bufs.get(b)
            if st:
                if st[0] is not None:
                    deps.append(st[0])
                deps.extend(st[1])
        return deps

    def _record(self, tok, reads, writes):
        for b in reads:
            st = self.bufs.setdefault(b, [None, []])
            st[1].append(tok)
        for b in writes:
            self.bufs[b] = [tok, []]

    def op(self, eng, fn, reads=(), writes=(), nosync_same=False):
        deps = self._deps(reads, writes)
        if nosync_same:
            deps = [d for d in deps if not (d[0] == 'e' and d[1] == eng)]
        tok = ('e', eng, len(self.ops[eng]))
        self.ops[eng].append(['op', fn, deps, tok])
        self._record(tok, reads, writes)
        return tok

    def dma(self, eng, out, in_, reads=(), writes=(), **kw):
        deps = self._deps(reads, writes)
        k = self.dma_rr[eng]
        self.dma_rr[eng] = (k + 1) % NDMASEM
        prev = self.dma_last.get((eng, k))
        if prev is not None:
            deps.append(prev)
        val = self.dma_cnt.get((eng, k), 0) + 16
        self.dma_cnt[(eng, k)] = val
        tok = ('d', eng, k, val)
        self.dma_last[(eng, k)] = tok

        def fn(e, out=out, in_=in_, kw=kw):
            return e.dma_start(out=out, in_=in_, **kw)
        self.ops[eng].append(['dma', fn, deps, tok])
        self._record(tok, reads, writes)
        return tok

    def collective(self, fn, reads=(), writes=()):
        eng, k = 'pool', 'cc'
        deps = self._deps(reads, writes)
        prev = self.dma_last.get((eng, k))
        if prev is not None:
            deps.append(prev)
        val = self.dma_cnt.get((eng, k), 0) + 1
        self.dma_cnt[(eng, k)] = val
        tok = ('d', eng, k, val)
        self.dma_last[(eng, k)] = tok
        self.ops[eng].append(['cc', fn, deps, tok])
        self._record(tok, reads, writes)
        return tok

    def barrier(self):
        toks = []
        for e in ENGS:
            for o in reversed(self.ops[e]):
                if o[3] is not None and o[3][0] == 'e':
                    toks.append(o[3])
                    break
        for key, tok in self.dma_last.items():
            toks.append(tok)
        for e in ENGS:
            self.ops[e].append(['bar', None, list(toks), None])
        self.bufs = {}

    def emit(self):
        nc = self.nc
        needed = {e: set() for e in ENGS}
        for e in ENGS:
            for o in self.ops[e]:
                for d in o[2]:
                    if d[0] == 'e':
                        needed[d[1]].add(d[2])
        rank = {}
        for e in ENGS:
            r = 0
            for i in sorted(needed[e]):
                r += 1
                rank[(e, i)] = r
        with contextlib.ExitStack() as st:
            esem = {e: st.enter_context(nc.semaphore("s_" + e)) for e in ENGS}
            dsem = {}
            for (e, k) in self.dma_cnt:
                dsem[(e, k)] = st.enter_context(nc.semaphore("d_%s%s" % (e, k)))
            block = st.enter_context(nc.Block())
            engobj = {"pe": "tensor", "act": "scalar", "dve": "vector", "pool": "gpsimd", "sp": "sync"}

            def body_for(e):
                def body(eng):
                    waited = {}
                    for kind, fn, deps, tok in self.ops[e]:
                        for d in deps:
                            if d[0] == 'e':
                                s, v = esem[d[1]], rank[(d[1], d[2])]
                                key = ('e', d[1])
                            else:
                                s, v = dsem[(d[1], d[2])], d[3]
                                key = ('d', d[1], d[2])
                            if waited.get(key, 0) >= v:
                                continue
                            waited[key] = v
                            eng.wait_ge(s, v)
                        if kind == 'op':
                            ins = fn(eng)
                            if tok[2] in needed[e]:
                                ins.then_inc(esem[e], 1)
                        elif kind == 'dma':
                            ins = fn(eng)
                            ins.then_inc(dsem[(tok[1], tok[2])], 16)
                        elif kind == 'cc':
                            ins = fn(eng)
                            ins.then_inc(dsem[(tok[1], tok[2])])
                return body
            for e in ENGS:
                getattr(block, engobj[e])(body_for(e))


class Ctx:
    pass


_UID = [0]


def sbuf(nc, st, name, shape, dt):
    _UID[0] += 1
    return st.enter_context(nc.sbuf_tensor("sb%d_%s" % (_UID[0], name), shape, dt))


def psum(nc, st, name, shape, dt):
    _UID[0] += 1
    return st.enter_context(nc.psum_tensor("ps%d_%s" % (_UID[0], name), shape, dt))


def ln_phase(P, nc, st, x_dram, g_dram, hT, ident, tag, xres=None):
    gt = sbuf(nc, st, tag + "gt", [128, KC], F32)
    P.dma('sp', gt[:], g_dram, writes=[tag + 'gt'])
    xt = [sbuf(nc, st, tag + "xt%d" % i, [128, D], F32) for i in range(2)] if xres is None else None
    hb = [sbuf(nc, st, tag + "hb%d" % i, [128, D], BF16) for i in range(2)]
    junk = sbuf(nc, st, tag + "junk", [128, D], F32)
    sm = [sbuf(nc, st, tag + "sm%d" % i, [128, 4], F32) for i in range(2)]
    pT = [psum(nc, st, tag + "pT%d" % i, [128, 8, 128], BF16) for i in range(2)]
    for t in range(NT):
        b = t % 2
        if xres is None:
            P.dma('sp', xt[b][:], x_dram[t * 128:(t + 1) * 128, :], writes=[(tag, 'xt', b)])
            xin = xt[b][:]
            rk = [(tag, 'xt', b)]
        else:
            xin = xres[:, t, :]
            rk = [('xres', t)]
        s = sm[b]
        P.op('act', lambda e, xin=xin: e.activation(out=junk[:], in_=xin, func=AF.Square),
             reads=rk, writes=[(tag, 'junk')])
        P.op('dve', lambda e, s=s: e.tensor_reduce(out=s[:, 0:1], in_=junk[:], axis=AX.X, op=ALU.add),
             reads=[(tag, 'junk')], writes=[(tag, 'sm', b)])
        P.op('dve', lambda e, s=s: e.tensor_scalar(out=s[:, 1:2], in0=s[:, 0:1], scalar1=1.0 / D, scalar2=EPS,
                                                   op0=ALU.mult, op1=ALU.add),
             reads=[(tag, 'sm', b)], writes=[(tag, 'sm', b)])
        P.op('act', lambda e, s=s: e.activation(out=s[:, 2:3], in_=s[:, 1:2], func=AF.Sqrt),
             reads=[(tag, 'sm', b)], writes=[(tag, 'sm', b)])
        P.op('dve', lambda e, s=s: e.reciprocal(out=s[:, 3:4], in_=s[:, 2:3]),
             reads=[(tag, 'sm', b)], writes=[(tag, 'sm', b)])
        P.op('act', lambda e, xin=xin, s=s, b=b: e.activation(out=hb[b][:], in_=xin, func=AF.Copy, scale=s[:, 3:4]),
             reads=rk + [(tag, 'sm', b)], writes=[(tag, 'hb', b)])
        for half in range(2):
            pb = pT[half]
            for kk in range(8):
                k = half * 8 + kk
                P.op('pe', lambda e, pb=pb, kk=kk, k=k, b=b: e.transpose(out=pb[:, kk, :], in_=hb[b][:, k * 128:(k + 1) * 128],
                                                                         identity=ident[:]),
                     reads=[(tag, 'hb', b), 'ident'], writes=[(tag, 'pT', half)], nosync_same=True)
            P.op('dve', lambda e, pb=pb, half=half, t=t: e.tensor_tensor(
                out=hT[:, half * 8:half * 8 + 8, t * 128:(t + 1) * 128], in0=pb[:],
                in1=gt[:, half * 8:half * 8 + 8].unsqueeze(2).to_broadcast([128, 8, 128]), op=ALU.mult),
                reads=[(tag, 'pT', half), tag + 'gt'], writes=[('hT', t)])


class Gemm:
    def __init__(self, P, nc, st, nacc, tag):
        self.P, self.nc = P, nc
        self.Wb = [sbuf(nc, st, tag + "Wb%d" % i, [128, KC, 512], BF16) for i in range(2)]
        self.Wf = [sbuf(nc, st, tag + "Wf%d" % i, [128, 8, 512], F32) for i in range(2)]
        self.fcnt = 0
        self.acc = [psum(nc, st, tag + "acc%d" % i, [128, 512], F32) for i in range(nacc)]
        self.wcnt = 0
        self.acnt = 0
        self.tag = tag

    def block(self, w_ap, ncols, mode, actT, act_key, epi, tiles=range(8), keyfn=None):
        slot = self._load(w_ap, ncols)
        prev = getattr(self, 'pending', None)
        self.pending = (slot, ncols, mode, actT, act_key, epi, tiles, keyfn)
        if prev is not None:
            self._compute(*prev)

    def flush(self):
        prev = getattr(self, 'pending', None)
        self.pending = None
        if prev is not None:
            self._compute(*prev)

    def _load(self, w_ap, ncols):
        P = self.P
        slot = self.wcnt % 2
        self.wcnt += 1
        Wb = self.Wb[slot]
        wkeys = [(self.tag, 'W', slot, 0), (self.tag, 'W', slot, 1)]
        for half in range(2):
            fs = self.fcnt % 2
            self.fcnt += 1
            Wf = self.Wf[fs]
            fkey = (self.tag, 'Wf', fs)
            P.dma('sp', Wf[:, :, 0:ncols], w_ap[half * 1024:(half + 1) * 1024, :].rearrange("(k p) n -> p k n", p=128), writes=[fkey])
            if half == 0:
                P.op('act', lambda e, Wf=Wf, half=half: e.copy(out=Wb[:, half * 8:half * 8 + 8, 0:ncols], in_=Wf[:, :, 0:ncols]),
                     reads=[fkey], writes=[wkeys[half]])
            else:
                P.op('dve', lambda e, Wf=Wf, half=half: e.tensor_copy(out=Wb[:, half * 8:half * 8 + 8, 0:ncols], in_=Wf[:, :, 0:ncols]),
                     reads=[fkey], writes=[wkeys[half]])
        return slot

    def _compute(self, slot, ncols, mode, actT, act_key, epi, tiles, keyfn):
        P = self.P
        Wb = self.Wb[slot]
        wkeys = [(self.tag, 'W', slot, 0), (self.tag, 'W', slot, 1)]
        for ti in tiles:
            ai = self.acnt % len(self.acc)
            self.acnt += 1
            ps = self.acc[ai]
            akey = (self.tag, 'acc', ai)
            for k in range(KC):
                if mode == 'TM':
                    lhsT = actT[:, k, ti * 128:(ti + 1) * 128]
                    rhs = Wb[:, k, 0:ncols]
                    out = ps[:, 0:ncols]
                    rkeys = [wkeys[k // 8]] + (keyfn('TM', ti) if keyfn else [(act_key, ti)])
                else:
                    mc, tg = ti // 2, ti % 2
                    lhsT = Wb[:, k, mc * 128:(mc + 1) * 128]
                    rhs = actT[:, k, tg * 512:(tg + 1) * 512]
                    out = ps[:, :]
                    rkeys = [wkeys[k // 8]] + (keyfn('FM', ti) if keyfn else [(act_key, tg * 4 + q) for q in range(4)])
                P.op('pe', lambda e, out=out, lhsT=lhsT, rhs=rhs, k=k: e.matmul(out, lhsT=lhsT, rhs=rhs, start=(k == 0),
                                                                                stop=(k == KC - 1)),
                     reads=rkeys, writes=[akey], nosync_same=True)
            epi(ti, ps, akey)


def rotary(P, eng, xin, C, S, out, tmp, keys_in, key_out, key_tmp):
    x1, x2 = xin[:, :, 0, :], xin[:, :, 1, :]
    t1, t2 = tmp[:, :, 0, :], tmp[:, :, 1, :]
    P.op(eng, lambda e: e.tensor_tensor(out=t1, in0=x1, in1=C, op=ALU.mult), reads=keys_in, writes=[key_tmp])
    P.op(eng, lambda e: e.tensor_tensor(out=t2, in0=x2, in1=S, op=ALU.mult), reads=keys_in + [key_tmp], writes=[key_tmp])
    P.op(eng, lambda e: e.tensor_tensor(out=out[:, :, 0, :], in0=t1, in1=t2, op=ALU.subtract), reads=[key_tmp], writes=[key_out])
    P.op(eng, lambda e: e.tensor_tensor(out=t1, in0=x2, in1=C, op=ALU.mult), reads=keys_in + [key_out], writes=[key_tmp])
    P.op(eng, lambda e: e.tensor_tensor(out=t2, in0=x1, in1=S, op=ALU.mult), reads=keys_in + [key_tmp], writes=[key_tmp])
    P.op(eng, lambda e: e.tensor_tensor(out=out[:, :, 1, :], in0=t1, in1=t2, op=ALU.add), reads=[key_tmp, key_out], writes=[key_out])


def head_rstd(P, xin, nh, hd, sq, sm, keys_in, tag):
    P.op('act', lambda e: e.activation(out=sq[:, 0:nh * hd], in_=xin, func=AF.Square), reads=keys_in, writes=[tag + 'sq'])
    P.op('dve', lambda e: e.tensor_reduce(out=sm[:, 0, 0:nh], in_=sq[:, 0:nh * hd].rearrange("p (h d) -> p h d", d=hd),
                                          axis=AX.X, op=ALU.add), reads=[tag + 'sq'], writes=[tag + 'sm'])
    P.op('dve', lambda e: e.tensor_scalar(out=sm[:, 1, 0:nh], in0=sm[:, 0, 0:nh], scalar1=1.0 / hd, scalar2=EPS,
                                          op0=ALU.mult, op1=ALU.add), reads=[tag + 'sm'], writes=[tag + 'sm'])
    P.op('act', lambda e: e.activation(out=sm[:, 2, 0:nh], in_=sm[:, 1, 0:nh], func=AF.Sqrt), reads=[tag + 'sm'], writes=[tag + 'sm'])
    P.op('dve', lambda e: e.reciprocal(out=sm[:, 3, 0:nh], in_=sm[:, 2, 0:nh]), reads=[tag + 'sm'], writes=[tag + 'sm'])


def phase_A(P, nc, d):
    projK = d['projK']
    with contextlib.ExitStack() as st:
        ident = sbuf(nc, st, "identA", [128, 128], BF16)
        P.dma('pool', ident[:], d['ident'], writes=['ident'])
        hT = sbuf(nc, st, "hT", [128, KC, T], BF16)
        with contextlib.ExitStack() as st2:
            ln_phase(P, nc, st2, d['x'], d['ln1g'], hT, ident, "lnA")
            P.barrier()
            if d.get('_stop') == 1:
                P.dma('sp', d['projK'].rearrange("t (k c) -> t k c", c=288)[0:128, :, 0:256].bitcast(BF16) if False else d['KTO'].rearrange("(k p) t -> p k t", p=128)[:, 0:8, :], hT[:, 0:8, :], reads=[('hT', t) for t in range(NT)])
                P.barrier()
                return
        with contextlib.ExitStack() as st2:
            G = Gemm(P, nc, st2, 6, "gA")
            stg = [sbuf(nc, st2, "stgA%d" % i, [128, 512], F32) for i in range(4)]
            cnt = [0]
            blocks = [(O_AK, 512, 0), (O_AV, 512, 512), (O_IK, 80, 1024), (O_RK, 512, 1536), (O_RK + 512, 512, 2048),
                      (O_RV, 512, 2560), (O_RV + 512, 512, 3072), (O_RV + 1024, 512, 3584), (O_RV + 1536, 512, 4096)]
            for (c0, ncols, dst) in blocks:
                def epi(ti, ps, akey, ncols=ncols, dst=dst):
                    s = cnt[0] % 4
                    cnt[0] += 1
                    eng = 'act' if s % 2 == 0 else 'dve'
                    if eng == 'act':
                        P.op('act', lambda e: e.copy(out=stg[s][:, 0:ncols], in_=ps[:, 0:ncols]), reads=[akey], writes=[('stgA', s)])
                    else:
                        P.op('dve', lambda e: e.tensor_copy(out=stg[s][:, 0:ncols], in_=ps[:, 0:ncols]), reads=[akey], writes=[('stgA', s)])
                    P.dma('pool', projK[ti * 128:(ti + 1) * 128, dst:dst + ncols], stg[s][:, 0:ncols], reads=[('stgA', s)],
                          writes=[('projK', ti)])
                G.block(d['w_in'][:, c0:c0 + ncols], ncols, 'TM', hT, 'hT', epi)
                if d.get('_stop') == 2 and d.get('_nblk', 99) <= blocks.index((c0, ncols, dst)) + 1:
                    break
            G.flush()
            P.barrier()
    if d.get('_stop') == 2:
        return
    with contextlib.ExitStack() as st:
        ident = sbuf(nc, st, "identA2", [128, 128], BF16)
        P.dma('pool', ident[:], d['ident'], writes=['ident'])
        kng = sbuf(nc, st, "kng", [128, 1], F32)
        P.dma('sp', kng[:], d['kng'], writes=['kng'])
        KTs = sbuf(nc, st, "KTs", [128, 4, T], BF16)
        Vs = sbuf(nc, st, "Vs", [128, NT, 512], BF16)
        IKs = sbuf(nc, st, "IKs", [128, T], BF16)
        KTOs = sbuf(nc, st, "KTOs", [128, 8, T], BF16)
        akv = [sbuf(nc, st, "akv%d" % i, [128, 1024], F32) for i in range(2)]
        ikt = [sbuf(nc, st, "ikt%d" % i, [128, 64], F32) for i in range(2)]
        rkt = [sbuf(nc, st, "rkt%d" % i, [128, 8, 2, 64], F32) for i in range(2)]
        rvt = [sbuf(nc, st, "rvt%d" % i, [128, 2048], F32) for i in range(2)]
        ckt = [sbuf(nc, st, "ckt%d" % i, [128, 8, 64], F32) for i in range(2)]
        skt = [sbuf(nc, st, "skt%d" % i, [128, 8, 64], F32) for i in range(2)]
        sq = sbuf(nc, st, "sqA", [128, 512], F32)
        sm = sbuf(nc, st, "smA", [128, 4, 16], F32)
        akn = sbuf(nc, st, "akn", [128, 4, 128], BF16)
        ikd = sbuf(nc, st, "ikd", [128, 128], BF16)
        rtmp = sbuf(nc, st, "rtmp", [128, 8, 2, 64], F32)
        kb = sbuf(nc, st, "kb", [128, 8, 2, 64], BF16)
        rvb = [sbuf(nc, st, "rvb%d" % i, [128, 2048], BF16) for i in range(2)]
        us = [sbuf(nc, st, "us%d" % i, [128, 2048], F32) for i in range(2)]
        pTk = psum(nc, st, "pTk", [128, 8, 128], BF16)
        pTi = psum(nc, st, "pTi", [128, 8, 128], BF16)
        pTr = psum(nc, st, "pTr", [128, 8, 128], BF16)
        pU = [psum(nc, st, "pU%d" % i, [128, 512], F32) for i in range(4)]
        for t in range(NT):
            b = t % 2
            r0, r1 = t * 128, (t + 1) * 128
            P.dma('sp', akv[b][:], projK[r0:r1, 0:1024], writes=[('akv', b)])
            P.dma('sp', ikt[b][:], projK[r0:r1, 1024:1088], writes=[('ikt', b)])
            P.dma('sp', rkt[b][:].rearrange("p a b c -> p (a b c)"), projK[r0:r1, 1536:2560], writes=[('rkt', b)])
            P.dma('sp', rvt[b][:], projK[r0:r1, 2560:4608], writes=[('rvt', b)])
            P.dma('sp', ckt[b][:].rearrange("p a c -> p (a c)"), d['cosk'][r0:r1, :], writes=[('ckt', b)])
            P.dma('sp', skt[b][:].rearrange("p a c -> p (a c)"), d['sink'][r0:r1, :], writes=[('skt', b)])
            lvl = d.get('_stop', 99)
            head_rstd(P, akv[b][:, 0:512], 4, 128, sq, sm, [('akv', b)], 'A')
            P.op('dve', lambda e, b=b: e.tensor_tensor(out=akn[:], in0=akv[b][:, 0:512].rearrange("p (h d) -> p h d", d=128),
                                                       in1=sm[:, 3, 0:4].unsqueeze(2).to_broadcast([128, 4, 128]), op=ALU.mult),
                 reads=[('akv', b), 'Asm'], writes=['akn'])
            for h in range(4):
                P.op('pe', lambda e, h=h: e.transpose(out=pTk[:, h, :], in_=akn[:, h, :], identity=ident[:]),
                     reads=['akn', 'ident'], writes=['pTk'], nosync_same=True)
            P.op('act', lambda e, t=t: e.activation(out=KTs[:, :, t * 128:(t + 1) * 128], in_=pTk[:, 0:4, :], func=AF.Copy,
                                                    scale=kng[:, 0:1]), reads=['pTk', 'kng'], writes=[('KTs', t)])
            if lvl < 4:
                continue
            P.op('act', lambda e, b=b, t=t: e.copy(out=Vs[:, t, :], in_=akv[b][:, 512:1024]), reads=[('akv', b)], writes=[('Vs', t)])
            P.op('dve', lambda e, b=b: e.tensor_copy(out=ikd[:, 0:64], in_=ikt[b][:]), reads=[('ikt', b)], writes=['ikd'])
            P.op('dve', lambda e, b=b: e.tensor_copy(out=ikd[:, 64:128], in_=ikt[b][:]), reads=[('ikt', b), 'ikd'], writes=['ikd'])
            P.op('pe', lambda e: e.transpose(out=pTi[:, 0, :], in_=ikd[:], identity=ident[:]), reads=['ikd', 'ident'], writes=['pTi'],
                 nosync_same=True)
            P.op('act', lambda e, t=t: e.copy(out=IKs[:, t * 128:(t + 1) * 128], in_=pTi[:, 0, :]), reads=['pTi'], writes=[('IKs', t)])
            if lvl < 5:
                continue
            rotary(P, 'dve', rkt[b], ckt[b][:], skt[b][:], kb, rtmp, [('rkt', b), ('ckt', b), ('skt', b)], 'kb', 'rtmp')
            for h in range(8):
                P.op('pe', lambda e, h=h: e.transpose(out=pTr[:, h, :], in_=kb[:, h, :, :].rearrange("p a c -> p (a c)"),
                                                      identity=ident[:]), reads=['kb', 'ident'], writes=['pTr'], nosync_same=True)
            P.op('act', lambda e, t=t: e.copy(out=KTOs[:, :, t * 128:(t + 1) * 128], in_=pTr[:]), reads=['pTr'], writes=[('KTOs', t)])
            if lvl < 6:
                continue
            P.op('act', lambda e, b=b: e.copy(out=rvb[b][:], in_=rvt[b][:]), reads=[('rvt', b)], writes=[('rvb', b)])
            P.dma('sp', d['RVO'][r0:r1, :], rvb[b][:], reads=[('rvb', b)])
            if lvl < 7:
                continue
            for h in range(8):
                pu = pU[h // 2]
                P.op('pe', lambda e, h=h, pu=pu, b=b: e.matmul(pu[:, (h % 2) * 256:(h % 2) * 256 + 256],
                                                               lhsT=kb[:, h, :, :].rearrange("p a c -> p (a c)"),
                                                               rhs=rvb[b][:, h * 256:(h + 1) * 256], start=True, stop=True),
                     reads=['kb', ('rvb', b)], writes=[('pU', h // 2)], nosync_same=True)
            if lvl < 8:
                continue
            for q in range(4):
                eng = 'act' if q % 2 == 0 else 'dve'
                if eng == 'act':
                    P.op('act', lambda e, q=q, b=b: e.copy(out=us[b][:, q * 512:(q + 1) * 512], in_=pU[q][:]), reads=[('pU', q)],
                         writes=[('us', b)])
                else:
                    P.op('dve', lambda e, q=q, b=b: e.tensor_copy(out=us[b][:, q * 512:(q + 1) * 512], in_=pU[q][:]), reads=[('pU', q)],
                         writes=[('us', b)])
            P.dma('sp', d['UPD'][r0:r1, :], us[b][:], reads=[('us', b)])
        P.dma('sp', d['KT'].rearrange("(h p) t -> p h t", p=128), KTs[:], reads=[('KTs', t) for t in range(NT)])
        P.dma('sp', d['V'].rearrange("(t p) c -> p t c", p=128), Vs[:], reads=[('Vs', t) for t in range(NT)])
        P.dma('sp', d['IKT'], IKs[:], reads=[('IKs', t) for t in range(NT)])
        P.dma('sp', d['KTO'].rearrange("(h p) t -> p h t", p=128), KTOs[:], reads=[('KTOs', t) for t in range(NT)])
        P.barrier()


def phase_B(P, nc, d):
    projQ, G = d['projQ'], d['G']
    SCALE = 128.0 ** -0.5
    with contextlib.ExitStack() as st:
        ident12 = sbuf(nc, st, "identB", [128, 128], BF16)
        P.dma('pool', ident12[:], d['ident'], writes=['ident'])
        hT = sbuf(nc, st, "hTB", [128, KC, T], BF16)
        with contextlib.ExitStack() as st2:
            ln_phase(P, nc, st2, d['x'], d['ln1g'], hT, ident12, "lnB")
            P.barrier()
        with contextlib.ExitStack() as st2:
            Gm = Gemm(P, nc, st2, 6, "gB")
            stg = [sbuf(nc, st2, "stgB%d" % i, [128, 512], F32) for i in range(4)]
            cnt = [0]
            tm_blocks = [(O_AQ + 512 * i, 512, 512 * i) for i in range(4)] + [(O_IQ, 512, 2048), (O_IQ + 512, 512, 2560),
                                                                               (O_IK, 80, 3072), (O_RQ, 512, 3584), (O_RQ + 512, 512, 4096)]
            for (c0, ncols, dst) in tm_blocks:
                def epi(ti, ps, akey, ncols=ncols, dst=dst):
                    s = cnt[0] % 4
                    cnt[0] += 1
                    if s % 2 == 0:
                        P.op('act', lambda e: e.copy(out=stg[s][:, 0:ncols], in_=ps[:, 0:ncols]), reads=[akey], writes=[('stgB', s)])
                    else:
                        P.op('dve', lambda e: e.tensor_copy(out=stg[s][:, 0:ncols], in_=ps[:, 0:ncols]), reads=[akey], writes=[('stgB', s)])
                    P.dma('pool', projQ[ti * 128:(ti + 1) * 128, dst:dst + ncols], stg[s][:, 0:ncols], reads=[('stgB', s)],
                          writes=[('projQ', ti)])
                Gm.block(d['w_in'][:, c0:c0 + ncols], ncols, 'TM', hT, 'hT', epi)
            for gi, c0 in enumerate((O_RG, O_GA, O_GB)):
                for nb in range(4):
                    def epi(ti, ps, akey, gi=gi, nb=nb):
                        s = cnt[0] % 4
                        cnt[0] += 1
                        mc, tg = ti // 2, ti % 2
                        P.op('act', lambda e: e.activation(out=stg[s][:], in_=ps[:], func=AF.Sigmoid), reads=[akey], writes=[('stgB', s)])
                        if gi == 0:
                            P.op('dve', lambda e: e.tensor_tensor(out=stg[s][:], in0=stg[s][:], in1=ps[:], op=ALU.mult),
                                 reads=[akey, ('stgB', s)], writes=[('stgB', s)])
                        f0 = nb * 512 + mc * 128
                        P.dma('pool', G[gi, f0:f0 + 128, tg * 512:(tg + 1) * 512], stg[s][:], reads=[('stgB', s)], writes=[('G', gi)])
                    Gm.block(d['w_in'][:, c0 + nb * 512:c0 + (nb + 1) * 512], 512, 'FM', hT, 'hT', epi)
            Gm.flush()
            P.barrier()
    with contextlib.ExitStack() as st:
        sb = lambda name, shape, dt: sbuf(nc, st, name, shape, dt)
        ident5 = sb("identB5", [128, 128], BF16)
        P.dma('pool', ident5[:], d['ident'], writes=['ident'])
        tri = sb("tri", [128, 128], BF16)
        P.dma('pool', tri[:], d['tri'], writes=['tri'])
        ones = sb("ones", [128, 128], BF16)
        P.op('pool', lambda e: e.memset(ones[:], 1.0), writes=['ones'])
        idxm = sb("idxm", [128, 256], F32)
        P.dma('sp', idxm[:], d['idxmask'], writes=['idxm'])
        cfl = sb("cfl", [128, 2], F32)
        P.dma('sp', cfl[:], d['cflag'], writes=['cfl'])
        qng = sb("qng", [128, 2], F32)
        P.dma('sp', qng[:, 0:1], d['qng'], writes=['qng'])
        P.op('pool', lambda e: e.tensor_scalar(out=qng[:, 1:2], in0=qng[:, 0:1], scalar1=SCALE, scalar2=None, op0=ALU.mult),
             reads=['qng'], writes=['qng'])
        gng = sb("gng", [128, KC], F32)
        gnb = sb("gnb", [128, KC], F32)
        P.dma('sp', gng[:], d['gng'], writes=['gng'])
        P.dma('sp', gnb[:], d['gnb'], writes=['gnb'])
        KTsb = sb("KTsb", [128, 4, 8, 2, 128], BF16)
        Vsb = sb("Vsb", [128, 8, 2, 512], BF16)
        IKsb = sb("IKsb", [128, 8, 2, 128], BF16)
        for c2 in range(2):
            for h in range(4):
                P.dma('sp', KTsb[:, h, :, c2, :], d['KTf'][c2, h * 128:(h + 1) * 128, :].rearrange("p (i r) -> p i r", r=128),
                      writes=[('KTsb', c2, h)])
            P.dma('sp', Vsb[:, :, c2, :], d['Vf'][c2].rearrange("(i p) c -> p i c", p=128), writes=[('Vsb', c2)])
            P.dma('sp', IKsb[:, :, c2, :], d['IKTf'][c2].rearrange("p (i r) -> p i r", r=128), writes=[('IKsb', c2)])
        kside = [('KTsb', c2, h) for c2 in range(2) for h in range(4)] + [('Vsb', 0), ('Vsb', 1), ('IKsb', 0), ('IKsb', 1)]
        KTv = KTsb[:].rearrange("p h i c r -> p h (i c r)")
        Vv = Vsb[:].rearrange("p i c f -> p (i c) f")
        IKv = IKsb[:].rearrange("p i c r -> p (i c r)")
        S = sb("Sst", [128, 8, 256], F32)
        P.op('pool', lambda e: e.memset(S[:], 0.0), writes=['S'])
        gct = sb("gct", [128, 8, 256], F32)
        for h in range(8):
            P.op('pool', lambda e, h=h: e.memset(gct[:, h, :], GC[h]), reads=['gct'], writes=['gct'])
        aqt = sb("aqt", [128, 2048], F32)
        iqt = sb("iqt", [128, 1024], F32)
        iwt = sb("iwt", [128, 16], F32)
        iws = sb("iws", [128, 16], F32)
        rqt = sb("rqt", [128, 8, 2, 64], F32)
        cqt = sb("cqt", [128, 8, 64], F32)
        sqt = sb("sqt", [128, 8, 64], F32)
        ktot = sb("ktot", [128, 8, 128], BF16)
        rvot = sb("rvot", [128, 2048], BF16)
        U0 = sb("U0", [128, 8, 256], F32)
        U1 = sb("U1", [128, 8, 256], F32)
        srg = sb("srg", [128, KC, 128], F32)
        sq = sb("sqB", [128, 2048], F32)
        sm = sb("smB", [128, 4, 16], F32)
        aqn = sb("aqn", [128, 16, 128], BF16)
        aqT2 = [sb("aqT%d" % q, [128, 16, 128], BF16) for q in range(2)]
        qT2 = [sb("qT%d" % q, [128, 8, 128], BF16) for q in range(2)]
        nselT2 = [sb("nselT%d" % q, [128, 16, 128], BF16) for q in range(2)]
        negI = sb("negI", [128, 128], BF16)
        P.op('dve', lambda e: e.tensor_scalar(out=negI[:], in0=ident5[:], scalar1=-30000.0, scalar2=None, op0=ALU.mult),
             reads=['ident'], writes=['negI'])
        iqb = sb("iqb", [128, 1024], BF16)
        iqT = sb("iqT", [128, 8, 128], BF16)
        rtmp = sb("rtmpB", [128, 8, 2, 64], F32)
        qb = sb("qb", [128, 8, 2, 64], BF16)
        score = sb("score", [128, 2048], F32)
        work = sb("work", [128, 2048], F32)
        m8 = sb("m8", [128, 8], F32)
        thr = sb("thr", [128, 1], F32)
        sel = sb("sel", [128, 2048], BF16)
        rlu = [sb("rlu%d" % i, [128, 512], F32) for i in range(2)]
        ex = [sb("ex%d" % i, [128, 512], BF16) for i in range(2)]
        rden = sb("rden", [128, 512], F32)
        OTt = sb("OTt", [128, 16, 128], BF16)
        Rb = sb("Rb", [128, 8, 256], BF16)
        Pm = sb("Pm", [128, 8, 128], BF16)
        st6 = sb("st6", [128, 8, 6], F32)
        mv = sb("mv", [128, 8, 2], F32)
        gs = sb("gs", [128, 4, 8], F32)
        yn = sb("yn", [128, 2048], BF16)
        otmp = sb("otmp", [128, 8, 128], F32)
        ORt = sb("ORt", [128, 16, 128], BF16)
        pT = [psum(nc, st, "pTB%d" % i, [128, 8, 128], BF16) for i in range(2)]
        pg = [psum(nc, st, "pg%d" % i, [128, 512], F32) for i in range(2)]
        poT = psum(nc, st, "poT", [128, 512], F32)
        pden = psum(nc, st, "pden", [128, 512], F32)
        py = [psum(nc, st, "py%d" % i, [128, 512], F32) for i in range(2)]
        ptc = [0]
        pgc = [0]
        exc = [0]

        def transposes(src_fn, n, evac_fn, rkeys):
            for g0 in range(0, n, 8):
                c = min(8, n - g0)
                pi = ptc[0] % 2
                ptc[0] += 1
                for kk in range(c):
                    P.op('pe', lambda e, pi=pi, kk=kk, g0=g0: e.transpose(out=pT[pi][:, kk, :], in_=src_fn(g0 + kk), identity=ident5[:]),
                         reads=rkeys + ['ident'], writes=[('pT', pi)], nosync_same=True)
                evac_fn(pT[pi], g0, c, ('pT', pi))

        def stage1(i):
            r0, r1 = i * 128, (i + 1) * 128
            nk = 2 * i + 2
            n = nk * 128
            bq = i % 2
            aqT, qT, nselT = aqT2[bq], qT2[bq], nselT2[bq]
            kq = ('q', bq)
            P.dma('sp', aqt[:], projQ[r0:r1, 0:2048], writes=['aqt'])
            P.dma('sp', iqt[:], projQ[r0:r1, 2048:3072], writes=['iqt'])
            P.dma('sp', iwt[:], projQ[r0:r1, 3072 + 64:3072 + 80], writes=['iwt'])
            P.dma('sp', rqt[:].rearrange("p a b c -> p (a b c)"), projQ[r0:r1, 3584:4608], writes=['rqt'])
            P.dma('sp', cqt[:].rearrange("p a c -> p (a c)"), d['cosq'][r0:r1, :], writes=['cqt'])
            P.dma('sp', sqt[:].rearrange("p a c -> p (a c)"), d['sinq'][r0:r1, :], writes=['sqt'])
            head_rstd(P, aqt[:], 16, 128, sq, sm, ['aqt'], 'B')
            P.op('dve', lambda e: e.tensor_tensor(out=aqn[:], in0=aqt[:].rearrange("p (h d) -> p h d", d=128),
                                                   in1=sm[:, 3, 0:16].unsqueeze(2).to_broadcast([128, 16, 128]), op=ALU.mult),
                 reads=['aqt', 'Bsm'], writes=['aqn'])
            transposes(lambda k: aqn[:, k, :], 16,
                       lambda pt, g0, c, key: P.op('act', lambda e: e.activation(out=aqT[:, g0:g0 + c, :], in_=pt[:, 0:c, :], func=AF.Copy,
                                                                                 scale=qng[:, 1:2]),
                                                   reads=[key, 'qng'], writes=[('aqT', bq)]), ['aqn'])
            P.op('act', lambda e: e.copy(out=iqb[:], in_=iqt[:]), reads=['iqt'], writes=['iqb'])
            transposes(lambda k: iqb[:, k * 128:(k + 1) * 128], 8,
                       lambda pt, g0, c, key: P.op('act', lambda e: e.copy(out=iqT[:, g0:g0 + c, :], in_=pt[:, 0:c, :]),
                                                   reads=[key], writes=['iqT']), ['iqb'])
            P.op('dve', lambda e: e.tensor_scalar(out=iws[:], in0=iwt[:], scalar1=0.25, scalar2=None, op0=ALU.mult),
                 reads=['iwt'], writes=['iws'])
            rotary(P, 'dve', rqt, cqt[:], sqt[:], qb, rtmp, ['rqt', 'cqt', 'sqt'], 'qb', 'rtmpB')
            transposes(lambda k: qb[:, k, :, :].rearrange("p a c -> p (a c)"), 8,
                       lambda pt, g0, c, key: P.op('act', lambda e: e.copy(out=qT[:, g0:g0 + c, :], in_=pt[:, 0:c, :]),
                                                   reads=[key], writes=[('qT', bq)]), ['qb'])
            for kc0 in range(0, n, 512):
                w = min(512, n - kc0)
                for h in range(16):
                    hp = h % 2
                    gi = pgc[0] % 2
                    pgc[0] += 1
                    P.op('pe', lambda e, gi=gi, h=h, hp=hp, kc0=kc0, w=w: e.matmul(
                        pg[gi][:, 0:w], lhsT=iqT[hp * 64:(hp + 1) * 64, h // 2, :], rhs=IKv[hp * 64:(hp + 1) * 64, kc0:kc0 + w],
                        start=True, stop=True), reads=['iqT'] + kside, writes=[('pg', gi)], nosync_same=True)
                    P.op('act', lambda e, gi=gi, w=w: e.activation(out=rlu[gi][:, 0:w], in_=pg[gi][:, 0:w], func=AF.Relu),
                         reads=[('pg', gi)], writes=[('rlu', gi)])
                    if h == 0:
                        P.op('dve', lambda e, gi=gi, w=w, kc0=kc0: e.tensor_scalar(out=score[:, kc0:kc0 + w], in0=rlu[gi][:, 0:w],
                                                                                   scalar1=iws[:, 0:1], scalar2=None, op0=ALU.mult),
                             reads=[('rlu', gi), 'iws'], writes=['score'])
                    else:
                        P.op('dve', lambda e, gi=gi, w=w, kc0=kc0, h=h: e.scalar_tensor_tensor(
                            out=score[:, kc0:kc0 + w], in0=rlu[gi][:, 0:w], scalar=iws[:, h:h + 1], in1=score[:, kc0:kc0 + w],
                            op0=ALU.mult, op1=ALU.add), reads=[('rlu', gi), 'iws', 'score'], writes=['score'])
            P.op('dve', lambda e, n=n: e.tensor_tensor(out=score[:, n - 256:n], in0=score[:, n - 256:n], in1=idxm[:], op=ALU.add),
                 reads=['score', 'idxm'], writes=['score'])
            for r in range(32 if n > 256 else 0):
                src = score if r == 0 else work
                P.op('dve', lambda e, src=src, n=n: e.max(out=m8[:], in_=src[:, 0:n]), reads=['score', 'work'], writes=['m8'])
                if r < 31:
                    P.op('dve', lambda e, src=src, n=n: e.match_replace(out=work[:, 0:n], in_to_replace=m8[:], in_values=src[:, 0:n],
                                                                        imm_value=NEG), reads=['score', 'work', 'm8'], writes=['work'])
            if n > 256:
                P.op('dve', lambda e: e.tensor_scalar(out=thr[:], in0=m8[:, 7:8], scalar1=-1.0e29, scalar2=None, op0=ALU.max),
                     reads=['m8'], writes=['thr'])
            else:
                P.op('dve', lambda e: e.memset(thr[:], -1.0e29), reads=['thr'], writes=['thr'])
            P.op('dve', lambda e, n=n: e.tensor_scalar(out=sel[:, 0:n], in0=score[:, 0:n], scalar1=thr[:, 0:1], scalar2=None, op0=ALU.is_ge),
                 reads=['score', 'thr'], writes=['sel'])
            transposes(lambda k: sel[:, k * 128:(k + 1) * 128], nk,
                       lambda pt, g0, c, key: P.op('dve', lambda e: e.tensor_scalar(out=nselT[:, g0:g0 + c, :], in0=pt[:, 0:c, :], scalar1=-1.0,
                                                                                    scalar2=1.0, op0=ALU.mult, op1=ALU.add),
                                                   reads=[key], writes=[('nselT', bq)]), ['sel'])
        def stage2(i):
            r0, r1 = i * 128, (i + 1) * 128
            nk = 2 * i + 2
            n = nk * 128
            bq = i % 2
            aqT, qT, nselT = aqT2[bq], qT2[bq], nselT2[bq]
            kq = ('q', bq)
            P.dma('sp', ktot[:], d['KTO'].rearrange("(h p) t -> p h t", p=128)[:, :, r0:r1], writes=['ktot'])
            P.dma('sp', rvot[:], d['RVO'][r0:r1, :], writes=['rvot'])
            P.dma('sp', U0[:].rearrange("p h v -> p (h v)"), d['UPDf_fn'](0, i), writes=['U0'])
            P.dma('sp', U1[:].rearrange("p h v -> p (h v)"), d['UPDf_fn'](1, i), writes=['U1'])
            P.dma('sp', srg[:], G[0].rearrange("(k p) t -> p k t", p=128)[:, :, r0:r1], writes=['srg'])
            for g in range(4):
                for kt in range(nk):
                    gi = pgc[0] % 2
                    pgc[0] += 1
                    xi = exc[0] % 2
                    exc[0] += 1
                    P.op('pe', lambda e, gi=gi, g=g, kt=kt: e.matmul(pg[gi][:].rearrange("p (h q) -> p h q", q=128),
                                                                     lhsT=KTv[:, g, kt * 128:(kt + 1) * 128], rhs=aqT[:, 4 * g:4 * g + 4, :],
                                                                     start=True, stop=False),
                         reads=[('aqT', bq)] + kside, writes=[('pg', gi)], nosync_same=True)
                    P.op('pe', lambda e, gi=gi, kt=kt: e.matmul(pg[gi][:].rearrange("p (h q) -> p h q", q=128), lhsT=negI[:],
                                                                rhs=nselT[:, kt:kt + 1, :].to_broadcast([128, 4, 128]),
                                                                start=False, stop=True),
                         reads=[('nselT', bq), 'negI'], writes=[('pg', gi)], nosync_same=True)
                    P.op('act', lambda e, gi=gi, xi=xi: e.activation(out=ex[xi][:], in_=pg[gi][:], func=AF.Exp),
                         reads=[('pg', gi)], writes=[('ex', xi)])
                    P.op('pe', lambda e, xi=xi, g=g, kt=kt, nk=nk: e.matmul(poT[:], lhsT=Vv[:, kt, g * 128:(g + 1) * 128], rhs=ex[xi][:],
                                                                            start=(kt == 0), stop=(kt == nk - 1)),
                         reads=[('ex', xi)] + kside, writes=['poT'], nosync_same=True)
                    P.op('pe', lambda e, xi=xi, kt=kt, nk=nk: e.matmul(pden[:], lhsT=ones[:], rhs=ex[xi][:],
                                                                       start=(kt == 0), stop=(kt == nk - 1)),
                         reads=[('ex', xi), 'ones'], writes=['pden'], nosync_same=True)
                P.op('dve', lambda e: e.reciprocal(out=rden[:], in_=pden[:]), reads=['pden'], writes=['rden'])
                P.op('dve', lambda e, g=g: e.tensor_tensor(out=OTt[:, 4 * g:4 * g + 4, :], in0=poT[:].rearrange("p (h q) -> p h q", q=128),
                                                           in1=rden[:].rearrange("p (h q) -> p h q", q=128), op=ALU.mult),
                     reads=['poT', 'rden'], writes=['OTt'])
            P.dma('sp', d['OT'].rearrange("(h p) t -> p h t", p=128)[:, :, r0:r1], OTt[:], reads=['OTt'])
            P.op('dve', lambda e: e.tensor_tensor(out=U0[:], in0=U0[:], in1=S[:], op=ALU.add), reads=['U0', 'S'], writes=['U0'])
            P.op('dve', lambda e: e.tensor_tensor(out=U0[:], in0=U0[:], in1=gct[:], op=ALU.mult), reads=['U0', 'gct'], writes=['U0'])
            P.op('dve', lambda e: e.tensor_tensor(out=U1[:], in0=U1[:], in1=U0[:], op=ALU.add), reads=['U0', 'U1'], writes=['U1'])
            P.op('dve', lambda e: e.tensor_scalar(out=U0[:], in0=U0[:], scalar1=cfl[:, 0:1], scalar2=None, op0=ALU.mult),
                 reads=['U0', 'cfl'], writes=['U0'])
            P.op('dve', lambda e: e.scalar_tensor_tensor(out=Rb[:], in0=S[:], scalar=cfl[:, 1:2], in1=U0[:], op0=ALU.mult, op1=ALU.add),
                 reads=['S', 'U0', 'cfl'], writes=['Rb'])
            P.op('dve', lambda e: e.tensor_tensor(out=S[:], in0=U1[:], in1=gct[:], op=ALU.mult), reads=['U1', 'gct', 'Rb'], writes=['S'])
            for hh in range(2):
                gi = pgc[0] % 2
                pgc[0] += 1
                for h4 in range(4):
                    h = hh * 4 + h4
                    P.op('pe', lambda e, gi=gi, h=h, h4=h4: e.matmul(pg[gi][:, h4 * 128:(h4 + 1) * 128], lhsT=ktot[:, h, :], rhs=qT[:, h, :],
                                                                     start=True, stop=True),
                         reads=['ktot', ('qT', bq)], writes=[('pg', gi)], nosync_same=True)
                P.op('dve', lambda e, hh=hh, gi=gi: e.tensor_tensor(out=Pm[:, hh * 4:hh * 4 + 4, :],
                                                                    in0=pg[gi][:].rearrange("p (h q) -> p h q", q=128),
                                                                    in1=tri[:].unsqueeze(1).to_broadcast([128, 4, 128]), op=ALU.mult),
                     reads=[('pg', gi), 'tri'], writes=[('Pm', hh)])
            for hh in range(2):
                for h4 in range(4):
                    h = hh * 4 + h4
                    yo = py[h4 // 2][:, (h4 % 2) * 256:(h4 % 2) * 256 + 256]
                    P.op('pe', lambda e, yo=yo, h=h: e.matmul(yo, lhsT=Pm[:, h, :], rhs=rvot[:, h * 256:(h + 1) * 256], start=True, stop=False),
                         reads=[('Pm', hh), 'rvot'], writes=[('py', h4 // 2)], nosync_same=True)
                    P.op('pe', lambda e, yo=yo, h=h: e.matmul(yo, lhsT=qT[:, h, :], rhs=Rb[:, h, :], start=False, stop=True),
                         reads=[('qT', bq), 'Rb'], writes=[('py', h4 // 2)], nosync_same=True)
                for h4 in range(4):
                    h = hh * 4 + h4
                    yo = py[h4 // 2][:, (h4 % 2) * 256:(h4 % 2) * 256 + 256]
                    P.op('dve', lambda e, yo=yo, h=h: e.bn_stats(out=st6[:, h, :], in_=yo), reads=[('py', h4 // 2)], writes=['st6'])
                    P.op('dve', lambda e, h=h: e.bn_aggr(out=mv[:, h, :], in_=st6[:, h, :]), reads=['st6'], writes=['mv'])
                sl = slice(hh * 4, hh * 4 + 4)
                P.op('dve', lambda e, sl=sl: e.tensor_scalar(out=gs[:, 0, sl], in0=mv[:, sl, 1], scalar1=EPS, scalar2=None, op0=ALU.add),
                     reads=['mv'], writes=['gs'])
                P.op('act', lambda e, sl=sl: e.activation(out=gs[:, 1, sl], in_=gs[:, 0, sl], func=AF.Sqrt), reads=['gs'], writes=['gs'])
                P.op('dve', lambda e, sl=sl: e.reciprocal(out=gs[:, 2, sl], in_=gs[:, 1, sl]), reads=['gs'], writes=['gs'])
                for h4 in range(4):
                    h = hh * 4 + h4
                    yo = py[h4 // 2][:, (h4 % 2) * 256:(h4 % 2) * 256 + 256]
                    P.op('dve', lambda e, yo=yo, h=h: e.tensor_scalar(out=yn[:, h * 256:(h + 1) * 256], in0=yo, scalar1=mv[:, h, 0:1],
                                                                      scalar2=gs[:, 2, h:h + 1], op0=ALU.subtract, op1=ALU.mult),
                         reads=[('py', h4 // 2), 'mv', 'gs'], writes=['yn'])
            def evac_or(pt, g0, c, key):
                for kk in range(c):
                    P.op('act', lambda e, kk=kk: e.activation(out=otmp[:, kk, :], in_=pt[:, kk, :], func=AF.Identity,
                                                              scale=gng[:, g0 + kk:g0 + kk + 1], bias=gnb[:, g0 + kk:g0 + kk + 1]),
                         reads=[key, 'gng', 'gnb'], writes=['otmp'], nosync_same=True)
                P.op('dve', lambda e: e.tensor_tensor(out=ORt[:, g0:g0 + c, :], in0=otmp[:, 0:c, :], in1=srg[:, g0:g0 + c, :], op=ALU.mult),
                     reads=['otmp', 'srg'], writes=['ORt'])
            transposes(lambda k: yn[:, k * 128:(k + 1) * 128], 16, evac_or, ['yn'])
            P.dma('sp', d['ORT'].rearrange("(h p) t -> p h t", p=128)[:, :, r0:r1], ORt[:], reads=['ORt'])
        stage1(0)
        for i in range(NT):
            if i + 1 < NT:
                stage1(i + 1)
            stage2(i)
        P.barrier()
    with contextlib.ExitStack() as st:
        OTa = sbuf(nc, st, "OTa", [128, KC, T], BF16)
        ORa = sbuf(nc, st, "ORa", [128, KC, T], BF16)
        MTa = sbuf(nc, st, "MTa", [128, KC, T], BF16)
        P.dma('sp', OTa[:], d['OT'].rearrange("(k p) t -> p k t", p=128), writes=[('OTa', q) for q in range(8)])
        P.dma('sp', ORa[:], d['ORT'].rearrange("(k p) t -> p k t", p=128), writes=[('ORa', q) for q in range(8)])
        Gm = Gemm(P, nc, st, 6, "g6")
        tmpA = [sbuf(nc, st, "tmpA%d" % i, [128, 512], F32) for i in range(8)]
        gta = [sbuf(nc, st, "gta%d" % i, [128, 512], F32) for i in range(2)]
        mm = [sbuf(nc, st, "mm%d" % i, [128, 512], F32) for i in range(2)]
        cnt = [0]
        for nb in range(4):
            def epiA(ti, ps, akey, nb=nb):
                s = cnt[0] % 2
                cnt[0] += 1
                mc, tg = ti // 2, ti % 2
                f0 = nb * 512 + mc * 128
                P.dma('pool', gta[s][:], G[1, f0:f0 + 128, tg * 512:(tg + 1) * 512], writes=[('gta', s)])
                P.op('dve', lambda e: e.tensor_tensor(out=tmpA[ti][:], in0=ps[:], in1=gta[s][:], op=ALU.mult),
                     reads=[akey, ('gta', s)], writes=[('tmpA', ti)])
            Gm.block(d['w_ua'][:, nb * 512:(nb + 1) * 512], 512, 'FM', OTa, 'OTa', epiA)

            def epiR(ti, ps, akey, nb=nb):
                s = cnt[0] % 2
                cnt[0] += 1
                mc, tg = ti // 2, ti % 2
                f0 = nb * 512 + mc * 128
                P.dma('pool', gta[s][:], G[2, f0:f0 + 128, tg * 512:(tg + 1) * 512], writes=[('gta', s)])
                P.op('dve', lambda e: e.tensor_tensor(out=mm[s][:], in0=ps[:], in1=gta[s][:], op=ALU.mult),
                     reads=[akey, ('gta', s)], writes=[('mm', s)])
                P.op('dve', lambda e: e.tensor_tensor(out=MTa[:, nb * 4 + mc, tg * 512:(tg + 1) * 512], in0=mm[s][:], in1=tmpA[ti][:],
                                                      op=ALU.add), reads=[('mm', s), ('tmpA', ti)], writes=[('MTa', nb * 4 + mc, tg)])
            Gm.block(d['w_ur'][:, nb * 512:(nb + 1) * 512], 512, 'FM', ORa, 'ORa', epiR)
        Gm.flush()
        P.dma('sp', d['MT'].rearrange("(k p) t -> p k t", p=128), MTa[:], reads=[('MTa', k, tg) for k in range(KC) for tg in range(2)])
        P.barrier()
    with contextlib.ExitStack() as st:
        xres = sbuf(nc, st, "xres", [128, NT, D], F32)
        for t in range(NT):
            P.dma('sp', xres[:, t, :], d['x'][t * 128:(t + 1) * 128, :], writes=[('xres', t)])
        with contextlib.ExitStack() as st2:
            MTb = sbuf(nc, st2, "MTb", [128, KC, T], BF16)
            P.dma('sp', MTb[:], d['MT'].rearrange("(k p) t -> p k t", p=128), writes=[('MTb', q) for q in range(8)])
            Gm = Gemm(P, nc, st2, 6, "g7")
            for nb in range(4):
                def epi(ti, ps, akey, nb=nb):
                    P.op('dve', lambda e: e.tensor_tensor(out=xres[:, ti, nb * 512:(nb + 1) * 512], in0=ps[:],
                                                          in1=xres[:, ti, nb * 512:(nb + 1) * 512], op=ALU.add),
                         reads=[akey, ('xres', ti)], writes=[('xres', ti)])
                Gm.block(d['w_out'][:, nb * 512:(nb + 1) * 512], 512, 'TM', MTb, 'MTb', epi)
            Gm.flush()
            P.barrier()
        with contextlib.ExitStack() as st2:
            ident8 = sbuf(nc, st2, "identB8", [128, 128], BF16)
            P.dma('pool', ident8[:], d['ident'], writes=['ident'])
            h2T = sbuf(nc, st2, "h2T", [128, KC, T], BF16)
            with contextlib.ExitStack() as st3:
                ln_phase(P, nc, st3, None, d['ln2g'], h2T, ident8, "ln2", xres=xres)
                P.barrier()
            aT = sbuf(nc, st2, "aT", [128, KC, T], BF16)
            Gm = Gemm(P, nc, st2, 6, "g8")
            rl = [sbuf(nc, st2, "rl%d" % i, [128, 512], F32) for i in range(2)]
            cnt = [0]
            for kg in range(4):
                for nb in range(4):
                    def epi1(ti, ps, akey, nb=nb):
                        s = cnt[0] % 2
                        cnt[0] += 1
                        mc, tg = ti // 2, ti % 2
                        P.op('act', lambda e: e.activation(out=rl[s][:], in_=ps[:], func=AF.Relu), reads=[akey], writes=[('rl', s)])
                        P.op('act', lambda e: e.activation(out=aT[:, nb * 4 + mc, tg * 512:(tg + 1) * 512], in_=rl[s][:], func=AF.Square),
                             reads=[('rl', s)], writes=[('aT', nb * 4 + mc, tg)])
                    Gm.block(d['w_ff1'][:, kg * 2048 + nb * 512:kg * 2048 + (nb + 1) * 512], 512, 'FM', h2T, 'hT', epi1)
                for nb in range(4):
                    def epi2(ti, ps, akey, nb=nb):
                        P.op('dve', lambda e: e.tensor_tensor(out=xres[:, ti, nb * 512:(nb + 1) * 512], in0=ps[:],
                                                              in1=xres[:, ti, nb * 512:(nb + 1) * 512], op=ALU.add),
                             reads=[akey, ('xres', ti)], writes=[('xres', ti)])
                    Gm.block(d['w_ff2'][kg * 2048:(kg + 1) * 2048, nb * 512:(nb + 1) * 512], 512, 'TM', aT, 'aT', epi2,
                             keyfn=lambda mode, ti: [('aT', k, ti // 4) for k in range(KC)])
            Gm.flush()
            for t in range(NT):
                P.dma('sp', d['xout'][t * 128:(t + 1) * 128, :], xres[:, t, :], reads=[('xres', t)])
            P.barrier()


def _dr(nc, name, shape, dt, kind):
    return nc.dram_tensor(name, list(shape), dt, kind=kind).ap()


A_IN = dict(x=([T, D], F32), w_in=([D, D_IN], F32), ln1g=([128, KC], F32), kng=([128, 1], F32),
            cosk=([T, 512], F32), sink=([T, 512], F32), ident=([128, 128], F32))
A_OUT = dict(KT=([512, T], BF16), V=([T, 512], BF16), IKT=([128, T], BF16), UPD=([T, D], F32),
             KTO=([1024, T], BF16), RVO=([T, D], BF16))
A_TMP = dict(projK=([T, 4608], F32))
B_IN = dict(x=([T, D], F32), w_in=([D, D_IN], F32), w_ua=([D, D], F32), w_ur=([D, D], F32), w_out=([D, D], F32),
            w_ff1=([D, 4 * D], F32), w_ff2=([4 * D, D], F32), ln1g=([128, KC], F32), ln2g=([128, KC], F32),
            qng=([128, 1], F32), gng=([128, KC], F32), gnb=([128, KC], F32), cosq=([T, 512], F32), sinq=([T, 512], F32),
            ident=([128, 128], F32), tri=([128, 128], F32), idxmask=([128, 256], F32), cflag=([128, 2], F32),
            KTf=([2, 512, T], BF16), Vf=([2, T, 512], BF16), IKTf=([2, 128, T], BF16), UPDf=([2, T, D], F32),
            KTO=([1024, T], BF16), RVO=([T, D], BF16))
B_OUT = dict(xout=([T, D], F32))
B_DBG = dict(DBG=([128, 20480], F32))
B_TMP = dict(projQ=([T, 4608], F32), G=([3, D, T], F32), OT=([D, T], BF16), ORT=([D, T], BF16), MT=([D, T], BF16))


def build_A(debug=()):
    nc = bass.Bass("TRN2", target_bir_lowering=False)
    d = {}
    for k, (s, t) in A_IN.items():
        d[k] = _dr(nc, k, s, t, "ExternalInput")
    for k, (s, t) in A_OUT.items():
        d[k] = _dr(nc, k, s, t, "ExternalOutput")
    for k, (s, t) in A_TMP.items():
        d[k] = _dr(nc, k, s, t, "ExternalOutput" if k in debug else "Internal")
    P = Prog(nc)
    phase_A(P, nc, d)
    P.emit()
    return nc


def build_B(debug=()):
    nc = bass.Bass("TRN2", target_bir_lowering=False)
    d = {}
    for k, (s, t) in B_IN.items():
        d[k] = _dr(nc, k, s, t, "ExternalInput")
    for k, (s, t) in B_OUT.items():
        d[k] = _dr(nc, k, s, t, "ExternalOutput")
    for k, (s, t) in B_TMP.items():
        d[k] = _dr(nc, k, s, t, "ExternalOutput" if k in debug else "Internal")
    if 'DBG' in debug:
        d['DBG'] = _dr(nc, 'DBG', [128, 20480], F32, "ExternalOutput")
    d['UPDf_fn'] = lambda c2, i: d['UPDf'][c2, i * 128:(i + 1) * 128, :]
    P = Prog(nc)
    phase_B(P, nc, d)
    P.emit()
    return nc


def _pk(v):
    return np.ascontiguousarray(np.asarray(v, np.float32).reshape(KC, 128).T)


def const_tables(c):
    i = np.arange(NT)[:, None]
    r = np.arange(128)[None, :]
    pos = (128 * (2 * i + c) + r).reshape(-1).astype(np.float64)
    inv = 10000.0 ** (-np.arange(0, 128, 2, dtype=np.float64) / 128.0)
    ang = pos[:, None] * inv[None, :]
    cos = np.cos(ang.astype(np.float32).astype(np.float64))
    sin = np.sin(ang.astype(np.float32).astype(np.float64))
    lg = np.log(np.asarray(GAMMA, np.float64))
    rr = np.tile(np.arange(128, dtype=np.float64), NT)
    xiq = np.exp(lg[None, :] * (rr[:, None] + 1.0))
    xik = np.exp(-lg[None, :] * (rr[:, None] + 1.0)) * 128.0 ** -0.5
    tabs = {}
    tabs['cosq'] = (cos[:, None, :] * xiq[:, :, None]).reshape(T, 512).astype(np.float32)
    tabs['sinq'] = (sin[:, None, :] * xiq[:, :, None]).reshape(T, 512).astype(np.float32)
    tabs['cosk'] = (cos[:, None, :] * xik[:, :, None]).reshape(T, 512).astype(np.float32)
    tabs['sink'] = (sin[:, None, :] * xik[:, :, None]).reshape(T, 512).astype(np.float32)
    tabs['ident'] = np.eye(128, dtype=np.float32)
    j = np.arange(128)
    tabs['tri'] = (j[:, None] <= j[None, :]).astype(np.float32)
    causal = np.where(j[None, :] <= j[:, None], 0.0, NEG).astype(np.float32)
    full = np.full((128, 128), NEG, np.float32)
    zero = np.zeros((128, 128), np.float32)
    tabs['idxmask'] = np.concatenate([causal, full], 1) if c == 0 else np.concatenate([zero, causal], 1)
    tabs['cflag'] = np.tile(np.array([[float(c), 1.0 - float(c)]], np.float32), (128, 1))
    return tabs


def shard_x(x):
    out = []
    for b in range(4):
        xt = x[b].reshape(16, 128, D)
        for c in range(2):
            out.append(np.ascontiguousarray(xt[c::2].reshape(T, D)))
    return out


def unshard_x(parts):
    out = np.empty((4, 2048, D), np.float32)
    for b in range(4):
        xt = out[b].reshape(16, 128, D)
        for c in range(2):
            xt[c::2] = parts[2 * b + c].reshape(NT, 128, D)
    return out


_CACHE = {}
RG = [[0, 1], [2, 3], [4, 5], [6, 7]]

F_IN = dict(x=([T, D], F32), w_in=([2, D, D_IN], F32), w_ua=([2, D, D], F32), w_ur=([2, D, D], F32), w_out=([2, D, D], F32),
            w_ff1=([2, D, 4 * D], F32), w_ff2=([2, 4 * D, D], F32), ln1g=([2, 128, KC], F32), ln2g=([2, 128, KC], F32),
            qng=([2, 128, 1], F32), kng=([2, 128, 1], F32), gng=([2, 128, KC], F32), gnb=([2, 128, KC], F32),
            cosq=([T, 512], F32), sinq=([T, 512], F32), cosk=([T, 512], F32), sink=([T, 512], F32),
            ident=([128, 128], F32), tri=([128, 128], F32), idxmask=([128, 256], F32), cflag=([128, 2], F32))


def build_fused():
    nc = bass.Bass("TRN2", target_bir_lowering=False)
    I = {k: _dr(nc, k, s_, t_, "ExternalInput") for k, (s_, t_) in F_IN.items()}
    xout = _dr(nc, "xout", [T, D], F32, "ExternalOutput")
    x1 = _dr(nc, "x1", [T, D], F32, "Internal")
    P = Prog(nc)
    for l in range(2):
        tmp = lambda nm, shp, dt: _dr(nc, "%s_l%d" % (nm, l), shp, dt, "Internal")
        SB = tmp("SB", [1152, T], BF16)
        RBa = tmp("RBa", [2 * 512, T], BF16)
        RBb = tmp("RBb", [2 * 640, T], BF16)
        US = tmp("US", [T, D], F32)
        UB = tmp("UB", [2 * T, D], F32)
        xin = I['x'] if l == 0 else x1
        dA = dict(x=xin, w_in=I['w_in'][l], ln1g=I['ln1g'][l], kng=I['kng'][l], cosk=I['cosk'], sink=I['sink'], ident=I['ident'],
                  projK=tmp("projK", [T, 4608], F32), KT=SB[0:512, :],
                  V=SB[512:1024, :].rearrange("a (two c) -> (a two) c", two=2), IKT=SB[1024:1152, :], UPD=US,
                  KTO=tmp("KTO", [1024, T], BF16), RVO=tmp("RVO", [T, D], BF16))
        phase_A(P, nc, dA)
        P.collective(lambda e, SB=SB, RBa=RBa: e.collective_compute("AllGather", ALU.bypass, replica_groups=RG, ins=[SB[0:512, :]],
                                                                    outs=[RBa]), writes=['RBa'])
        P.collective(lambda e, SB=SB, RBb=RBb: e.collective_compute("AllGather", ALU.bypass, replica_groups=RG, ins=[SB[512:1152, :]],
                                                                    outs=[RBb]), writes=['RBb'])
        for j in range(4):
            P.collective(lambda e, US=US, UB=UB, j=j: e.collective_compute("AllGather", ALU.bypass, replica_groups=RG,
                                                                           ins=[US[j * 256:(j + 1) * 256, :]],
                                                                           outs=[UB[j * 512:(j + 1) * 512, :]]), writes=[('UB', j)])
        P.barrier()
        RBa3 = RBa.rearrange("(r a) t -> r a t", r=2)
        RBb3 = RBb.rearrange("(r a) t -> r a t", r=2)
        updf = lambda c2, i, UB=UB: UB[(i // 2) * 512 + c2 * 256 + (i % 2) * 128:(i // 2) * 512 + c2 * 256 + (i % 2) * 128 + 128, :]
        dB = dict(x=xin, xout=(x1 if l == 0 else xout), w_in=I['w_in'][l], w_ua=I['w_ua'][l], w_ur=I['w_ur'][l], w_out=I['w_out'][l],
                  w_ff1=I['w_ff1'][l], w_ff2=I['w_ff2'][l], ln1g=I['ln1g'][l], ln2g=I['ln2g'][l], qng=I['qng'][l], gng=I['gng'][l],
                  gnb=I['gnb'][l], cosq=I['cosq'], sinq=I['sinq'], ident=I['ident'], tri=I['tri'], idxmask=I['idxmask'],
                  cflag=I['cflag'], KTf=RBa3, Vf=RBb3[:, 0:512, :].rearrange("r a (two c) -> r (a two) c", two=2),
                  IKTf=RBb3[:, 512:640, :], UPDf_fn=updf, KTO=dA['KTO'], RVO=dA['RVO'],
                  projQ=tmp("projQ", [T, 4608], F32), G=tmp("G", [3, D, T], F32), OT=tmp("OT", [D, T], BF16),
                  ORT=tmp("ORT", [D, T], BF16), MT=tmp("MT", [D, T], BF16))
        phase_B(P, nc, dB)
    P.emit()
    return nc


def kernel(x, ln1_g, w_in, q_norm_g, k_norm_g, ret_gn_g, ret_gn_b, w_up_attn, w_up_ret, w_out, ln2_g, w_ff1, w_ff2):
    f = lambda a: np.ascontiguousarray(np.asarray(a, dtype=np.float32))
    if 'F' not in _CACHE:
        _CACHE['F'] = build_fused()
    nc = _CACHE['F']
    tabs = [const_tables(c) for c in range(2)]
    xs = shard_x(f(x))
    pk2 = lambda v: np.stack([_pk(v[l]) for l in range(2)])
    shared = dict(w_in=f(w_in), w_ua=f(w_up_attn), w_ur=f(w_up_ret), w_out=f(w_out), w_ff1=f(w_ff1), w_ff2=f(w_ff2),
                  ln1g=pk2(ln1_g), ln2g=pk2(ln2_g), gng=pk2(ret_gn_g), gnb=pk2(ret_gn_b),
                  qng=f(q_norm_g).reshape(2, 128, 1), kng=f(k_norm_g).reshape(2, 128, 1))
    in_maps = []
    for k in range(8):
        m = dict(shared)
        m['x'] = xs[k]
        for nm in ('cosq', 'sinq', 'cosk', 'sink', 'ident', 'tri', 'idxmask', 'cflag'):
            m[nm] = tabs[k % 2][nm]
        in_maps.append(m)
    res = run_bass_kernel_spmd(nc, in_maps, core_ids=list(range(8))).results
    return unshard_x([np.asarray(res[k]['xout'], np.float32) for k in range(8)])
```
